# Optimizing a Trainium2 kernel written in Bass

```python
import math
import jax, jax.numpy as jnp
from jax import lax
import numpy as np

D_MODEL = 2048
BATCH = 2
SEQ = 4096
DEPTH = 4

EXPAND = 2
D_INNER = EXPAND * D_MODEL
HEAD_DIM = 128
D_A = D_INNER // 2
D_B = D_INNER - D_A
H_A = D_A // HEAD_DIM
H_B = D_B // HEAD_DIM
DIFF_QK_DIM = HEAD_DIM // 2
H_C = D_INNER // HEAD_DIM
P_IN = 4 * D_INNER
DILATED_PATTERNS = ((128, 1), (512, 4), (2048, 16))
N_BUCKETS = 32
MAX_DISTANCE = 2048
H_BIAS = H_A + H_B
Q_BLOCK = 128
A_Q_BLOCK = 16
N_EVEN = (DEPTH + 1) // 2
EPS = 1e-6

kernel_name = "hybrid_dilated_diff_stickbreak_trunk"


def rms_norm(x, g):
    xf = x.astype(jnp.float32)
    y = xf * lax.rsqrt(jnp.mean(xf * xf, axis=-1, keepdims=True) + EPS)
    return (y * g.astype(jnp.float32)).astype(x.dtype)


def t5_bucket(dist):
    max_exact = N_BUCKETS // 2
    d = jnp.maximum(dist, 1).astype(jnp.float32)
    large = max_exact + (jnp.log(d / max_exact) / math.log(MAX_DISTANCE / max_exact)
                         * (N_BUCKETS - max_exact)).astype(jnp.int32)
    large = jnp.minimum(large, N_BUCKETS - 1)
    return jnp.where(dist < max_exact, dist, large)


def _blocked(fn, n_blocks):
    out = lax.map(fn, jnp.arange(n_blocks))
    nb, b, qb, h, dh = out.shape
    return jnp.transpose(out, (1, 0, 2, 3, 4)).reshape(b, nb * qb, h, dh)


def dilated_attention(q, k, v, bias_table):
    s = q.shape[1]
    scale = HEAD_DIM ** -0.5
    dist = jnp.asarray(np.stack([np.arange(w // d + 1) * d for w, d in DILATED_PATTERNS]), dtype=jnp.int32)
    bias = jnp.transpose(bias_table[t5_bucket(dist)], (2, 0, 1))

    def block(i):
        start = i * A_Q_BLOCK
        t = start + jnp.arange(A_Q_BLOCK)
        kpos = t[:, None, None] - dist[None]
        valid = kpos >= 0
        kidx = jnp.maximum(kpos, 0)
        qb = lax.dynamic_slice_in_dim(q, start, A_Q_BLOCK, axis=1)
        kb = k[:, kidx]
        vb = v[:, kidx]
        logits = jnp.einsum('bqhd,bqgjhd->bhqgj', qb, kb) * scale + bias[:, None]
        logits = jnp.where(valid, logits, -jnp.inf)
        lse = jax.nn.logsumexp(logits, axis=-1)
        p = jnp.exp(logits - lse[..., None])
        o_g = jnp.einsum('bhqgj,bqgjhd->bqhgd', p, vb)
        w = jax.nn.softmax(lse, axis=-1)
        return jnp.einsum('bqhgd,bhqg->bqhd', o_g, w)

    return _blocked(block, s // A_Q_BLOCK)


def diff_attention(q1, q2, k1, k2, v, bias_table, lam):
    s = q1.shape[1]
    scale = DIFF_QK_DIM ** -0.5
    kpos = jnp.arange(s)

    def block(i):
        start = i * Q_BLOCK
        t = start + jnp.arange(Q_BLOCK)
        rel = t[:, None] - kpos[None, :]
        causal = rel >= 0
        bias = jnp.transpose(bias_table[t5_bucket(jnp.maximum(rel, 0))], (2, 0, 1))

        def softmax_map(q, k):
            qb = lax.dynamic_slice_in_dim(q, start, Q_BLOCK, axis=1)
            logits = jnp.einsum('bqhd,bshd->bhqs', qb, k) * scale + bias
            return jax.nn.softmax(jnp.where(causal, logits, -jnp.inf), axis=-1)

        a = softmax_map(q1, k1) - lam * softmax_map(q2, k2)
        return jnp.einsum('bhqs,bshd->bqhd', a, v)

    return _blocked(block, s // Q_BLOCK)


def stick_breaking_attention(q, k, v):
    s = q.shape[1]
    scale = HEAD_DIM ** -0.5
    kpos = jnp.arange(s)

    def block(i):
        start = i * Q_BLOCK
        t = start + jnp.arange(Q_BLOCK)
        strict = kpos[None, :] < t[:, None]
        qb = lax.dynamic_slice_in_dim(q, start, Q_BLOCK, axis=1)
        z = jnp.einsum('bqhd,bshd->bhqs', qb, k) * scale
        log_beta = jax.nn.log_sigmoid(z)
        log_1m_beta = jnp.where(strict, jax.nn.log_sigmoid(-z), 0.0)
        tail = lax.cumsum(log_1m_beta, axis=3, reverse=True) - log_1m_beta
        a = jnp.where(strict, jnp.exp(log_beta + tail), 0.0)
        return jnp.einsum('bhqs,bshd->bqhd', a, v)

    return _blocked(block, s // Q_BLOCK)


def even_mixer(mix_in, rel_bias, lam_vec, subln_g, lam_init):
    b, s = mix_in.shape[:2]
    sizes = [D_A, D_A, D_A, D_B // 2, D_B // 2, D_B // 2, D_B // 2]
    qa, ka, va, q1, q2, k1, k2, vb = jnp.split(mix_in.astype(jnp.float32), list(np.cumsum(sizes)), axis=-1)
    heads = lambda t, h, d: t.reshape(b, s, h, d)
    o_a = dilated_attention(heads(qa, H_A, HEAD_DIM), heads(ka, H_A, HEAD_DIM), heads(va, H_A, HEAD_DIM),
                            rel_bias[:, :H_A])
    lam_vec = lam_vec.astype(jnp.float32)
    lam = (jnp.exp(jnp.sum(lam_vec[0] * lam_vec[1])) - jnp.exp(jnp.sum(lam_vec[2] * lam_vec[3])) + lam_init)
    o_b = diff_attention(heads(q1, H_B, DIFF_QK_DIM), heads(q2, H_B, DIFF_QK_DIM),
                         heads(k1, H_B, DIFF_QK_DIM), heads(k2, H_B, DIFF_QK_DIM),
                         heads(vb, H_B, HEAD_DIM), rel_bias[:, H_A:], lam)
    o_b = rms_norm(o_b, subln_g) * (1.0 - lam_init)
    out = jnp.concatenate([o_a.reshape(b, s, D_A), o_b.reshape(b, s, D_B)], axis=-1)
    return out.astype(mix_in.dtype)


def odd_mixer(mix_in):
    b, s = mix_in.shape[:2]
    q, k, v = jnp.split(mix_in.astype(jnp.float32), 3, axis=-1)
    heads = lambda t: t.reshape(b, s, H_C, HEAD_DIM)
    out = stick_breaking_attention(heads(q), heads(k), heads(v))
    return out.reshape(b, s, D_INNER).astype(mix_in.dtype)


def setup_inputs(seed: int = 0) -> dict:
    key = jax.random.key(seed)
    ks = jax.random.split(key, 11)
    nrm = lambda k, shape, sc: jax.random.normal(k, shape, jnp.float32) * sc
    return {
        "x": nrm(ks[0], (BATCH, SEQ, D_MODEL), 1.0),
        "c": nrm(ks[1], (BATCH, D_MODEL), 1.0),
        "norm_g": 1.0 + nrm(ks[2], (DEPTH, D_MODEL), 0.02),
        "w_mod": nrm(ks[3], (DEPTH, D_MODEL, 3 * D_MODEL), 0.5 * D_MODEL ** -0.5),
        "b_mod": nrm(ks[4], (DEPTH, 3 * D_MODEL), 0.02),
        "w_in": nrm(ks[5], (DEPTH, D_MODEL, P_IN), D_MODEL ** -0.5),
        "w_out": nrm(ks[6], (DEPTH, D_INNER, D_MODEL), D_INNER ** -0.5),
        "rel_bias": nrm(ks[7], (N_BUCKETS, H_BIAS), 0.5),
        "diff_lambda": nrm(ks[8], (N_EVEN, 4, DIFF_QK_DIM), 0.1),
        "diff_subln_g": 1.0 + nrm(ks[9], (N_EVEN, HEAD_DIM), 0.02),
        "final_norm_g": 1.0 + nrm(ks[10], (D_MODEL,), 0.02),
    }


def reference(x, c, norm_g, w_mod, b_mod, w_in, w_out, rel_bias, diff_lambda, diff_subln_g, final_norm_g):
    h = x
    c_act = jax.nn.silu(c)
    for layer in range(DEPTH):
        mod = c_act @ w_mod[layer] + b_mod[layer]
        shift, scale, gate = jnp.split(mod, 3, axis=-1)
        u = rms_norm(h, norm_g[layer]) * (1.0 + scale[:, None]) + shift[:, None]
        proj = u @ w_in[layer]
        mix_in, z = proj[..., :3 * D_INNER], proj[..., 3 * D_INNER:]
        if layer % 2 == 0:
            e = layer // 2
            lam_init = 0.8 - 0.6 * math.exp(-0.3 * layer)
            mixed = even_mixer(mix_in, rel_bias, diff_lambda[e], diff_subln_g[e], lam_init)
        else:
            mixed = odd_mixer(mix_in)
        y = (mixed * jax.nn.silu(z)) @ w_out[layer]
        h = h + gate[:, None] * y
    return rms_norm(h, final_norm_g)
```

```python
import math
from contextlib import ExitStack

import numpy as np
import ml_dtypes
import concourse.bass as bass
import concourse.mybir as mybir
from concourse.bass_utils import run_bass_kernel_spmd

F32 = mybir.dt.float32
BF16 = mybir.dt.bfloat16
AF = mybir.ActivationFunctionType
ALU = mybir.AluOpType

D_MODEL = 2048
SEQ = 4096
BATCH = 2
DEPTH = 4
D_INNER = 4096
EPS = 1e-6
NEG = -30000.0

ENGS = ("sync", "scalar", "vector", "gpsimd", "tensor")
DMA_K = 6


class Buf:
    __slots__ = ("name", "last_w", "readers")

    def __init__(self, name=""):
        self.name = name
        self.last_w = None
        self.readers = {}


class Sched:
    def __init__(self, nc, stack):
        self.nc = nc
        self.streams = {e: [] for e in ENGS}
        self.count = {e: 0 for e in ENGS}
        self.seen = {e: {} for e in ENGS}
        self.sems = {}
        for e in ENGS:
            self.sems[e] = stack.enter_context(nc.semaphore("s_" + e))
        self.dma_n = {e: 0 for e in ("sync", "scalar", "gpsimd")}
        for e in ("sync", "scalar", "gpsimd"):
            for k in range(DMA_K):
                key = ("d", e, k)
                self.sems[key] = stack.enter_context(nc.semaphore(f"d_{e}_{k}"))
                self.count[key] = 0
        self.final_events = []
        self.sems["cc"] = stack.enter_context(nc.semaphore("s_cc"))
        self.count["cc"] = 0
        self.pending = {}
        self.rt = {}

    def barrier(self):
        snap = {k: v for k, v in self.count.items() if v > 0}
        for e in ENGS:
            p = self.pending.setdefault(e, {})
            for k, v in snap.items():
                if p.get(k, 0) < v:
                    p[k] = v

    def cc(self, fn, reads=(), writes=()):
        deps = self._deps(reads, writes)
        prev = self.count["cc"]
        if prev > 0 and deps.get("cc", 0) < prev:
            deps["cc"] = prev
        self.count["cc"] += 1
        ev = ("cc", self.count["cc"])
        self._finish("gpsimd", deps, fn, ev, None, reads, writes)
        return ev

    def _deps(self, reads, writes):
        deps = {}
        for b in reads:
            ev = b.last_w
            if ev is not None and deps.get(ev[0], 0) < ev[1]:
                deps[ev[0]] = ev[1]
        for b in writes:
            ev = b.last_w
            if ev is not None and deps.get(ev[0], 0) < ev[1]:
                deps[ev[0]] = ev[1]
            for k, v in b.readers.items():
                if deps.get(k, 0) < v:
                    deps[k] = v
        return deps

    def _finish(self, eng, deps, fn, ev, inc, reads, writes):
        pend = self.pending.pop(eng, None)
        if pend:
            for k, v in pend.items():
                if deps.get(k, 0) < v:
                    deps[k] = v
        waits = []
        seen = self.seen[eng]
        for k, v in deps.items():
            if k == "tensor" and eng == "tensor":
                continue
            if seen.get(k, 0) >= v:
                continue
            seen[k] = v
            waits.append((k, v))
        self.streams[eng].append((waits, fn, (ev[0], inc)))
        for b in reads:
            if b.readers.get(ev[0], 0) < ev[1]:
                b.readers[ev[0]] = ev[1]
        for b in writes:
            b.last_w = ev
            b.readers = {}

    def op(self, eng, fn, reads=(), writes=()):
        deps = self._deps(reads, writes)
        self.count[eng] += 1
        ev = (eng, self.count[eng])
        self._finish(eng, deps, fn, ev, 1, reads, writes)
        return ev

    def dma(self, eng, fn, reads=(), writes=(), final=False):
        deps = self._deps(reads, writes)
        n = self.dma_n[eng]
        self.dma_n[eng] += 1
        key = ("d", eng, n % DMA_K)
        prev = self.count[key]
        if prev > 0 and deps.get(key, 0) < prev:
            deps[key] = prev
        self.count[key] += 16
        ev = (key, self.count[key])
        self._finish(eng, deps, fn, ev, 16, reads, writes)
        if final:
            self.final_events.append(ev)
        return ev

    def emit(self, final_eng="sync"):
        nc = self.nc
        fw = {}
        for k, v in self.final_events:
            fw[k] = max(fw.get(k, 0), v)
        streams = self.streams
        sems = self.sems
        with nc.Block() as block:
            def mk(ename):
                def body(eng):
                    if ename == "sync":
                        self.rt["pid"] = nc.partition_id([eng.engine])
                    for waits, fn, (sk, inc) in streams[ename]:
                        for k, v in waits:
                            eng.wait_ge(sems[k], v)
                        inst = fn(eng)
                        if inc is None:
                            inst.then_inc(sems[sk])
                        else:
                            inst.then_inc(sems[sk], inc)
                    if ename == final_eng:
                        for k, v in fw.items():
                            eng.wait_ge(sems[k], v)
                return body
            block.sync(mk("sync"))
            block.scalar(mk("scalar"))
            block.vector(mk("vector"))
            block.gpsimd(mk("gpsimd"))
            block.tensor(mk("tensor"))


class Ctx:
    def __init__(self):
        self.nc = bass.Bass("TRN2", target_bir_lowering=False)
        self.stack = ExitStack()
        self.S = Sched(self.nc, self.stack)
        self.n = 0

    def sb(self, shape, dt, name=None):
        self.n += 1
        return self.stack.enter_context(self.nc.sbuf_tensor(name or f"sb{self.n}", list(shape), dt))

    def ps(self, shape, dt, name=None):
        self.n += 1
        return self.stack.enter_context(self.nc.psum_tensor(name or f"ps{self.n}", list(shape), dt))

    def din(self, name, shape, dt):
        return self.nc.dram_tensor(name, list(shape), dt, kind="ExternalInput").ap()

    def dout(self, name, shape, dt):
        return self.nc.dram_tensor(name, list(shape), dt, kind="ExternalOutput").ap()

    def dscr(self, name, shape, dt):
        return self.nc.dram_tensor(name, list(shape), dt, kind="Internal").ap()

    def finish(self):
        self.S.emit()
        self.stack.close()
        return self.nc


def build_M():
    C = Ctx()
    nc, S = C.nc, C.S
    cT = C.din("cT", [128, 16], F32)
    wm = C.din("wm", [2048, 6144], F32)
    bm = C.din("bm", [1, 6144], F32)
    out = C.dout("mod", [128, 6144], F32)

    c_sb = C.sb([128, 16], F32); b_c = Buf()
    e_sb = C.sb([128, 16], F32); b_e = Buf()
    ca = C.sb([128, 16], F32); b_ca = Buf()
    ones = C.sb([128, 128], F32); b_ones = Buf()
    L = C.sb([128, 16, 128], F32); b_L = Buf()
    bmr = C.sb([1, 6144], F32); b_bmr = Buf()
    W = [C.sb([128, 16, 512], F32) for _ in range(2)]; b_W = [Buf(), Buf()]
    res = C.sb([128, 6144], F32); b_res = Buf()
    P = [C.ps([128, 512], F32) for _ in range(2)]; b_P = [Buf(), Buf()]

    S.dma("sync", lambda e: e.dma_start(out=c_sb[:], in_=cT[:, :]), writes=[b_c])
    S.dma("sync", lambda e: e.dma_start(out=bmr[:], in_=bm[:, :]), writes=[b_bmr])
    S.op("vector", lambda e: e.memset(ones[:], 1.0), writes=[b_ones])
    S.op("scalar", lambda e: e.activation(out=e_sb[:], in_=c_sb[:], func=AF.Exp, scale=-1.0), reads=[b_c], writes=[b_e])
    S.op("vector", lambda e: e.tensor_scalar(out=e_sb[:], in0=e_sb[:], scalar1=1.0, scalar2=None, op0=ALU.add), reads=[b_e], writes=[b_e])
    S.op("vector", lambda e: e.reciprocal(out=e_sb[:], in_=e_sb[:]), reads=[b_e], writes=[b_e])
    S.op("vector", lambda e: e.tensor_tensor(out=ca[:], in0=c_sb[:], in1=e_sb[:], op=ALU.mult), reads=[b_c, b_e], writes=[b_ca])
    for k in range(16):
        S.op("vector", lambda e, k=k: e.tensor_scalar(out=L[:, k, :], in0=ones[:], scalar1=ca[:, k:k + 1], scalar2=None, op0=ALU.mult),
             reads=[b_ones, b_ca], writes=[b_L])
    wv = wm.rearrange("(k p) c -> p k c", p=128)
    for n in range(12):
        w = W[n % 2]; bw = b_W[n % 2]; p = P[n % 2]; bp = b_P[n % 2]
        S.dma("sync", lambda e, w=w, n=n: e.dma_start(out=w[:], in_=wv[:, :, n * 512:(n + 1) * 512]), writes=[bw])
        for k in range(16):
            S.op("tensor", lambda e, w=w, p=p, k=k: e.matmul(p[:, :], lhsT=L[:, k, :], rhs=w[:, k, :], start=(k == 0), stop=False),
                 reads=[b_L, bw], writes=[bp])
        S.op("tensor", lambda e, p=p, n=n: e.matmul(p[:, :], lhsT=ones[0:1, :], rhs=bmr[0:1, n * 512:(n + 1) * 512], start=False, stop=True),
             reads=[b_ones, b_bmr], writes=[bp])
        S.op("scalar", lambda e, p=p, n=n: e.activation(out=res[:, n * 512:(n + 1) * 512], in_=p[:, :], func=AF.Copy), reads=[bp], writes=[b_res])
    S.dma("sync", lambda e: e.dma_start(out=out[:, :], in_=res[:]), reads=[b_res], final=True)
    return C.finish()


def build_L31(do_outproj, do_norm, do_final):
    C = Ctx()
    nc, S = C.nc, C.S
    h_in = C.din("h_in", [1024, 2048], F32)
    if do_outproj:
        mzT = C.din("mzT", [4096, 1024], BF16)
        wout = C.din("wout", [4096, 2048], F32)
        modp = C.din("modp", [128, 6144], F32)
    if do_norm:
        modn = C.din("modn", [128, 6144], F32)
        gn = C.din("gn", [128, 2048], F32)
        uT_out = C.dout("uT", [2048, 1024], BF16)
        h_out = C.dout("h_out", [1024, 2048], F32)
    if do_final:
        gf = C.din("gf", [128, 2048], F32)
        y_out = C.dout("y", [1024, 2048], F32)

    H = C.sb([128, 8, 2048], F32, "H"); b_H = [Buf() for _ in range(8)]
    hv = h_in.rearrange("(t p) f -> p t f", p=128)
    for t in range(8):
        S.dma("sync", lambda e, t=t: e.dma_start(out=H[:, t, :], in_=hv[:, t, :]), writes=[b_H[t]])

    PS = [C.ps([128, 512], F32) for _ in range(4)]; b_PS = [Buf() for _ in range(4)]

    if do_outproj:
        MZ = C.sb([128, 32, 1024], BF16, "MZ"); b_MZ = Buf()
        mzv = mzT.rearrange("(k p) t -> p k t", p=128)
        for q in range(4):
            S.dma("sync", lambda e, q=q: e.dma_start(out=MZ[:, q * 8:(q + 1) * 8, :], in_=mzv[:, q * 8:(q + 1) * 8, :]), writes=[b_MZ])
        gate = C.sb([128, 2048], F32, "gate"); b_gate = Buf()
        S.dma("sync", lambda e: e.dma_start(out=gate[:], in_=modp[:, 4096:6144]), writes=[b_gate])
        W = [C.sb([128, 32, 256], BF16) for _ in range(2)]; b_W = [Buf(), Buf()]
        tmp = [C.sb([128, 256], F32) for _ in range(2)]; b_tmp = [Buf(), Buf()]
        wv = wout.rearrange("(k p) c -> p k c", p=128)
        it = 0
        for f in range(8):
            w = W[f % 2]; bw = b_W[f % 2]
            for q in range(2):
                S.dma("gpsimd", lambda e, w=w, f=f, q=q: e.dma_start(out=w[:, q * 16:(q + 1) * 16, :], in_=wv[:, q * 16:(q + 1) * 16, f * 256:(f + 1) * 256]),
                      writes=[bw])
            for t in range(8):
                p = PS[it % 4]; bp = b_PS[it % 4]; tm = tmp[it % 2]; btm = b_tmp[it % 2]
                it += 1
                for k in range(32):
                    S.op("tensor", lambda e, p=p, w=w, k=k, t=t: e.matmul(p[:, 0:256], lhsT=MZ[:, k, t * 128:(t + 1) * 128], rhs=w[:, k, :],
                                                                            start=(k == 0), stop=(k == 31)),
                         reads=[b_MZ, bw], writes=[bp])
                S.op("vector", lambda e, p=p, tm=tm, f=f: e.tensor_tensor(out=tm[:], in0=p[:, 0:256], in1=gate[:, f * 256:(f + 1) * 256], op=ALU.mult),
                     reads=[bp, b_gate], writes=[btm])
                S.op("gpsimd", lambda e, tm=tm, t=t, f=f: e.tensor_tensor(out=H[:, t, f * 256:(f + 1) * 256], in0=H[:, t, f * 256:(f + 1) * 256], in1=tm[:], op=ALU.add),
                     reads=[btm, b_H[t]], writes=[b_H[t]])

    if do_norm or do_final:
        if do_outproj:
            sq = W[0][:, 16:24, :].rearrange("p a b -> p (a b)"); b_sq = b_W[0]
        else:
            sq = C.sb([128, 2048], BF16, "sq")[:]; b_sq = Buf()
        st = C.sb([128, 8, 4], F32, "st"); b_st = [Buf() for _ in range(8)]
        if do_norm:
            shift = C.sb([128, 2048], F32, "shift"); b_shift = Buf()
            gs = C.sb([128, 2048], F32, "gs"); b_gs = Buf()
            g_sb = C.sb([128, 2048], F32, "g_sb"); b_g = Buf()
            S.dma("sync", lambda e: e.dma_start(out=shift[:], in_=modn[:, 0:2048]), writes=[b_shift])
            S.dma("sync", lambda e: e.dma_start(out=gs[:], in_=modn[:, 2048:4096]), writes=[b_gs])
            S.dma("sync", lambda e: e.dma_start(out=g_sb[:], in_=gn[:, :]), writes=[b_g])
            S.op("vector", lambda e: e.scalar_tensor_tensor(out=gs[:], in0=gs[:], scalar=1.0, in1=g_sb[:], op0=ALU.add, op1=ALU.mult),
                 reads=[b_gs, b_g], writes=[b_gs])
            ident = C.sb([128, 128], BF16, "ident_sb"); b_id = Buf()
            idf = C.sb([128, 128], F32, "idf"); b_idf = Buf()
            idin = C.din("ident", [128, 128], F32)
            S.dma("sync", lambda e: e.dma_start(out=idf[:], in_=idin[:, :]), writes=[b_idf])
            S.op("vector", lambda e: e.tensor_copy(out=ident[:], in_=idf[:]), reads=[b_idf], writes=[b_id])
            if do_outproj:
                ub = [W[1][:, 0:8, :].rearrange("p a b -> p (a b)"), W[1][:, 8:16, :].rearrange("p a b -> p (a b)")]
                b_ub = [b_W[1], b_W[1]]
                uf = gate[:]; b_uf = b_gate
                UT = MZ[:, 0:16, :]; b_UT = [b_MZ]
            else:
                ub = [C.sb([128, 2048], BF16)[:] for _ in range(2)]; b_ub = [Buf(), Buf()]
                uf = C.sb([128, 2048], F32, "uf")[:]; b_uf = Buf()
                UT = C.sb([128, 16, 1024], BF16, "UT")[:]; b_UT = [Buf()]
            PT = [C.ps([128, 512], BF16) for _ in range(2)]; b_PT = [Buf(), Buf()]
            hov = h_out.rearrange("(t p) f -> p t f", p=128)
        else:
            g_sb = C.sb([128, 2048], F32, "g_sb"); b_g = Buf()
            S.dma("sync", lambda e: e.dma_start(out=g_sb[:], in_=gf[:, :]), writes=[b_g])
            yo = [C.sb([128, 2048], F32) for _ in range(2)]; b_yo = [Buf(), Buf()]
            yv = y_out.rearrange("(t p) f -> p t f", p=128)
        nt = 0
        for t in range(8):
            if do_norm:
                S.dma("sync", lambda e, t=t: e.dma_start(out=hov[:, t, :], in_=H[:, t, :]), reads=[b_H[t]], final=True)
            S.op("scalar", lambda e, t=t: e.activation(out=sq, in_=H[:, t, :], func=AF.Square, accum_out=st[:, t, 0:1]),
                 reads=[b_H[t]], writes=[b_sq, b_st[t]])
            S.op("scalar", lambda e, t=t: e.activation(out=st[:, t, 1:2], in_=st[:, t, 0:1], func=AF.Sqrt, scale=1.0 / 2048, bias=EPS),
                 reads=[b_st[t]], writes=[b_st[t]])
            S.op("vector", lambda e, t=t: e.reciprocal(out=st[:, t, 2:3], in_=st[:, t, 1:2]), reads=[b_st[t]], writes=[b_st[t]])
            if do_norm:
                u = ub[t % 2]; bu = b_ub[t % 2]
                S.op("vector", lambda e, t=t: e.scalar_tensor_tensor(out=uf, in0=H[:, t, :], scalar=st[:, t, 2:3], in1=gs[:], op0=ALU.mult, op1=ALU.mult),
                     reads=[b_H[t], b_st[t], b_gs], writes=[b_uf])
                S.op("gpsimd", lambda e, u=u: e.tensor_tensor(out=u, in0=uf, in1=shift[:], op=ALU.add), reads=[b_uf, b_shift], writes=[bu])
                for kq in range(4):
                    pt = PT[nt % 2]; bpt = b_PT[nt % 2]
                    nt += 1
                    for j in range(4):
                        k = kq * 4 + j
                        S.op("tensor", lambda e, pt=pt, u=u, k=k, j=j: e.transpose(out=pt[:, j * 128:(j + 1) * 128], in_=u[:, k * 128:(k + 1) * 128], identity=ident[:]),
                             reads=[bu, b_id], writes=[bpt])
                    S.op("scalar", lambda e, pt=pt, kq=kq, t=t: e.activation(out=UT[:, kq * 4:(kq + 1) * 4, t * 128:(t + 1) * 128],
                                                                              in_=pt[:, :].rearrange("p (a b) -> p a b", a=4), func=AF.Copy),
                         reads=[bpt], writes=[b_UT[0]])
            else:
                y = yo[t % 2]; by = b_yo[t % 2]
                S.op("vector", lambda e, t=t, y=y: e.scalar_tensor_tensor(out=y[:], in0=H[:, t, :], scalar=st[:, t, 2:3], in1=g_sb[:], op0=ALU.mult, op1=ALU.mult),
                     reads=[b_H[t], b_st[t], b_g], writes=[by])
                S.dma("sync", lambda e, t=t, y=y: e.dma_start(out=yv[:, t, :], in_=y[:]), reads=[by], final=True)
        if do_norm:
            uv = uT_out.rearrange("(k p) t -> p k t", p=128)
            for q in range(4):
                S.dma("sync", lambda e, q=q: e.dma_start(out=uv[:, q * 4:(q + 1) * 4, :], in_=UT[:, q * 4:(q + 1) * 4, :]), reads=b_UT, final=True)
    return C.finish()


GL = 3584
GW_A = 3072
GW_B = 2432
NSB = 8
PIPE_ODD = (1, 3)
PIPE_EVEN = (1, 2)


def build_L2(even, lam_init=0.0):
    C = Ctx()
    nc, S = C.nc, C.S
    uT = C.din("uT", [2048, 4096], BF16)
    wc = C.din("wc", [2048, 4096], F32)
    mz_out = C.dout("mz", [1024, 4096], BF16)
    jin = C.din("J", [128, 128], F32)

    PSB = [C.ps([128, 512], F32) for _ in range(8)]
    b_PSB = [Buf() for _ in range(8)]

    cst = C.sb([128, 128], F32, "cst"); b_cst = Buf()
    Jb = C.sb([128, 128], BF16, "Jb"); b_J = Buf()
    S.dma("sync", lambda e: e.dma_start(out=cst[:], in_=jin[:, :]), writes=[b_cst])
    S.op("vector", lambda e: e.tensor_copy(out=Jb[:], in_=cst[:]), reads=[b_cst], writes=[b_J])

    if not even:
        tri_in = C.din("TRI", [128, 128], F32)
        cm_in = C.din("CM", [128, 896], F32)
        TRIb = C.sb([128, 128], BF16, "TRIb"); b_TRI = Buf()
        NEGb = C.sb([128, 128], BF16, "NEGb"); b_NEG = Buf()
        CMb = C.sb([128, 896], BF16, "CMb"); b_CM = Buf()
        cmf = C.sb([128, 896], F32, "cmf"); b_cmf = Buf()
        S.dma("sync", lambda e: e.dma_start(out=cst[:], in_=tri_in[:, :]), writes=[b_cst])
        S.op("vector", lambda e: e.tensor_copy(out=TRIb[:], in_=cst[:]), reads=[b_cst], writes=[b_TRI])
        S.op("vector", lambda e: e.memset(NEGb[:], -1.0), writes=[b_NEG])
        S.dma("sync", lambda e: e.dma_start(out=cmf[:], in_=cm_in[:, :]), writes=[b_cmf])
        S.op("vector", lambda e: e.tensor_copy(out=CMb[:], in_=cmf[:]), reads=[b_cmf], writes=[b_CM])
    else:
        oh_in = C.din("OH", [32, GL], F32)
        ohb_in = C.din("OHB", [32, GL], F32)
        lma_in = C.din("LMA", [1, GL], F32)
        ngb_in = C.din("NGB", [1, GL], F32)
        rb_in = C.din("RB", [32, 8], F32)
        lam_in = C.din("lamv", [1, 256], F32)
        sg_in = C.din("sg", [128, 1], F32)
        frA = C.dscr("frA", [4, GL], BF16); b_frA = Buf()
        frB = C.dscr("frB", [4, GL], BF16); b_frB = Buf()
        ONESb = C.sb([128, 128], BF16, "ONESb"); b_ONES = Buf()
        S.op("vector", lambda e: e.memset(ONESb[:], 1.0), writes=[b_ONES])
        ones4 = C.sb([1, 4], F32, "ones4"); b_ones4 = Buf()
        S.op("vector", lambda e: e.memset(ones4[:], 1.0), writes=[b_ones4])
        RBs = C.sb([32, 8], F32, "RBs"); b_RB = Buf()
        S.dma("sync", lambda e: e.dma_start(out=RBs[:], in_=rb_in[:, :]), writes=[b_RB])
        FRA = C.sb([4, GL], BF16, "FRA"); b_FRA = Buf()
        FRB = C.sb([4, GL], BF16, "FRB"); b_FRB = Buf()
        ohc = [C.sb([32, 512], F32) for _ in range(2)]; b_ohc = [Buf(), Buf()]
        ohbc = [C.sb([32, 512], F32) for _ in range(2)]; b_ohbc = [Buf(), Buf()]
        lmc = [C.sb([1, 512], F32) for _ in range(2)]; b_lmc = [Buf(), Buf()]
        ngc = [C.sb([1, 512], F32) for _ in range(2)]; b_ngc = [Buf(), Buf()]
        for ch in range(GL // 512):
            i = ch % 2
            cs = slice(ch * 512, (ch + 1) * 512)
            S.dma("sync", lambda e, i=i, cs=cs: e.dma_start(out=ohc[i][:], in_=oh_in[:, cs]), writes=[b_ohc[i]])
            S.dma("sync", lambda e, i=i, cs=cs: e.dma_start(out=ohbc[i][:], in_=ohb_in[:, cs]), writes=[b_ohbc[i]])
            S.dma("sync", lambda e, i=i, cs=cs: e.dma_start(out=lmc[i][:], in_=lma_in[:, cs]), writes=[b_lmc[i]])
            S.dma("sync", lambda e, i=i, cs=cs: e.dma_start(out=ngc[i][:], in_=ngb_in[:, cs]), writes=[b_ngc[i]])
            S.op("tensor", lambda e, i=i: e.matmul(PSB[6][0:4, :], lhsT=RBs[:, 0:4], rhs=ohc[i][:], start=True, stop=False),
                 reads=[b_RB, b_ohc[i]], writes=[b_PSB[6]])
            S.op("tensor", lambda e, i=i: e.matmul(PSB[6][0:4, :], lhsT=ones4[0:1, 0:4], rhs=lmc[i][:], start=False, stop=True),
                 reads=[b_ones4, b_lmc[i]], writes=[b_PSB[6]])
            S.op("scalar", lambda e, cs=cs: e.activation(out=FRA[:, cs], in_=PSB[6][0:4, :], func=AF.Copy), reads=[b_PSB[6]], writes=[b_FRA])
            S.op("tensor", lambda e, i=i: e.matmul(PSB[7][0:4, :], lhsT=RBs[:, 4:8], rhs=ohbc[i][:], start=True, stop=False),
                 reads=[b_RB, b_ohbc[i]], writes=[b_PSB[7]])
            S.op("tensor", lambda e, i=i: e.matmul(PSB[7][0:4, :], lhsT=ones4[0:1, 0:4], rhs=ngc[i][:], start=False, stop=True),
                 reads=[b_ones4, b_ngc[i]], writes=[b_PSB[7]])
            S.op("scalar", lambda e, cs=cs: e.activation(out=FRB[:, cs], in_=PSB[7][0:4, :], func=AF.Copy), reads=[b_PSB[7]], writes=[b_FRB])
        S.dma("sync", lambda e: e.dma_start(out=frA[:, :], in_=FRA[:]), reads=[b_FRA], writes=[b_frA])
        S.dma("sync", lambda e: e.dma_start(out=frB[:, :], in_=FRB[:]), reads=[b_FRB], writes=[b_frB])
        GA = [C.sb([128, GW_A], BF16) for _ in range(4)]; b_GA = [Buf() for _ in range(4)]
        GB = [C.sb([128, GW_B], BF16) for _ in range(4)]; b_GB = [Buf() for _ in range(4)]
        for j in range(4):
            srcA = bass.AP(frA.tensor, frA.offset + j * GL, [[1, 128], [1, GW_A]])
            srcB = bass.AP(frB.tensor, frB.offset + j * GL, [[1, 128], [1, GW_B]])
            S.dma("sync", lambda e, j=j, srcA=srcA: e.dma_start(out=GA[j][:], in_=srcA), reads=[b_frA], writes=[b_GA[j]])
            S.dma("sync", lambda e, j=j, srcB=srcB: e.dma_start(out=GB[j][:], in_=srcB), reads=[b_frB], writes=[b_GB[j]])
        lam_sb = C.sb([128, 256], F32, "lam_sb"); b_lam = Buf()
        S.dma("sync", lambda e: e.dma_start(out=lam_sb[:], in_=lam_in[0:1, :].partition_broadcast(128)), writes=[b_lam])
        lt = C.sb([128, 2, 64], F32, "lt"); b_lt = Buf()
        ls = C.sb([128, 8], F32, "ls"); b_ls = Buf()
        S.op("vector", lambda e: e.tensor_tensor(out=lt[:, 0, :], in0=lam_sb[:, 0:64], in1=lam_sb[:, 64:128], op=ALU.mult), reads=[b_lam], writes=[b_lt])
        S.op("vector", lambda e: e.tensor_tensor(out=lt[:, 1, :], in0=lam_sb[:, 128:192], in1=lam_sb[:, 192:256], op=ALU.mult), reads=[b_lam], writes=[b_lt])
        S.op("scalar", lambda e: e.activation(out=lam_sb[:, 0:64], in_=lt[:, 0, :], func=AF.Copy, accum_out=ls[:, 0:1]), reads=[b_lt], writes=[b_lam, b_ls])
        S.op("scalar", lambda e: e.activation(out=lam_sb[:, 64:128], in_=lt[:, 1, :], func=AF.Copy, accum_out=ls[:, 1:2]), reads=[b_lt], writes=[b_lam, b_ls])
        S.op("scalar", lambda e: e.activation(out=ls[:, 2:4], in_=ls[:, 0:2], func=AF.Exp), reads=[b_ls], writes=[b_ls])
        S.op("vector", lambda e: e.tensor_tensor(out=ls[:, 4:5], in0=ls[:, 2:3], in1=ls[:, 3:4], op=ALU.subtract), reads=[b_ls], writes=[b_ls])
        S.op("vector", lambda e: e.tensor_scalar(out=ls[:, 5:6], in0=ls[:, 4:5], scalar1=float(lam_init), scalar2=-1.0, op0=ALU.add, op1=ALU.mult),
             reads=[b_ls], writes=[b_ls])
        neglam = ls[:, 5:6]
        sgs = C.sb([128, 2], F32, "sgs"); b_sg = Buf()
        S.dma("sync", lambda e: e.dma_start(out=sgs[:, 0:1], in_=sg_in[:, :]), writes=[b_sg])
        S.op("vector", lambda e: e.tensor_scalar(out=sgs[:, 1:2], in0=sgs[:, 0:1], scalar1=float(1.0 - lam_init), scalar2=None, op0=ALU.mult),
             reads=[b_sg], writes=[b_sg])
        gsc = sgs[:, 1:2]

    NSET = 2 if not even else 1
    QT = [C.sb([128, 4096], BF16) for _ in range(NSET)]
    KT = [C.sb([128, 4096], BF16) for _ in range(NSET)]
    SZ = [C.sb([128, 4096], BF16) for _ in range(NSET)]
    VV = [C.sb([128, 32, 128], BF16) for _ in range(NSET)]
    b_QT = [[Buf() for _ in range(8)] for _ in range(NSET)]
    b_KT = [[Buf() for _ in range(8)] for _ in range(NSET)]
    b_SZ = [[Buf() for _ in range(8)] for _ in range(NSET)]
    b_VV = [[Buf() for _ in range(8)] for _ in range(NSET)]
    Wh = [C.sb([128, 16, 512], BF16) for _ in range(2)]; b_Wh = [Buf(), Buf()]
    U = [C.sb([128, 16, 512], BF16) for _ in range(2)]; b_U = [Buf(), Buf()]
    wv = wc.rearrange("(k p) c -> p k c", p=128)
    uv = uT.rearrange("(k p) t -> p k t", p=128)
    MZo = [C.sb([128, 512], BF16) for _ in range(2)]; b_MZo = [Buf(), Buf()]
    state = {"u": 0, "pp": 0, "mzo": 0}

    def load_w(j):
        w = Wh[j % 2]
        for q in range(2):
            S.dma("gpsimd", lambda e, w=w, q=q, j=j: e.dma_start(out=w[:, q * 8:(q + 1) * 8, :], in_=wv[:, q * 8:(q + 1) * 8, j * 512:(j + 1) * 512]),
                  writes=[b_Wh[j % 2]])

    def proj(j, qscale):
        s = j % NSET
        w = Wh[j % 2]; bw = b_Wh[j % 2]
        for c in range(8):
            ui = state["u"] % 2
            state["u"] += 1
            u = U[ui]; bu = b_U[ui]
            for q in range(2):
                S.dma("sync", lambda e, u=u, q=q, c=c: e.dma_start(out=u[:, q * 8:(q + 1) * 8, :], in_=uv[:, q * 8:(q + 1) * 8, c * 512:(c + 1) * 512]),
                      writes=[bu])
            cs = slice(c * 512, (c + 1) * 512)
            for which in range(3):
                pi = 6 + state["pp"] % 2
                state["pp"] += 1
                p = PSB[pi]; bp = b_PSB[pi]
                col = (0, 128, 384)[which]
                for kk in range(16):
                    S.op("tensor", lambda e, p=p, w=w, u=u, kk=kk, col=col: e.matmul(p[:, :], lhsT=w[:, kk, col:col + 128], rhs=u[:, kk, :],
                                                                                      start=(kk == 0), stop=(kk == 15)),
                         reads=[bw, bu], writes=[bp])
                if which == 0:
                    S.op("scalar", lambda e, p=p, s=s, cs=cs: e.activation(out=QT[s][:, cs], in_=p[:, :], func=AF.Copy, scale=float(qscale)),
                         reads=[bp], writes=[b_QT[s][c]])
                elif which == 1:
                    S.op("vector", lambda e, p=p, s=s, cs=cs: e.tensor_copy(out=KT[s][:, cs], in_=p[:, :]), reads=[bp], writes=[b_KT[s][c]])
                else:
                    S.op("scalar", lambda e, p=p, s=s, cs=cs: e.activation(out=SZ[s][:, cs], in_=p[:, :], func=AF.Silu), reads=[bp], writes=[b_SZ[s][c]])
            pi = 6 + state["pp"] % 2
            state["pp"] += 1
            p = PSB[pi]; bp = b_PSB[pi]
            for sub in range(4):
                for kk in range(16):
                    S.op("tensor", lambda e, p=p, w=w, u=u, kk=kk, sub=sub: e.matmul(p[:, sub * 128:(sub + 1) * 128], lhsT=u[:, kk, sub * 128:(sub + 1) * 128],
                                                                                      rhs=w[:, kk, 256:384], start=(kk == 0), stop=(kk == 15)),
                         reads=[bw, bu], writes=[bp])
            S.op("vector", lambda e, p=p, s=s, c=c: e.tensor_copy(out=VV[s][:, 4 * c:4 * c + 4, :], in_=p[:, :].rearrange("p (a b) -> p a b", a=4)),
                 reads=[bp], writes=[b_VV[s][c]])

    def store_mz(j, qs, src_fn, reads):
        i = state["mzo"] % 2
        state["mzo"] += 1
        src_fn(MZo[i], b_MZo[i])
        S.dma("sync", lambda e, i=i: e.dma_start(out=mz_out[j * 128:(j + 1) * 128, qs * 512:(qs + 1) * 512], in_=MZo[i][:]),
              reads=[b_MZo[i]], final=True)

    if not even:
        E = [C.sb([128, 512], F32) for _ in range(2)]; b_E = [Buf(), Buf()]
        Lb = [C.sb([128, 512], BF16) for _ in range(3)]; b_Lb = [Buf() for _ in range(3)]
        LA = [C.sb([128, 512], BF16) for _ in range(3)]; b_LA = [Buf() for _ in range(3)]
        Ab = [C.sb([128, 512], BF16) for _ in range(3)]; b_Ab = [Buf() for _ in range(3)]

        def attn_odd(j):
            s = j % NSET
            units = [(qs, kb) for qs in range(NSB) for kb in range(4 * qs + 3, -1, -1)]
            n = len(units)
            lacc = {}

            def S0(u):
                qs, kb = units[u]
                z = PSB[u % 4]; bz = b_PSB[u % 4]
                diag = kb >= 4 * qs
                S.op("tensor", lambda e: e.matmul(z[:, :], lhsT=KT[s][:, kb * 128:(kb + 1) * 128], rhs=QT[s][:, qs * 512:(qs + 1) * 512],
                                                  start=True, stop=False),
                     reads=[b_KT[s][kb // 4], b_QT[s][qs]], writes=[bz])
                if diag:
                    x0 = 512 * qs - 128 * kb + 384
                    S.op("tensor", lambda e: e.matmul(z[:, :], lhsT=Jb[:], rhs=CMb[:, x0:x0 + 512], start=False, stop=False),
                         reads=[b_J, b_CM], writes=[bz])

            def S1(u):
                z = PSB[u % 4]; bz = b_PSB[u % 4]
                ee = E[u % 2]; be = b_E[u % 2]
                S.op("scalar", lambda e: e.activation(out=ee[:], in_=z[:, :], func=AF.Exp), reads=[bz], writes=[be])
                S.op("scalar", lambda e: e.activation(out=Lb[u % 3][:], in_=ee[:], func=AF.Ln, bias=1.0), reads=[be], writes=[b_Lb[u % 3]])

            def S23(u):
                qs, kb = units[u]
                z = PSB[u % 4]; bz = b_PSB[u % 4]
                first = kb == 4 * qs + 3
                last = kb == 0
                S.op("tensor", lambda e: e.matmul(z[:, :], lhsT=TRIb[:], rhs=Lb[u % 3][:], start=False, stop=first),
                     reads=[b_TRI, b_Lb[u % 3]], writes=[bz])
                if not first:
                    pa, pb = lacc[u - 1]
                    S.op("tensor", lambda e: e.matmul(z[:, :], lhsT=NEGb[:], rhs=pa[:], start=False, stop=True),
                         reads=[b_NEG, pb], writes=[bz])
                if not last:
                    if first:
                        lacc[u] = (Lb[u % 3], b_Lb[u % 3])
                    else:
                        pa, pb = lacc[u - 1]
                        S.op("vector", lambda e: e.tensor_tensor(out=LA[u % 3][:], in0=pa[:], in1=Lb[u % 3][:], op=ALU.add),
                             reads=[pb, b_Lb[u % 3]], writes=[b_LA[u % 3]])
                        lacc[u] = (LA[u % 3], b_LA[u % 3])
                lacc.pop(u - 2, None)

            def S45(u):
                qs, kb = units[u]
                z = PSB[u % 4]; bz = b_PSB[u % 4]
                first = kb == 4 * qs + 3
                last = kb == 0
                o = PSB[4 + qs % 2]; bo = b_PSB[4 + qs % 2]
                S.op("scalar", lambda e: e.activation(out=Ab[u % 3][:], in_=z[:, :], func=AF.Exp), reads=[bz], writes=[b_Ab[u % 3]])
                S.op("tensor", lambda e: e.matmul(o[:, :], lhsT=VV[s][:, kb, :], rhs=Ab[u % 3][:], start=first, stop=last),
                     reads=[b_VV[s][kb // 4], b_Ab[u % 3]], writes=[bo])
                if last:
                    def fin(dst, bdst):
                        S.op("vector", lambda e: e.tensor_tensor(out=dst[:], in0=o[:, :], in1=SZ[s][:, qs * 512:(qs + 1) * 512], op=ALU.mult),
                             reads=[bo, b_SZ[s][qs]], writes=[bdst])
                    store_mz(j, qs, fin, None)

            d1, d2 = PIPE_ODD
            for i in range(n + d2):
                if i < n:
                    S0(i)
                if 0 <= i - d1 < n:
                    S1(i - d1)
                    S23(i - d1)
                if 0 <= i - d2 < n:
                    S45(i - d2)

    else:
        Pb = [C.sb([128, 512], BF16) for _ in range(3)]; b_Pb = [Buf() for _ in range(3)]
        Rr = [C.sb([128, 512], F32) for _ in range(2)]; b_Rr = [Buf(), Buf()]
        On = [C.sb([128, 512], F32) for _ in range(3)]; b_On = [Buf() for _ in range(3)]
        Dd = C.sb([128, 512], F32, "Dd"); b_Dd = Buf()
        Dq = C.sb([128, 512], BF16, "Dq"); b_Dq = Buf()
        Sd = C.sb([128, 512], F32, "Sd"); b_Sd = Buf()
        cnt = {"u": 0, "acc": 0, "on": 0, "rr": 0}

        def softmax_sb(s, qs, rows, G, bG, is_A):
            if is_A:
                kbs = [kb for kb in range(4 * qs + 3, -1, -1) if 512 * qs - 128 * kb <= 2176]
            else:
                kbs = list(range(4 * qs + 3, -1, -1))
            ai = cnt["acc"] % 2
            cnt["acc"] += 1
            o = PSB[2 + 2 * ai]; bo = b_PSB[2 + 2 * ai]
            l = PSB[3 + 2 * ai]; bl = b_PSB[3 + 2 * ai]
            n = len(kbs)
            ids = []

            def S0(i):
                kb = kbs[i]
                u = cnt["u"]; cnt["u"] += 1
                ids.append(u)
                sp = PSB[u % 2]; bs = b_PSB[u % 2]
                D = 512 * qs - 128 * kb
                biased = is_A or D <= 1536
                S.op("tensor", lambda e: e.matmul(sp[:, :], lhsT=KT[s][rows, kb * 128:(kb + 1) * 128], rhs=QT[s][rows, qs * 512:(qs + 1) * 512],
                                                  start=True, stop=not biased),
                     reads=[b_KT[s][kb // 4], b_QT[s][qs]], writes=[bs])
                if biased:
                    x0 = D + 384
                    S.op("tensor", lambda e: e.matmul(sp[:, :], lhsT=Jb[:], rhs=G[:, x0:x0 + 512], start=False, stop=True),
                         reads=[b_J, bG], writes=[bs])

            def S1(i):
                u = ids[i]
                sp = PSB[u % 2]; bs = b_PSB[u % 2]
                S.op("scalar", lambda e: e.activation(out=Pb[u % 3][:], in_=sp[:, :], func=AF.Exp), reads=[bs], writes=[b_Pb[u % 3]])

            def S2(i):
                u = ids[i]
                kb = kbs[i]
                S.op("tensor", lambda e: e.matmul(o[:, :], lhsT=VV[s][:, kb, :], rhs=Pb[u % 3][:], start=(i == 0), stop=(i == n - 1)),
                     reads=[b_VV[s][kb // 4], b_Pb[u % 3]], writes=[bo])
                S.op("tensor", lambda e: e.matmul(l[:, :], lhsT=ONESb[:], rhs=Pb[u % 3][:], start=(i == 0), stop=(i == n - 1)),
                     reads=[b_ONES, b_Pb[u % 3]], writes=[bl])

            d1, d2 = PIPE_EVEN
            for i in range(n + d2):
                if i < n:
                    S0(i)
                if 0 <= i - d1 < n:
                    S1(i - d1)
                if 0 <= i - d2 < n:
                    S2(i - d2)
            ri = cnt["rr"] % 2; cnt["rr"] += 1
            oi = cnt["on"] % 3; cnt["on"] += 1
            S.op("vector", lambda e: e.reciprocal(out=Rr[ri][:], in_=l[:, :]), reads=[bl], writes=[b_Rr[ri]])
            S.op("vector", lambda e: e.tensor_tensor(out=On[oi][:], in0=o[:, :], in1=Rr[ri][:], op=ALU.mult), reads=[bo, b_Rr[ri]], writes=[b_On[oi]])
            return On[oi], b_On[oi]

        def attn_A(j):
            s = j % NSET
            for qs in range(NSB):
                on, bon = softmax_sb(s, qs, slice(0, 128), GA[j], b_GA[j], True)

                def fin(dst, bdst, on=on, bon=bon, qs=qs):
                    S.op("vector", lambda e: e.tensor_tensor(out=dst[:], in0=on[:], in1=SZ[s][:, qs * 512:(qs + 1) * 512], op=ALU.mult),
                         reads=[bon, b_SZ[s][qs]], writes=[bdst])
                store_mz(j, qs, fin, None)

        def attn_B(j):
            s = j % NSET
            jb = j - 4
            for qs in range(NSB):
                o1, bo1 = softmax_sb(s, qs, slice(0, 64), GB[jb], b_GB[jb], False)
                o2, bo2 = softmax_sb(s, qs, slice(64, 128), GB[jb], b_GB[jb], False)
                S.op("vector", lambda e, o1=o1, o2=o2: e.scalar_tensor_tensor(out=Dd[:], in0=o2[:], scalar=neglam, in1=o1[:], op0=ALU.mult, op1=ALU.add),
                     reads=[bo1, bo2, b_ls], writes=[b_Dd])
                S.op("vector", lambda e: e.tensor_tensor(out=Dq[:], in0=Dd[:], in1=Dd[:], op=ALU.mult), reads=[b_Dd], writes=[b_Dq])
                S.op("tensor", lambda e: e.matmul(PSB[6][:, :], lhsT=ONESb[:], rhs=Dq[:], start=True, stop=True), reads=[b_ONES, b_Dq], writes=[b_PSB[6]])
                S.op("scalar", lambda e: e.activation(out=Sd[:], in_=PSB[6][:, :], func=AF.Sqrt, scale=1.0 / 128, bias=EPS), reads=[b_PSB[6]], writes=[b_Sd])
                S.op("vector", lambda e: e.reciprocal(out=Sd[:], in_=Sd[:]), reads=[b_Sd], writes=[b_Sd])
                S.op("vector", lambda e: e.tensor_tensor(out=Dd[:], in0=Dd[:], in1=Sd[:], op=ALU.mult), reads=[b_Dd, b_Sd], writes=[b_Dd])

                def fin(dst, bdst, qs=qs):
                    S.op("vector", lambda e: e.scalar_tensor_tensor(out=dst[:], in0=Dd[:], scalar=gsc, in1=SZ[s][:, qs * 512:(qs + 1) * 512],
                                                                     op0=ALU.mult, op1=ALU.mult),
                         reads=[b_Dd, b_sg, b_SZ[s][qs]], writes=[bdst])
                store_mz(j, qs, fin, None)

    load_w(0)
    for j in range(8):
        if j + 1 < 8:
            load_w(j + 1)
        if not even:
            proj(j, 128 ** -0.5)
            attn_odd(j)
        elif j < 4:
            proj(j, 128 ** -0.5)
            attn_A(j)
        else:
            proj(j, 64 ** -0.5)
            attn_B(j)
    return C.finish()


def _t5_bucket_np(dist):
    dist = np.asarray(dist, dtype=np.int64)
    d = np.maximum(dist, 1).astype(np.float32)
    ratio = (np.log(d / np.float32(16.0)) / np.float32(math.log(2048 / 16)) * np.float32(16.0)).astype(np.float32)
    large = 16 + ratio.astype(np.int32)
    large = np.minimum(large, 31)
    return np.where(dist < 16, dist, large).astype(np.int64)


_CONST_CACHE = {}


def host_consts():
    if _CONST_CACHE:
        return _CONST_CACHE
    J = np.zeros((128, 128), np.float32)
    J[np.arange(128), 127 - np.arange(128)] = 1.0
    jj, ss = np.meshgrid(np.arange(128), np.arange(128), indexing="ij")
    TRI = np.where(jj >= ss, -1.0, 0.0).astype(np.float32)
    kk, xx = np.meshgrid(np.arange(128), np.arange(896), indexing="ij")
    CM = np.where(xx + kk - 511 <= 0, NEG, 0.0).astype(np.float32)
    idx = np.arange(GL)
    delta = idx - 511
    valid = delta >= 0
    bk = _t5_bucket_np(np.maximum(delta, 0))
    OH = np.zeros((32, GL), np.float32)
    OH[bk[valid], idx[valid]] = 1.0
    OHB = OH.copy()
    OHB[31, valid] -= 1.0
    mult = ((delta <= 128).astype(np.int64) + ((delta % 4 == 0) & (delta <= 512)).astype(np.int64)
            + ((delta % 16 == 0) & (delta <= 2048)).astype(np.int64))
    mult = np.where(valid, mult, 0)
    LMA = np.where(mult > 0, np.log(np.maximum(mult, 1).astype(np.float64)), NEG).astype(np.float32).reshape(1, GL)
    NGB = np.where(valid, 0.0, NEG).astype(np.float32).reshape(1, GL)
    _CONST_CACHE.update(J=J, TRI=TRI, CM=CM, OH=OH, OHB=OHB, LMA=LMA, NGB=NGB)
    return _CONST_CACHE


def head_cols(g, j, even):
    if not even:
        h = 8 * g + j
        return [(0, h * 128, 128), (128, 4096 + h * 128, 128), (256, 8192 + h * 128, 128), (384, 12288 + h * 128, 128)]
    if j < 4:
        h = 4 * g + j
        return [(0, h * 128, 128), (128, 2048 + h * 128, 128), (256, 4096 + h * 128, 128), (384, 12288 + h * 128, 128)]
    h = 4 * g + j - 4
    return [(0, 6144 + h * 64, 64), (64, 7168 + h * 64, 64), (128, 8192 + h * 64, 64), (192, 9216 + h * 64, 64),
            (256, 10240 + h * 128, 128), (384, 12288 + 2048 + h * 128, 128)]


def make_wc(w_in_l, g, even):
    wc = np.empty((2048, 4096), np.float32)
    for j in range(8):
        for d, s, w in head_cols(g, j, even):
            wc[:, j * 512 + d:j * 512 + d + w] = w_in_l[:, s:s + w]
    return wc


def inner_row(g, j, even):
    if not even:
        return (8 * g + j) * 128
    if j < 4:
        return (4 * g + j) * 128
    return 2048 + (4 * g + j - 4) * 128


U8 = mybir.dt.uint8
ARENA_KB = 200
GROUPS4 = [[0, 1, 2, 3], [4, 5, 6, 7]]


def _dtsize(dt):
    return 4 if dt == F32 else 2


class Arena:
    def __init__(self, C):
        self.t = C.sb([128, ARENA_KB * 1024], U8, "arena")
        self.base = 0
        self.off = 0

    def alloc(self, shape, dt):
        n = _dtsize(dt)
        for d in shape[1:]:
            n *= d
        n = (n + 63) // 64 * 64
        assert self.off + n <= ARENA_KB * 1024, ("arena overflow", self.off, n)
        v = self.t[:, self.off:self.off + n].bitcast(dt)
        self.off += n
        nel = 1
        for d in shape[1:]:
            nel *= d
        v = v[:, 0:nel]
        if len(shape) == 3:
            v = v.rearrange("p (a b) -> p a b", a=shape[1])
        if shape[0] < 128:
            v = v[0:shape[0]]
        return v

    def persist(self):
        self.base = self.off

    def reset(self):
        self.off = self.base


def build_fused(nphase=None, dbg=False):
    C = Ctx()
    nc, S = C.nc, C.S
    A = Arena(C)
    hc_names = {}
    x_in = C.din("x", [1024, 2048], F32)
    cT = C.din("cT", [128, 16], F32)
    wm = C.din("wm", [2048, 6144], F32)
    bm = C.din("bm", [1, 6144], F32)
    wc = C.din("wc", [DEPTH, 2048, 4096], F32) if (nphase is None or nphase >= 4) else None
    wout = C.din("wout", [DEPTH, 4096, 2048], F32) if (nphase is None or nphase >= 5) else None
    gn = C.din("gn", [DEPTH, 2048], F32)
    gf = C.din("gf", [1, 2048], F32)
    id_in = C.din("ident", [128, 128], F32)
    j_in = C.din("J", [128, 128], F32)
    tri_in = C.din("TRI", [128, 128], F32)
    cm_in = C.din("CM", [128, 896], F32)
    oh_in = C.din("OH", [32, GL], F32)
    ohb_in = C.din("OHB", [32, GL], F32)
    lma_in = C.din("LMA", [1, GL], F32)
    ngb_in = C.din("NGB", [1, GL], F32)
    rb_in = C.din("RB", [32, 8], F32)
    lam_in = C.din("lamv", [2, 256], F32)
    sg_in = C.din("sg", [2, 128, 1], F32)
    y_out = C.dout("y", [1024, 2048], F32)

    mod_loc = C.dscr("mod_loc", [1, 6144], F32); b_modloc = Buf()
    mod_all = C.dscr("mod_all", [4, 6144], F32); b_modall = Buf()
    uT_loc = C.dscr("uT_loc", [2048, 1024], BF16); b_uTloc = [Buf() for _ in range(4)]
    uT_all = C.dscr("uT_all", [4 * 2048, 1024], BF16); b_uTall = [Buf() for _ in range(4)]
    mz_loc = C.dscr("mz_loc", [1024, 4096], BF16); b_mzloc = [Buf() for _ in range(8)]
    mz_all = C.dscr("mz_all", [4 * 1024, 4096], BF16); b_mzall = Buf()
    frA = C.dscr("frA", [4, GL], BF16); b_frA = Buf()
    frB = C.dscr("frB", [4, GL], BF16); b_frB = Buf()

    PSB = [C.ps([128, 512], F32) for _ in range(8)]
    b_PSB = [Buf() for _ in range(8)]

    H = A.alloc([128, 8, 2048], F32); b_H = [Buf() for _ in range(8)]
    ident = A.alloc([128, 128], BF16); b_id = Buf()
    Jb = A.alloc([128, 128], BF16); b_J = Buf()
    TRIb = A.alloc([128, 128], BF16); b_TRI = Buf()
    NEGb = A.alloc([128, 128], BF16); b_NEG = Buf()
    ONESb = A.alloc([128, 128], BF16); b_ONES = Buf()
    CMb = A.alloc([128, 896], BF16); b_CM = Buf()
    A.persist()
    S.dma("gpsimd", lambda e: e.dma_start(out=ident, in_=id_in[:, :]), writes=[b_id])
    S.dma("gpsimd", lambda e: e.dma_start(out=Jb, in_=j_in[:, :]), writes=[b_J])
    S.dma("gpsimd", lambda e: e.dma_start(out=TRIb, in_=tri_in[:, :]), writes=[b_TRI])
    S.dma("gpsimd", lambda e: e.dma_start(out=CMb, in_=cm_in[:, :]), writes=[b_CM])
    S.op("vector", lambda e: e.memset(NEGb, -1.0), writes=[b_NEG])
    S.op("vector", lambda e: e.memset(ONESb, 1.0), writes=[b_ONES])
    hv = x_in.rearrange("(t p) f -> p t f", p=128)
    for t in range(8):
        S.dma("sync", lambda e, t=t: e.dma_start(out=H[:, t, :], in_=hv[:, t, :]), writes=[b_H[t]])

    def phase_M():
        c_sb = A.alloc([128, 16], F32); b_c = Buf()
        e_sb = A.alloc([128, 16], F32); b_e = Buf()
        ca = A.alloc([128, 16], F32); b_ca = Buf()
        ones = A.alloc([128, 128], F32); b_ones = Buf()
        L = A.alloc([128, 16, 128], F32); b_L = Buf()
        bmr = A.alloc([1, 6144], F32); b_bmr = Buf()
        W = [A.alloc([128, 16, 512], F32) for _ in range(2)]; b_W = [Buf(), Buf()]
        res = A.alloc([1, 6144], F32); b_res = Buf()
        S.dma("sync", lambda e: e.dma_start(out=c_sb, in_=cT[:, :]), writes=[b_c])
        S.dma("sync", lambda e: e.dma_start(out=bmr, in_=bm[:, :]), writes=[b_bmr])
        S.op("vector", lambda e: e.memset(ones, 1.0), writes=[b_ones])
        S.op("scalar", lambda e: e.activation(out=e_sb, in_=c_sb, func=AF.Exp, scale=-1.0), reads=[b_c], writes=[b_e])
        S.op("vector", lambda e: e.tensor_scalar(out=e_sb, in0=e_sb, scalar1=1.0, scalar2=None, op0=ALU.add), reads=[b_e], writes=[b_e])
        S.op("vector", lambda e: e.reciprocal(out=e_sb, in_=e_sb), reads=[b_e], writes=[b_e])
        S.op("vector", lambda e: e.tensor_tensor(out=ca, in0=c_sb, in1=e_sb, op=ALU.mult), reads=[b_c, b_e], writes=[b_ca])
        for k in range(16):
            S.op("vector", lambda e, k=k: e.tensor_scalar(out=L[:, k, :], in0=ones, scalar1=ca[:, k:k + 1], scalar2=None, op0=ALU.mult),
                 reads=[b_ones, b_ca], writes=[b_L])
        wv = wm.rearrange("(k p) c -> p k c", p=128)
        for n in range(12):
            w = W[n % 2]; bw = b_W[n % 2]; p = PSB[n % 2]; bp = b_PSB[n % 2]
            S.dma("sync", lambda e, w=w, n=n: e.dma_start(out=w, in_=wv[:, :, n * 512:(n + 1) * 512]), writes=[bw])
            for k in range(16):
                S.op("tensor", lambda e, w=w, p=p, k=k: e.matmul(p[0:1, :], lhsT=L[:, k, 0:1], rhs=w[:, k, :], start=(k == 0), stop=False),
                     reads=[b_L, bw], writes=[bp])
            S.op("tensor", lambda e, p=p, n=n: e.matmul(p[0:1, :], lhsT=ones[0:1, 0:1], rhs=bmr[0:1, n * 512:(n + 1) * 512], start=False, stop=True),
                 reads=[b_ones, b_bmr], writes=[bp])
            S.op("scalar", lambda e, p=p, n=n: e.activation(out=res[0:1, n * 512:(n + 1) * 512], in_=p[0:1, :], func=AF.Copy), reads=[bp], writes=[b_res])
        S.dma("sync", lambda e: e.dma_start(out=mod_loc[:, :], in_=res), reads=[b_res], writes=[b_modloc])
        S.cc(lambda e: e.collective_compute("AllGather", ALU.bypass, replica_groups=GROUPS4, ins=[mod_loc[:, :]], outs=[mod_all[:, :]]),
             reads=[b_modloc], writes=[b_modall])

    def phase_G():
        ones4 = A.alloc([1, 4], F32); b_ones4 = Buf()
        S.op("vector", lambda e: e.memset(ones4, 1.0), writes=[b_ones4])
        RBs = A.alloc([32, 8], F32); b_RB = Buf()
        S.dma("sync", lambda e: e.dma_start(out=RBs, in_=rb_in[:, :]), writes=[b_RB])
        FRA = A.alloc([4, GL], BF16); b_FRA = Buf()
        FRB = A.alloc([4, GL], BF16); b_FRB = Buf()
        ohc = [A.alloc([32, 512], F32) for _ in range(2)]; b_ohc = [Buf(), Buf()]
        ohbc = [A.alloc([32, 512], F32) for _ in range(2)]; b_ohbc = [Buf(), Buf()]
        lmc = [A.alloc([1, 512], F32) for _ in range(2)]; b_lmc = [Buf(), Buf()]
        ngc = [A.alloc([1, 512], F32) for _ in range(2)]; b_ngc = [Buf(), Buf()]
        for ch in range(GL // 512):
            i = ch % 2
            cs = slice(ch * 512, (ch + 1) * 512)
            S.dma("sync", lambda e, i=i, cs=cs: e.dma_start(out=ohc[i], in_=oh_in[:, cs]), writes=[b_ohc[i]])
            S.dma("sync", lambda e, i=i, cs=cs: e.dma_start(out=ohbc[i], in_=ohb_in[:, cs]), writes=[b_ohbc[i]])
            S.dma("sync", lambda e, i=i, cs=cs: e.dma_start(out=lmc[i], in_=lma_in[:, cs]), writes=[b_lmc[i]])
            S.dma("sync", lambda e, i=i, cs=cs: e.dma_start(out=ngc[i], in_=ngb_in[:, cs]), writes=[b_ngc[i]])
            S.op("tensor", lambda e, i=i: e.matmul(PSB[6][0:4, :], lhsT=RBs[:, 0:4], rhs=ohc[i], start=True, stop=False),
                 reads=[b_RB, b_ohc[i]], writes=[b_PSB[6]])
            S.op("tensor", lambda e, i=i: e.matmul(PSB[6][0:4, :], lhsT=ones4[0:1, 0:4], rhs=lmc[i], start=False, stop=True),
                 reads=[b_ones4, b_lmc[i]], writes=[b_PSB[6]])
            S.op("scalar", lambda e, cs=cs: e.activation(out=FRA[:, cs], in_=PSB[6][0:4, :], func=AF.Copy), reads=[b_PSB[6]], writes=[b_FRA])
            S.op("tensor", lambda e, i=i: e.matmul(PSB[7][0:4, :], lhsT=RBs[:, 4:8], rhs=ohbc[i], start=True, stop=False),
                 reads=[b_RB, b_ohbc[i]], writes=[b_PSB[7]])
            S.op("tensor", lambda e, i=i: e.matmul(PSB[7][0:4, :], lhsT=ones4[0:1, 0:4], rhs=ngc[i], start=False, stop=True),
                 reads=[b_ones4, b_ngc[i]], writes=[b_PSB[7]])
            S.op("scalar", lambda e, cs=cs: e.activation(out=FRB[:, cs], in_=PSB[7][0:4, :], func=AF.Copy), reads=[b_PSB[7]], writes=[b_FRB])
        S.dma("sync", lambda e: e.dma_start(out=frA[:, :], in_=FRA), reads=[b_FRA], writes=[b_frA])
        S.dma("sync", lambda e: e.dma_start(out=frB[:, :], in_=FRB), reads=[b_FRB], writes=[b_frB])

    def phase_L31(l_prev, l_next, final):
        do_outproj = l_prev is not None
        do_norm = l_next is not None
        if do_outproj:
            even = l_prev % 2 == 0
            MZ = A.alloc([128, 32, 1024], BF16); b_MZ = Buf()
            W = [A.alloc([128, 32, 256], BF16) for _ in range(2)]; b_W = [Buf(), Buf()]
            gate = A.alloc([128, 2048], F32); b_gate = Buf()
            tmp = [A.alloc([128, 256], F32) for _ in range(2)]; b_tmp = [Buf(), Buf()]
            mzv = mz_all.rearrange("(k p) t -> p k t", p=128)
            for q in range(4):
                S.dma("sync", lambda e, q=q: e.dma_start(out=MZ[:, q * 8:(q + 1) * 8, :],
                                                          in_=mzv[:, q * 8:(q + 1) * 8, bass.ds((S.rt["pid"] % 4) * 1024, 1024)]),
                      reads=[b_mzall], writes=[b_MZ])
            S.dma("sync", lambda e: e.dma_start(out=gate, in_=mod_all[l_prev:l_prev + 1, 4096:6144].partition_broadcast(128)),
                  reads=[b_modall], writes=[b_gate])
            if not even:
                wsrc = [wout[l_prev].rearrange("(g j p) c -> p j g c", g=4, j=8)[:, j] for j in range(8)]
            else:
                wa = wout[l_prev][0:2048, :].rearrange("(g j p) c -> p j g c", g=4, j=4)
                wb = wout[l_prev][2048:4096, :].rearrange("(g j p) c -> p j g c", g=4, j=4)
                wsrc = [wa[:, j] for j in range(4)] + [wb[:, j] for j in range(4)]
            def load_wout(f):
                w = W[f % 2]; bw = b_W[f % 2]
                fs = slice(f * 256, (f + 1) * 256)
                for j in range(8):
                    S.dma("gpsimd", lambda e, w=w, fs=fs, j=j: e.dma_start(out=w[:, j * 4:(j + 1) * 4, :], in_=wsrc[j][:, :, fs]), writes=[bw])

            it = 0
            load_wout(0)
            for f in range(8):
                w = W[f % 2]; bw = b_W[f % 2]
                fs = slice(f * 256, (f + 1) * 256)
                if f + 1 < 8:
                    load_wout(f + 1)
                for t in range(8):
                    p = PSB[it % 4]; bp = b_PSB[it % 4]; tm = tmp[it % 2]; btm = b_tmp[it % 2]
                    it += 1
                    for k in range(32):
                        S.op("tensor", lambda e, p=p, w=w, k=k, t=t: e.matmul(p[:, 0:256], lhsT=MZ[:, k, t * 128:(t + 1) * 128], rhs=w[:, k, :],
                                                                                start=(k == 0), stop=(k == 31)),
                             reads=[b_MZ, bw], writes=[bp])
                    S.op("vector", lambda e, p=p, tm=tm, fs=fs: e.tensor_tensor(out=tm, in0=p[:, 0:256], in1=gate[:, fs], op=ALU.mult),
                         reads=[bp, b_gate], writes=[btm])
                    S.op("vector", lambda e, tm=tm, t=t, fs=fs: e.tensor_tensor(out=H[:, t, fs], in0=H[:, t, fs], in1=tm, op=ALU.add),
                         reads=[btm, b_H[t]], writes=[b_H[t]])
            sq = W[0][:, 16:24, :].rearrange("p a b -> p (a b)"); b_sq = b_W[0]
            ub = [W[1][:, 0:8, :].rearrange("p a b -> p (a b)"), W[1][:, 8:16, :].rearrange("p a b -> p (a b)")]
            b_ub = [b_W[1], b_W[1]]
            uf = gate; b_uf = b_gate
            UT = MZ[:, 0:16, :]; b_UT = b_MZ
        else:
            sq = A.alloc([128, 2048], BF16); b_sq = Buf()
            ub = [A.alloc([128, 2048], BF16) for _ in range(2)]; b_ub = [Buf(), Buf()]
            uf = A.alloc([128, 2048], F32); b_uf = Buf()
            UT = A.alloc([128, 16, 1024], BF16); b_UT = Buf()
        st = A.alloc([128, 8, 4], F32); b_st = [Buf() for _ in range(8)]
        gs = A.alloc([128, 2048], F32); b_gs = Buf()
        if do_norm:
            shift = A.alloc([128, 2048], F32); b_shift = Buf()
            S.dma("sync", lambda e: e.dma_start(out=shift, in_=mod_all[l_next:l_next + 1, 0:2048].partition_broadcast(128)), reads=[b_modall], writes=[b_shift])
            S.dma("sync", lambda e: e.dma_start(out=gs, in_=mod_all[l_next:l_next + 1, 2048:4096].partition_broadcast(128)), reads=[b_modall], writes=[b_gs])
            S.dma("sync", lambda e: e.dma_start(out=uf, in_=gn[l_next:l_next + 1, :].partition_broadcast(128)), writes=[b_uf])
            S.op("vector", lambda e: e.scalar_tensor_tensor(out=gs, in0=gs, scalar=1.0, in1=uf, op0=ALU.add, op1=ALU.mult),
                 reads=[b_gs, b_uf], writes=[b_gs])
        else:
            S.dma("sync", lambda e: e.dma_start(out=gs, in_=gf[0:1, :].partition_broadcast(128)), writes=[b_gs])
            yo = [uf, A.alloc([128, 2048], F32)]; b_yo = [b_uf, Buf()]
            yv = y_out.rearrange("(t p) f -> p t f", p=128)
        nt = 0
        for t in range(8):
            S.op("scalar", lambda e, t=t: e.activation(out=sq, in_=H[:, t, :], func=AF.Square, accum_out=st[:, t, 0:1]),
                 reads=[b_H[t]], writes=[b_sq, b_st[t]])
            S.op("scalar", lambda e, t=t: e.activation(out=st[:, t, 1:2], in_=st[:, t, 0:1], func=AF.Sqrt, scale=1.0 / 2048, bias=EPS),
                 reads=[b_st[t]], writes=[b_st[t]])
            S.op("vector", lambda e, t=t: e.reciprocal(out=st[:, t, 2:3], in_=st[:, t, 1:2]), reads=[b_st[t]], writes=[b_st[t]])
            if do_norm:
                u = ub[t % 2]; bu = b_ub[t % 2]
                S.op("vector", lambda e, t=t: e.scalar_tensor_tensor(out=uf, in0=H[:, t, :], scalar=st[:, t, 2:3], in1=gs, op0=ALU.mult, op1=ALU.mult),
                     reads=[b_H[t], b_st[t], b_gs], writes=[b_uf])
                S.op("gpsimd", lambda e, u=u: e.tensor_tensor(out=u, in0=uf, in1=shift, op=ALU.add), reads=[b_uf, b_shift], writes=[bu])
                for kq in range(4):
                    pi = 4 + nt % 2
                    nt += 1
                    pt = PSB[pi][:, 0:256].bitcast(BF16); bpt = b_PSB[pi]
                    for j in range(4):
                        k = kq * 4 + j
                        S.op("tensor", lambda e, pt=pt, u=u, k=k, j=j: e.transpose(out=pt[:, j * 128:(j + 1) * 128], in_=u[:, k * 128:(k + 1) * 128], identity=ident),
                             reads=[bu, b_id], writes=[bpt])
                    S.op("scalar", lambda e, pt=pt, kq=kq, t=t: e.activation(out=UT[:, kq * 4:(kq + 1) * 4, t * 128:(t + 1) * 128],
                                                                              in_=pt.rearrange("p (a b) -> p a b", a=4), func=AF.Copy),
                         reads=[bpt], writes=[b_UT])
            else:
                y = yo[t % 2]; by = b_yo[t % 2]
                S.op("vector", lambda e, t=t, y=y: e.scalar_tensor_tensor(out=y, in0=H[:, t, :], scalar=st[:, t, 2:3], in1=gs, op0=ALU.mult, op1=ALU.mult),
                     reads=[b_H[t], b_st[t], b_gs], writes=[by])
                S.dma("sync", lambda e, t=t, y=y: e.dma_start(out=yv[:, t, :], in_=y), reads=[by], final=True)
        if do_norm:
            uv = uT_loc.rearrange("(k p) t -> p k t", p=128)
            for q in range(4):
                S.dma("sync", lambda e, q=q: e.dma_start(out=uv[:, q * 4:(q + 1) * 4, :], in_=UT[:, q * 4:(q + 1) * 4, :]), reads=[b_UT], writes=[b_uTloc[q]])
                S.cc(lambda e, q=q: e.collective_compute("AllGather", ALU.bypass, replica_groups=GROUPS4, ins=[uT_loc[q * 512:(q + 1) * 512, :]],
                                                         outs=[uT_all[q * 2048:(q + 1) * 2048, :]]),
                     reads=[b_uTloc[q]], writes=[b_uTall[q]])

    def phase_L2(l):
        even = l % 2 == 0
        lam_init = 0.8 - 0.6 * math.exp(-0.3 * l)
        NSET = 1 if even else 2
        CW = 512 if even else 256
        QTs = [A.alloc([128, 4096], BF16) for _ in range(NSET)]; KTs = [A.alloc([128, 4096], BF16) for _ in range(NSET)]
        SZs = [A.alloc([128, 4096], BF16) for _ in range(NSET)]; VVs = [A.alloc([128, 32, 128], BF16) for _ in range(NSET)]
        b_QTs = [[Buf() for _ in range(8)] for _ in range(NSET)]; b_KTs = [[Buf() for _ in range(8)] for _ in range(NSET)]
        b_SZs = [[Buf() for _ in range(8)] for _ in range(NSET)]; b_VVs = [[Buf() for _ in range(8)] for _ in range(NSET)]
        QT, KT, SZ, VV = QTs[0], KTs[0], SZs[0], VVs[0]
        b_QT, b_KT, b_SZ, b_VV = b_QTs[0], b_KTs[0], b_SZs[0], b_VVs[0]
        Wh = [A.alloc([128, 16, 512], BF16) for _ in range(2)]; b_Wh = [Buf(), Buf()]
        U = [A.alloc([128, 16, CW], BF16) for _ in range(2)]; b_U = [Buf(), Buf()]
        MZo = [A.alloc([128, 512], BF16) for _ in range(2)]; b_MZo = [Buf(), Buf()]
        wv = wc[l].rearrange("(k p) c -> p k c", p=128)
        state = {"u": 0, "pp": 0, "mzo": 0}

        def load_w(j):
            w = Wh[j % 2]
            for q in range(2):
                S.dma("gpsimd", lambda e, w=w, q=q, j=j: e.dma_start(out=w[:, q * 8:(q + 1) * 8, :], in_=wv[:, q * 8:(q + 1) * 8, j * 512:(j + 1) * 512]),
                      writes=[b_Wh[j % 2]])

        def proj_items(j, qscale):
            s = j % NSET
            w = Wh[j % 2]; bw = b_Wh[j % 2]
            items = []
            NCH = 4096 // CW
            ubuf = {}

            def load_u(c):
                ui = state["u"] % 2
                state["u"] += 1
                u = U[ui]; bu = b_U[ui]
                ubuf[c] = (u, bu)
                r = (c * CW) // 1024
                co = (c * CW) % 1024
                for q in range(4):
                    src = uT_all[q * 2048 + r * 512:q * 2048 + (r + 1) * 512, :].rearrange("(k p) t -> p k t", p=128)
                    S.dma("sync", lambda e, u=u, q=q, src=src, co=co: e.dma_start(out=u[:, q * 4:(q + 1) * 4, :], in_=src[:, :, co:co + CW]),
                          reads=[b_uTall[q]], writes=[bu])

            items.append(lambda: load_u(0))
            for c in range(NCH):
                if c + 1 < NCH:
                    items.append(lambda c=c: load_u(c + 1))
                cs = slice(c * CW, (c + 1) * CW)
                c8 = (c * CW) // 512
                for which in range(3):
                    pi = 6 + state["pp"] % 2
                    state["pp"] += 1
                    col = (0, 128, 384)[which]
                    for k4 in range(4):
                        def it(c=c, pi=pi, col=col, k4=k4, which=which, cs=cs, c8=c8):
                            u, bu = ubuf[c]
                            p = PSB[pi]; bp = b_PSB[pi]
                            for kk in range(4 * k4, 4 * k4 + 4):
                                S.op("tensor", lambda e, kk=kk: e.matmul(p[:, 0:CW], lhsT=w[:, kk, col:col + 128], rhs=u[:, kk, :],
                                                                          start=(kk == 0), stop=(kk == 15)),
                                     reads=[bw, bu], writes=[bp])
                            if k4 == 3:
                                if which == 0:
                                    S.op("scalar", lambda e: e.activation(out=QTs[s][:, cs], in_=p[:, 0:CW], func=AF.Copy, scale=float(qscale)),
                                         reads=[bp], writes=[b_QTs[s][c8]])
                                elif which == 1:
                                    S.op("vector", lambda e: e.tensor_copy(out=KTs[s][:, cs], in_=p[:, 0:CW]), reads=[bp], writes=[b_KTs[s][c8]])
                                else:
                                    S.op("vector", lambda e: e.tensor_copy(out=SZs[s][:, cs], in_=p[:, 0:CW]), reads=[bp], writes=[b_SZs[s][c8]])
                        items.append(it)
                pi = 6 + state["pp"] % 2
                state["pp"] += 1
                nsub = CW // 128
                for sub in range(nsub):
                    for k4 in range(4):
                        def it(c=c, pi=pi, sub=sub, k4=k4, c8=c8, nsub=nsub):
                            u, bu = ubuf[c]
                            p = PSB[pi]; bp = b_PSB[pi]
                            for kk in range(4 * k4, 4 * k4 + 4):
                                S.op("tensor", lambda e, kk=kk: e.matmul(p[:, sub * 128:(sub + 1) * 128], lhsT=u[:, kk, sub * 128:(sub + 1) * 128],
                                                                          rhs=w[:, kk, 256:384], start=(kk == 0), stop=(kk == 15)),
                                     reads=[bw, bu], writes=[bp])
                            if sub == nsub - 1 and k4 == 3:
                                b0 = (c * CW) // 128
                                S.op("vector", lambda e: e.tensor_copy(out=VVs[s][:, b0:b0 + nsub, :],
                                                                        in_=p[:, 0:CW].rearrange("p (a b) -> p a b", a=nsub)),
                                     reads=[bp], writes=[b_VVs[s][c8]])
                        items.append(it)
            return items

        def silu_sz(j):
            s = j % NSET
            for c in range(8):
                cs = slice(c * 512, (c + 1) * 512)
                S.op("scalar", lambda e, cs=cs: e.activation(out=SZs[s][:, cs], in_=SZs[s][:, cs], func=AF.Silu), reads=[b_SZs[s][c]], writes=[b_SZs[s][c]])

        def proj(j, qscale):
            w = Wh[j % 2]; bw = b_Wh[j % 2]
            for c in range(8):
                ui = state["u"] % 2
                state["u"] += 1
                u = U[ui]; bu = b_U[ui]
                r = c // 2
                co = (c % 2) * 512
                for q in range(4):
                    src = uT_all[q * 2048 + r * 512:q * 2048 + (r + 1) * 512, :].rearrange("(k p) t -> p k t", p=128)
                    S.dma("sync", lambda e, u=u, q=q, src=src, co=co: e.dma_start(out=u[:, q * 4:(q + 1) * 4, :], in_=src[:, :, co:co + 512]),
                          reads=[b_uTall[q]], writes=[bu])
                cs = slice(c * 512, (c + 1) * 512)
                for which in range(3):
                    pi = 6 + state["pp"] % 2
                    state["pp"] += 1
                    p = PSB[pi]; bp = b_PSB[pi]
                    col = (0, 128, 384)[which]
                    for kk in range(16):
                        S.op("tensor", lambda e, p=p, w=w, u=u, kk=kk, col=col: e.matmul(p[:, :], lhsT=w[:, kk, col:col + 128], rhs=u[:, kk, :],
                                                                                          start=(kk == 0), stop=(kk == 15)),
                             reads=[bw, bu], writes=[bp])
                    if which == 0:
                        S.op("scalar", lambda e, p=p, cs=cs: e.activation(out=QT[:, cs], in_=p[:, :], func=AF.Copy, scale=float(qscale)),
                             reads=[bp], writes=[b_QT[c]])
                    elif which == 1:
                        S.op("vector", lambda e, p=p, cs=cs: e.tensor_copy(out=KT[:, cs], in_=p[:, :]), reads=[bp], writes=[b_KT[c]])
                    else:
                        S.op("scalar", lambda e, p=p, cs=cs: e.activation(out=SZ[:, cs], in_=p[:, :], func=AF.Silu), reads=[bp], writes=[b_SZ[c]])
                pi = 6 + state["pp"] % 2
                state["pp"] += 1
                p = PSB[pi]; bp = b_PSB[pi]
                for sub in range(4):
                    for kk in range(16):
                        S.op("tensor", lambda e, p=p, w=w, u=u, kk=kk, sub=sub: e.matmul(p[:, sub * 128:(sub + 1) * 128], lhsT=u[:, kk, sub * 128:(sub + 1) * 128],
                                                                                          rhs=w[:, kk, 256:384], start=(kk == 0), stop=(kk == 15)),
                             reads=[bw, bu], writes=[bp])
                S.op("vector", lambda e, p=p, c=c: e.tensor_copy(out=VV[:, 4 * c:4 * c + 4, :], in_=p[:, :].rearrange("p (a b) -> p a b", a=4)),
                     reads=[bp], writes=[b_VV[c]])

        def store_mz(j, qs, src_fn):
            i = state["mzo"] % 2
            state["mzo"] += 1
            src_fn(MZo[i], b_MZo[i])
            S.dma("sync", lambda e, i=i: e.dma_start(out=mz_loc[j * 128:(j + 1) * 128, qs * 512:(qs + 1) * 512], in_=MZo[i]),
                  reads=[b_MZo[i]], writes=[b_mzloc[j]])

        if not even:
            E = [A.alloc([128, 512], F32) for _ in range(2)]; b_E = [Buf(), Buf()]
            Lb = [A.alloc([128, 512], BF16) for _ in range(3)]; b_Lb = [Buf() for _ in range(3)]
            LA = [A.alloc([128, 512], BF16) for _ in range(3)]; b_LA = [Buf() for _ in range(3)]
            Ab = [A.alloc([128, 512], BF16) for _ in range(3)]; b_Ab = [Buf() for _ in range(3)]

            def attn_odd(j, bg):
                s_ = j % NSET
                QT, KT, SZ, VV = QTs[s_], KTs[s_], SZs[s_], VVs[s_]
                b_QT, b_KT, b_SZ, b_VV = b_QTs[s_], b_KTs[s_], b_SZs[s_], b_VVs[s_]
                units = [(qs, kb) for qs in range(NSB) for kb in range(4 * qs + 3, -1, -1)]
                n = len(units)
                lacc = {}
                nbg = len(bg)
                done = [0]

                def S0(u):
                    qs, kb = units[u]
                    z = PSB[u % 4]; bz = b_PSB[u % 4]
                    diag = kb >= 4 * qs
                    S.op("tensor", lambda e: e.matmul(z[:, :], lhsT=KT[:, kb * 128:(kb + 1) * 128], rhs=QT[:, qs * 512:(qs + 1) * 512], start=True, stop=False),
                         reads=[b_KT[kb // 4], b_QT[qs]], writes=[bz])
                    if diag:
                        x0 = 512 * qs - 128 * kb + 384
                        S.op("tensor", lambda e: e.matmul(z[:, :], lhsT=Jb, rhs=CMb[:, x0:x0 + 512], start=False, stop=False),
                             reads=[b_J, b_CM], writes=[bz])

                def S1a(u):
                    z = PSB[u % 4]; bz = b_PSB[u % 4]
                    ee = E[u % 2]; be = b_E[u % 2]
                    S.op("scalar", lambda e: e.activation(out=ee, in_=z[:, :], func=AF.Exp), reads=[bz], writes=[be])

                def S1b(u):
                    ee = E[u % 2]; be = b_E[u % 2]
                    S.op("scalar", lambda e: e.activation(out=Lb[u % 3], in_=ee, func=AF.Ln, bias=1.0), reads=[be], writes=[b_Lb[u % 3]])

                def S23(u):
                    qs, kb = units[u]
                    z = PSB[u % 4]; bz = b_PSB[u % 4]
                    first = kb == 4 * qs + 3
                    last = kb == 0
                    S.op("tensor", lambda e: e.matmul(z[:, :], lhsT=TRIb, rhs=Lb[u % 3], start=False, stop=first),
                         reads=[b_TRI, b_Lb[u % 3]], writes=[bz])
                    if not first:
                        pa, pb = lacc[u - 1]
                        S.op("tensor", lambda e: e.matmul(z[:, :], lhsT=NEGb, rhs=pa, start=False, stop=True), reads=[b_NEG, pb], writes=[bz])
                    if not last:
                        if first:
                            lacc[u] = (Lb[u % 3], b_Lb[u % 3])
                        else:
                            pa, pb = lacc[u - 1]
                            S.op("vector", lambda e: e.tensor_tensor(out=LA[u % 3], in0=pa, in1=Lb[u % 3], op=ALU.add),
                                 reads=[pb, b_Lb[u % 3]], writes=[b_LA[u % 3]])
                            lacc[u] = (LA[u % 3], b_LA[u % 3])
                    lacc.pop(u - 2, None)

                def S4(u):
                    z = PSB[u % 4]; bz = b_PSB[u % 4]
                    S.op("scalar", lambda e: e.activation(out=Ab[u % 3], in_=z[:, :], func=AF.Exp), reads=[bz], writes=[b_Ab[u % 3]])

                def S5(u):
                    qs, kb = units[u]
                    first = kb == 4 * qs + 3
                    last = kb == 0
                    o = PSB[4 + qs % 2]; bo = b_PSB[4 + qs % 2]
                    S.op("tensor", lambda e: e.matmul(o[:, :], lhsT=VV[:, kb, :], rhs=Ab[u % 3], start=first, stop=last),
                         reads=[b_VV[kb // 4], b_Ab[u % 3]], writes=[bo])
                    if last:
                        def fin(dst, bdst):
                            S.op("vector", lambda e: e.tensor_tensor(out=dst, in0=o[:, :], in1=SZ[:, qs * 512:(qs + 1) * 512], op=ALU.mult),
                                 reads=[bo, b_SZ[qs]], writes=[bdst])
                        store_mz(j, qs, fin)

                d1, d2 = PIPE_ODD
                for i in range(n + d2):
                    tgt = min(nbg, ((i + 1) * nbg + n - 1) // n) if i < n else nbg
                    while done[0] < tgt:
                        bg[done[0]]()
                        done[0] += 1
                    if i < n:
                        S0(i)
                    if 0 <= i - d1 < n:
                        S1a(i - d1)
                    if 0 <= i - d2 < n:
                        S4(i - d2)
                    if 0 <= i - d1 < n:
                        S1b(i - d1)
                        S23(i - d1)
                    if 0 <= i - d2 < n:
                        S5(i - d2)
        else:
            e_idx = l // 2
            G = [A.alloc([128, GW_A], BF16) for _ in range(2)]; b_G = [Buf(), Buf()]
            Pb = [A.alloc([128, 512], BF16) for _ in range(3)]; b_Pb = [Buf() for _ in range(3)]
            Rr = [A.alloc([128, 512], F32) for _ in range(1)]; b_Rr = [Buf()]
            On = [A.alloc([128, 512], F32) for _ in range(2)]; b_On = [Buf() for _ in range(2)]
            Pacc = [[A.alloc([128, 512], BF16) for _ in range(2)] for _ in range(2)]; b_Pacc = [[Buf(), Buf()], [Buf(), Buf()]]
            Qp = [[A.alloc([128, 512], BF16) for _ in range(2)] for _ in range(2)]; b_Qp = [[Buf(), Buf()], [Buf(), Buf()]]
            for m_ in range(2):
                for r_ in range(2):
                    S.op("vector", lambda e, m_=m_, r_=r_: e.memset(Qp[m_][r_], 0.0), writes=[b_Qp[m_][r_]])
            Dd = A.alloc([128, 512], F32); b_Dd = Buf()
            Dq = A.alloc([128, 512], BF16); b_Dq = Buf()
            lam_sb = A.alloc([128, 256], F32); b_lam = Buf()
            lt = A.alloc([128, 2, 64], F32); b_lt = Buf()
            ls = A.alloc([128, 8], F32); b_ls = Buf()
            sgs = A.alloc([128, 2], F32); b_sg = Buf()
            S.dma("sync", lambda e: e.dma_start(out=lam_sb, in_=lam_in[e_idx:e_idx + 1, :].partition_broadcast(128)), writes=[b_lam])
            S.op("vector", lambda e: e.tensor_tensor(out=lt[:, 0, :], in0=lam_sb[:, 0:64], in1=lam_sb[:, 64:128], op=ALU.mult), reads=[b_lam], writes=[b_lt])
            S.op("vector", lambda e: e.tensor_tensor(out=lt[:, 1, :], in0=lam_sb[:, 128:192], in1=lam_sb[:, 192:256], op=ALU.mult), reads=[b_lam], writes=[b_lt])
            S.op("scalar", lambda e: e.activation(out=lam_sb[:, 0:64], in_=lt[:, 0, :], func=AF.Copy, accum_out=ls[:, 0:1]), reads=[b_lt], writes=[b_lam, b_ls])
            S.op("scalar", lambda e: e.activation(out=lam_sb[:, 64:128], in_=lt[:, 1, :], func=AF.Copy, accum_out=ls[:, 1:2]), reads=[b_lt], writes=[b_lam, b_ls])
            S.op("scalar", lambda e: e.activation(out=ls[:, 2:4], in_=ls[:, 0:2], func=AF.Exp), reads=[b_ls], writes=[b_ls])
            S.op("vector", lambda e: e.tensor_tensor(out=ls[:, 4:5], in0=ls[:, 2:3], in1=ls[:, 3:4], op=ALU.subtract), reads=[b_ls], writes=[b_ls])
            S.op("vector", lambda e: e.tensor_scalar(out=ls[:, 5:6], in0=ls[:, 4:5], scalar1=float(lam_init), scalar2=-1.0, op0=ALU.add, op1=ALU.mult),
                 reads=[b_ls], writes=[b_ls])
            neglam = ls[:, 5:6]
            S.dma("sync", lambda e: e.dma_start(out=sgs[:, 0:1], in_=sg_in[e_idx]), writes=[b_sg])
            S.op("vector", lambda e: e.tensor_scalar(out=sgs[:, 1:2], in0=sgs[:, 0:1], scalar1=float(1.0 - lam_init), scalar2=None, op0=ALU.mult),
                 reads=[b_sg], writes=[b_sg])
            gsc = sgs[:, 1:2]
            cnt = {"u": 0, "acc": 0, "on": 0, "rr": 0, "qp": 0}

            def load_G(j):
                gi = j % 2
                if j < 4:
                    src = bass.AP(frA.tensor, frA.offset + j * GL, [[1, 128], [1, GW_A]])
                    S.dma("sync", lambda e: e.dma_start(out=G[gi], in_=src), reads=[b_frA], writes=[b_G[gi]])
                else:
                    src = bass.AP(frB.tensor, frB.offset + (j - 4) * GL, [[1, 128], [1, GW_B]])
                    S.dma("sync", lambda e: e.dma_start(out=G[gi][:, 0:GW_B], in_=src), reads=[b_frB], writes=[b_G[gi]])

            def softmax_sb(qs, qap, bq, Gt, bG, is_A, deferred=None):
                if is_A:
                    kbs = [kb for kb in range(4 * qs + 3, -1, -1) if 512 * qs - 128 * kb <= 2176]
                else:
                    kbs = list(range(4 * qs + 3, -1, -1))
                ai = cnt["acc"] % 2
                cnt["acc"] += 1
                o = PSB[2 + 2 * ai]; bo = b_PSB[2 + 2 * ai]
                lq = PSB[3 + 2 * ai]; bl = b_PSB[3 + 2 * ai]
                pac = Pacc[ai]; bpac = b_Pacc[ai]
                n = len(kbs)
                ids = []
                used = [False, False]

                def S0(i):
                    kb = kbs[i]
                    u = cnt["u"]; cnt["u"] += 1
                    ids.append(u)
                    sp = PSB[u % 2]; bs = b_PSB[u % 2]
                    D = 512 * qs - 128 * kb
                    biased = is_A or D <= 1536
                    S.op("tensor", lambda e: e.matmul(sp[:, :], lhsT=KT[:, kb * 128:(kb + 1) * 128], rhs=qap, start=True, stop=not biased),
                         reads=[b_KT[kb // 4], bq], writes=[bs])
                    if biased:
                        x0 = D + 384
                        S.op("tensor", lambda e: e.matmul(sp[:, :], lhsT=Jb, rhs=Gt[:, x0:x0 + 512], start=False, stop=True),
                             reads=[b_J, bG], writes=[bs])

                def S1(i):
                    u = ids[i]
                    sp = PSB[u % 2]; bs = b_PSB[u % 2]
                    S.op("scalar", lambda e: e.activation(out=Pb[u % 3], in_=sp[:, :], func=AF.Exp), reads=[bs], writes=[b_Pb[u % 3]])

                def S2(i):
                    u = ids[i]
                    kb = kbs[i]
                    S.op("tensor", lambda e: e.matmul(o[:, :], lhsT=VV[:, kb, :], rhs=Pb[u % 3], start=(i == 0), stop=(i == n - 1)),
                         reads=[b_VV[kb // 4], b_Pb[u % 3]], writes=[bo])
                    w_ = i % 2
                    eng = ("vector", "gpsimd")[w_]
                    if not used[w_]:
                        used[w_] = True
                        S.op(eng, lambda e: e.tensor_copy(out=pac[w_], in_=Pb[u % 3]), reads=[b_Pb[u % 3]], writes=[bpac[w_]])
                    else:
                        S.op(eng, lambda e: e.tensor_tensor(out=pac[w_], in0=pac[w_], in1=Pb[u % 3], op=ALU.add),
                             reads=[b_Pb[u % 3], bpac[w_]], writes=[bpac[w_]])

                d1, d2 = PIPE_EVEN
                tot = n + d2
                dpos = min(3, tot - 1)
                for i in range(tot):
                    if i < n:
                        S0(i)
                    if 0 <= i - d1 < n:
                        S1(i - d1)
                    if 0 <= i - d2 < n:
                        S2(i - d2)
                    if i == dpos and deferred is not None:
                        deferred()

                def norm():
                    S.op("tensor", lambda e: e.matmul(lq[:, :], lhsT=ONESb, rhs=pac[0], start=True, stop=not used[1]),
                         reads=[b_ONES, bpac[0]], writes=[bl])
                    if used[1]:
                        S.op("tensor", lambda e: e.matmul(lq[:, :], lhsT=ONESb, rhs=pac[1], start=False, stop=True),
                             reads=[b_ONES, bpac[1]], writes=[bl])
                    oi = cnt["on"] % 2; cnt["on"] += 1
                    S.op("vector", lambda e: e.reciprocal(out=Rr[0], in_=lq[:, :]), reads=[bl], writes=[b_Rr[0]])
                    S.op("vector", lambda e: e.tensor_tensor(out=On[oi], in0=o[:, :], in1=Rr[0], op=ALU.mult), reads=[bo, b_Rr[0]], writes=[b_On[oi]])
                    return On[oi], b_On[oi]
                return norm

            def attn_A(j):
                pending = [None]
                for qs in range(NSB):
                    nrm = softmax_sb(qs, QT[:, qs * 512:(qs + 1) * 512], b_QT[qs], G[j % 2], b_G[j % 2], True, deferred=pending[0])

                    def fin_all(nrm=nrm, qs=qs):
                        on, bon = nrm()

                        def fin(dst, bdst):
                            S.op("vector", lambda e: e.tensor_tensor(out=dst, in0=on, in1=SZ[:, qs * 512:(qs + 1) * 512], op=ALU.mult),
                                 reads=[bon, b_SZ[qs]], writes=[bdst])
                        store_mz(j, qs, fin)
                    pending[0] = fin_all
                pending[0]()

            def attn_B(j):
                pending = [None]
                for qs in range(NSB):
                    r_ = cnt["qp"] % 2; cnt["qp"] += 1
                    S.op("gpsimd", lambda e, r_=r_, qs=qs: e.tensor_copy(out=Qp[0][r_][0:64, :], in_=QT[0:64, qs * 512:(qs + 1) * 512]),
                         reads=[b_QT[qs]], writes=[b_Qp[0][r_]])
                    S.op("gpsimd", lambda e, r_=r_, qs=qs: e.tensor_copy(out=Qp[1][r_][64:128, :], in_=QT[64:128, qs * 512:(qs + 1) * 512]),
                         reads=[b_QT[qs]], writes=[b_Qp[1][r_]])
                    n1 = softmax_sb(qs, Qp[0][r_], b_Qp[0][r_], G[j % 2], b_G[j % 2], False, deferred=pending[0])
                    res1 = {}

                    def d1(n1=n1, res1=res1):
                        res1["o"] = n1()
                    n2 = softmax_sb(qs, Qp[1][r_], b_Qp[1][r_], G[j % 2], b_G[j % 2], False, deferred=d1)

                    def fin_all(n2=n2, res1=res1, qs=qs):
                        o1, bo1 = res1["o"]
                        o2, bo2 = n2()
                        S.op("vector", lambda e: e.scalar_tensor_tensor(out=Dd, in0=o2, scalar=neglam, in1=o1, op0=ALU.mult, op1=ALU.add),
                             reads=[bo1, bo2, b_ls], writes=[b_Dd])
                        S.op("vector", lambda e: e.tensor_tensor(out=Dq, in0=Dd, in1=Dd, op=ALU.mult), reads=[b_Dd], writes=[b_Dq])
                        S.op("tensor", lambda e: e.matmul(PSB[6][:, :], lhsT=ONESb, rhs=Dq, start=True, stop=True), reads=[b_ONES, b_Dq], writes=[b_PSB[6]])
                        S.op("scalar", lambda e: e.activation(out=Rr[0], in_=PSB[6][:, :], func=AF.Ln, scale=1.0 / 128, bias=EPS), reads=[b_PSB[6]], writes=[b_Rr[0]])
                        S.op("scalar", lambda e: e.activation(out=Rr[0], in_=Rr[0], func=AF.Exp, scale=-0.5), reads=[b_Rr[0]], writes=[b_Rr[0]])
                        S.op("vector", lambda e: e.tensor_tensor(out=Dd, in0=Dd, in1=Rr[0], op=ALU.mult), reads=[b_Dd, b_Rr[0]], writes=[b_Dd])

                        def fin(dst, bdst):
                            S.op("vector", lambda e: e.scalar_tensor_tensor(out=dst, in0=Dd, scalar=gsc, in1=SZ[:, qs * 512:(qs + 1) * 512], op0=ALU.mult, op1=ALU.mult),
                                 reads=[b_Dd, b_sg, b_SZ[qs]], writes=[bdst])
                        store_mz(j, qs, fin)
                    pending[0] = fin_all
                pending[0]()

        load_w(0)
        if not even:
            load_w(1)
            for itf in proj_items(0, 128 ** -0.5):
                itf()
        for j in range(8):
            if even and j + 1 < 8:
                load_w(j + 1)
            if not even:
                if j + 2 < 8:
                    load_w(j + 2)
                silu_sz(j)
                bg = proj_items(j + 1, 128 ** -0.5) if j + 1 < 8 else []
                attn_odd(j, bg)
            elif j < 4:
                load_G(j)
                proj(j, 128 ** -0.5)
                attn_A(j)
            else:
                load_G(j)
                proj(j, 64 ** -0.5)
                attn_B(j)
            S.cc(lambda e, j=j: e.collective_compute("AllGather", ALU.bypass, replica_groups=GROUPS4, ins=[mz_loc[j * 128:(j + 1) * 128, :]],
                                                     outs=[mz_all[j * 512:(j + 1) * 512, :]]),
                 reads=[b_mzloc[j]], writes=[b_mzall])

    phases = [phase_M, phase_G, lambda: phase_L31(None, 0, False)]
    for l in range(DEPTH):
        last = l == DEPTH - 1
        phases.append(lambda l=l: phase_L2(l))
        phases.append(lambda l=l, last=last: phase_L31(l, None if last else l + 1, last))
    for i, ph in enumerate(phases):
        if nphase is not None and i >= nphase:
            break
        ph()
        S.barrier(); A.reset()
    if nphase is not None and nphase < len(phases):
        yv = y_out.rearrange("(t p) f -> p t f", p=128)
        for t in range(8):
            S.dma("sync", lambda e, t=t: e.dma_start(out=yv[:, t, :], in_=H[:, t, :]), reads=[b_H[t]], final=True)
    if dbg:
        d1 = C.dout("dbg_uT", [4 * 2048, 1024], BF16)
        d2 = C.dout("dbg_mz", [4 * 1024, 4096], BF16)
        d3 = C.dout("dbg_mod", [4, 6144], F32)
        for q in range(16 if (nphase is None or nphase >= 3) else 0):
            S.dma("sync", lambda e, q=q: e.dma_start(out=d1[q * 512:(q + 1) * 512, :], in_=uT_all[q * 512:(q + 1) * 512, :]), reads=b_uTall, final=True)
        if nphase is None or nphase >= 4:
            for q in range(32):
                S.dma("sync", lambda e, q=q: e.dma_start(out=d2[q * 128:(q + 1) * 128, :], in_=mz_all[q * 128:(q + 1) * 128, :]), reads=[b_mzall], final=True)
        S.dma("sync", lambda e: e.dma_start(out=d3[:, :], in_=mod_all[:, :]), reads=[b_modall], final=True)
    return C.finish()


_PROGS = {}


def kernel(x, c, norm_g, w_mod, b_mod, w_in, w_out, rel_bias, diff_lambda, diff_subln_g, final_norm_g):
    f32 = lambda a: np.ascontiguousarray(np.asarray(a, np.float32))
    x = f32(x); c = f32(c); norm_g = f32(norm_g); w_mod = f32(w_mod); b_mod = f32(b_mod); w_in = f32(w_in); w_out = f32(w_out)
    rel_bias = f32(rel_bias); diff_lambda = f32(diff_lambda); diff_subln_g = f32(diff_subln_g); final_norm_g = f32(final_norm_g)
    if "fused" not in _PROGS:
        _PROGS["fused"] = build_fused()
    nc = _PROGS["fused"]
    ims = make_inputs(x, c, norm_g, w_mod, b_mod, w_in, w_out, rel_bias, diff_lambda, diff_subln_g, final_norm_g)
    res = run_bass_kernel_spmd(nc, ims, core_ids=list(range(8))).results
    out = np.concatenate([np.asarray(res[i]["y"]) for i in range(8)], axis=0)
    return out.reshape(BATCH, SEQ, D_MODEL).astype(np.float32)


def make_inputs(x, c, norm_g, w_mod, b_mod, w_in, w_out, rel_bias, diff_lambda, diff_subln_g, final_norm_g):
    hc = host_consts()
    xf = x.reshape(8192, 2048)
    ident = np.eye(128, dtype=np.float32)
    ims = []
    for i in range(8):
        b, g = i // 4, i % 4
        lm = i % 4
        wcs = np.stack([make_wc(w_in[l], g, l % 2 == 0) for l in range(DEPTH)])
        cols = [4 * g + j for j in range(4)] + [16 + 4 * g + j for j in range(4)]
        ims.append({
            "x": np.ascontiguousarray(xf[i * 1024:(i + 1) * 1024]),
            "cT": np.ascontiguousarray(c[b].reshape(16, 128).T),
            "wm": w_mod[lm], "bm": np.ascontiguousarray(b_mod[lm].reshape(1, 6144)),
            "wc": wcs, "wout": w_out, "gn": norm_g, "gf": final_norm_g.reshape(1, 2048),
            "ident": ident, "J": hc["J"], "TRI": hc["TRI"], "CM": hc["CM"],
            "OH": hc["OH"], "OHB": hc["OHB"], "LMA": hc["LMA"], "NGB": hc["NGB"],
            "RB": np.ascontiguousarray(rel_bias[:, cols]),
            "lamv": np.ascontiguousarray(diff_lambda.reshape(2, 256)),
            "sg": np.ascontiguousarray(diff_subln_g.reshape(2, 128, 1)),
        })
    return ims
```

```python
import math
from contextlib import ExitStack

import numpy as np
import ml_dtypes
import concourse.bass as bass
import concourse.mybir as mybir
from concourse.bass_utils import run_bass_kernel_spmd

F32 = mybir.dt.float32
BF16 = mybir.dt.bfloat16
AF = mybir.ActivationFunctionType
ALU = mybir.AluOpType

D_MODEL = 2048
SEQ = 4096
BATCH = 2
DEPTH = 4
D_INNER = 4096
EPS = 1e-6
NEG = -30000.0

ENGS = ("sync", "scalar", "vector", "gpsimd", "tensor")
DMA_K = 6


class Buf:
    __slots__ = ("name", "last_w", "readers")

    def __init__(self, name=""):
        self.name = name
        self.last_w = None
        self.readers = {}


class Sched:
    def __init__(self, nc, stack):
        self.nc = nc
        self.streams = {e: [] for e in ENGS}
        self.count = {e: 0 for e in ENGS}
        self.seen = {e: {} for e in ENGS}
        self.sems = {}
        for e in ENGS:
            self.sems[e] = stack.enter_context(nc.semaphore("s_" + e))
        self.dma_n = {e: 0 for e in ("sync", "scalar", "gpsimd")}
        for e in ("sync", "scalar", "gpsimd"):
            for k in range(DMA_K):
                key = ("d", e, k)
                self.sems[key] = stack.enter_context(nc.semaphore(f"d_{e}_{k}"))
                self.count[key] = 0
        self.final_events = []
        self.sems["cc"] = stack.enter_context(nc.semaphore("s_cc"))
        self.count["cc"] = 0
        self.pending = {}
        self.rt = {}

    def barrier(self):
        snap = {k: v for k, v in self.count.items() if v > 0}
        for e in ENGS:
            p = self.pending.setdefault(e, {})
            for k, v in snap.items():
                if p.get(k, 0) < v:
                    p[k] = v

    def cc(self, fn, reads=(), writes=()):
        deps = self._deps(reads, writes)
        prev = self.count["cc"]
        if prev > 0 and deps.get("cc", 0) < prev:
            deps["cc"] = prev
        self.count["cc"] += 1
        ev = ("cc", self.count["cc"])
        self._finish("gpsimd", deps, fn, ev, None, reads, writes)
        return ev

    def _deps(self, reads, writes):
        deps = {}
        for b in reads:
            ev = b.last_w
            if ev is not None and deps.get(ev[0], 0) < ev[1]:
                deps[ev[0]] = ev[1]
        for b in writes:
            ev = b.last_w
            if ev is not None and deps.get(ev[0], 0) < ev[1]:
                deps[ev[0]] = ev[1]
            for k, v in b.readers.items():
                if deps.get(k, 0) < v:
                    deps[k] = v
        return deps

    def _finish(self, eng, deps, fn, ev, inc, reads, writes):
        pend = self.pending.pop(eng, None)
        if pend:
            for k, v in pend.items():
                if deps.get(k, 0) < v:
                    deps[k] = v
        waits = []
        seen = self.seen[eng]
        for k, v in deps.items():
            if k == "tensor" and eng == "tensor":
                continue
            if seen.get(k, 0) >= v:
                continue
            seen[k] = v
            waits.append((k, v))
        self.streams[eng].append((waits, fn, (ev[0], inc)))
        for b in reads:
            if b.readers.get(ev[0], 0) < ev[1]:
                b.readers[ev[0]] = ev[1]
        for b in writes:
            b.last_w = ev
            b.readers = {}

    def op(self, eng, fn, reads=(), writes=()):
        deps = self._deps(reads, writes)
        self.count[eng] += 1
        ev = (eng, self.count[eng])
        self._finish(eng, deps, fn, ev, 1, reads, writes)
        return ev

    def dma(self, eng, fn, reads=(), writes=(), final=False):
        deps = self._deps(reads, writes)
        n = self.dma_n[eng]
        self.dma_n[eng] += 1
        key = ("d", eng, n % DMA_K)
        prev = self.count[key]
        if prev > 0 and deps.get(key, 0) < prev:
            deps[key] = prev
        self.count[key] += 16
        ev = (key, self.count[key])
        self._finish(eng, deps, fn, ev, 16, reads, writes)
        if final:
            self.final_events.append(ev)
        return ev

    def emit(self, final_eng="sync"):
        nc = self.nc
        fw = {}
        for k, v in self.final_events:
            fw[k] = max(fw.get(k, 0), v)
        streams = self.streams
        sems = self.sems
        with nc.Block() as block:
            def mk(ename):
                def body(eng):
                    if ename == "sync":
                        self.rt["pid"] = nc.partition_id([eng.engine])
                    for waits, fn, (sk, inc) in streams[ename]:
                        for k, v in waits:
                            eng.wait_ge(sems[k], v)
                        inst = fn(eng)
                        if inc is None:
                            inst.then_inc(sems[sk])
                        else:
                            inst.then_inc(sems[sk], inc)
                    if ename == final_eng:
                        for k, v in fw.items():
                            eng.wait_ge(sems[k], v)
                return body
            block.sync(mk("sync"))
            block.scalar(mk("scalar"))
            block.vector(mk("vector"))
            block.gpsimd(mk("gpsimd"))
            block.tensor(mk("tensor"))


class Ctx:
    def __init__(self):
        self.nc = bass.Bass("TRN2", target_bir_lowering=False)
        self.stack = ExitStack()
        self.S = Sched(self.nc, self.stack)
        self.n = 0

    def sb(self, shape, dt, name=None):
        self.n += 1
        return self.stack.enter_context(self.nc.sbuf_tensor(name or f"sb{self.n}", list(shape), dt))

    def ps(self, shape, dt, name=None):
        self.n += 1
        return self.stack.enter_context(self.nc.psum_tensor(name or f"ps{self.n}", list(shape), dt))

    def din(self, name, shape, dt):
        return self.nc.dram_tensor(name, list(shape), dt, kind="ExternalInput").ap()

    def dout(self, name, shape, dt):
        return self.nc.dram_tensor(name, list(shape), dt, kind="ExternalOutput").ap()

    def dscr(self, name, shape, dt):
        return self.nc.dram_tensor(name, list(shape), dt, kind="Internal").ap()

    def finish(self):
        self.S.emit()
        self.stack.close()
        return self.nc


def build_M():
    C = Ctx()
    nc, S = C.nc, C.S
    cT = C.din("cT", [128, 16], F32)
    wm = C.din("wm", [2048, 6144], F32)
    bm = C.din("bm", [1, 6144], F32)
    out = C.dout("mod", [128, 6144], F32)

    c_sb = C.sb([128, 16], F32); b_c = Buf()
    e_sb = C.sb([128, 16], F32); b_e = Buf()
    ca = C.sb([128, 16], F32); b_ca = Buf()
    ones = C.sb([128, 128], F32); b_ones = Buf()
    L = C.sb([128, 16, 128], F32); b_L = Buf()
    bmr = C.sb([1, 6144], F32); b_bmr = Buf()
    W = [C.sb([128, 16, 512], F32) for _ in range(2)]; b_W = [Buf(), Buf()]
    res = C.sb([128, 6144], F32); b_res = Buf()
    P = [C.ps([128, 512], F32) for _ in range(2)]; b_P = [Buf(), Buf()]

    S.dma("sync", lambda e: e.dma_start(out=c_sb[:], in_=cT[:, :]), writes=[b_c])
    S.dma("sync", lambda e: e.dma_start(out=bmr[:], in_=bm[:, :]), writes=[b_bmr])
    S.op("vector", lambda e: e.memset(ones[:], 1.0), writes=[b_ones])
    S.op("scalar", lambda e: e.activation(out=e_sb[:], in_=c_sb[:], func=AF.Exp, scale=-1.0), reads=[b_c], writes=[b_e])
    S.op("vector", lambda e: e.tensor_scalar(out=e_sb[:], in0=e_sb[:], scalar1=1.0, scalar2=None, op0=ALU.add), reads=[b_e], writes=[b_e])
    S.op("vector", lambda e: e.reciprocal(out=e_sb[:], in_=e_sb[:]), reads=[b_e], writes=[b_e])
    S.op("vector", lambda e: e.tensor_tensor(out=ca[:], in0=c_sb[:], in1=e_sb[:], op=ALU.mult), reads=[b_c, b_e], writes=[b_ca])
    for k in range(16):
        S.op("vector", lambda e, k=k: e.tensor_scalar(out=L[:, k, :], in0=ones[:], scalar1=ca[:, k:k + 1], scalar2=None, op0=ALU.mult),
             reads=[b_ones, b_ca], writes=[b_L])
    wv = wm.rearrange("(k p) c -> p k c", p=128)
    for n in range(12):
        w = W[n % 2]; bw = b_W[n % 2]; p = P[n % 2]; bp = b_P[n % 2]
        S.dma("sync", lambda e, w=w, n=n: e.dma_start(out=w[:], in_=wv[:, :, n * 512:(n + 1) * 512]), writes=[bw])
        for k in range(16):
            S.op("tensor", lambda e, w=w, p=p, k=k: e.matmul(p[:, :], lhsT=L[:, k, :], rhs=w[:, k, :], start=(k == 0), stop=False),
                 reads=[b_L, bw], writes=[bp])
        S.op("tensor", lambda e, p=p, n=n: e.matmul(p[:, :], lhsT=ones[0:1, :], rhs=bmr[0:1, n * 512:(n + 1) * 512], start=False, stop=True),
             reads=[b_ones, b_bmr], writes=[bp])
        S.op("scalar", lambda e, p=p, n=n: e.activation(out=res[:, n * 512:(n + 1) * 512], in_=p[:, :], func=AF.Copy), reads=[bp], writes=[b_res])
    S.dma("sync", lambda e: e.dma_start(out=out[:, :], in_=res[:]), reads=[b_res], final=True)
    return C.finish()


def build_L31(do_outproj, do_norm, do_final):
    C = Ctx()
    nc, S = C.nc, C.S
    h_in = C.din("h_in", [1024, 2048], F32)
    if do_outproj:
        mzT = C.din("mzT", [4096, 1024], BF16)
        wout = C.din("wout", [4096, 2048], F32)
        modp = C.din("modp", [128, 6144], F32)
    if do_norm:
        modn = C.din("modn", [128, 6144], F32)
        gn = C.din("gn", [128, 2048], F32)
        uT_out = C.dout("uT", [2048, 1024], BF16)
        h_out = C.dout("h_out", [1024, 2048], F32)
    if do_final:
        gf = C.din("gf", [128, 2048], F32)
        y_out = C.dout("y", [1024, 2048], F32)

    H = C.sb([128, 8, 2048], F32, "H"); b_H = [Buf() for _ in range(8)]
    hv = h_in.rearrange("(t p) f -> p t f", p=128)
    for t in range(8):
        S.dma("sync", lambda e, t=t: e.dma_start(out=H[:, t, :], in_=hv[:, t, :]), writes=[b_H[t]])

    PS = [C.ps([128, 512], F32) for _ in range(4)]; b_PS = [Buf() for _ in range(4)]

    if do_outproj:
        MZ = C.sb([128, 32, 1024], BF16, "MZ"); b_MZ = Buf()
        mzv = mzT.rearrange("(k p) t -> p k t", p=128)
        for q in range(4):
            S.dma("sync", lambda e, q=q: e.dma_start(out=MZ[:, q * 8:(q + 1) * 8, :], in_=mzv[:, q * 8:(q + 1) * 8, :]), writes=[b_MZ])
        gate = C.sb([128, 2048], F32, "gate"); b_gate = Buf()
        S.dma("sync", lambda e: e.dma_start(out=gate[:], in_=modp[:, 4096:6144]), writes=[b_gate])
        W = [C.sb([128, 32, 256], BF16) for _ in range(2)]; b_W = [Buf(), Buf()]
        tmp = [C.sb([128, 256], F32) for _ in range(2)]; b_tmp = [Buf(), Buf()]
        wv = wout.rearrange("(k p) c -> p k c", p=128)
        it = 0
        for f in range(8):
            w = W[f % 2]; bw = b_W[f % 2]
            for q in range(2):
                S.dma("gpsimd", lambda e, w=w, f=f, q=q: e.dma_start(out=w[:, q * 16:(q + 1) * 16, :], in_=wv[:, q * 16:(q + 1) * 16, f * 256:(f + 1) * 256]),
                      writes=[bw])
            for t in range(8):
                p = PS[it % 4]; bp = b_PS[it % 4]; tm = tmp[it % 2]; btm = b_tmp[it % 2]
                it += 1
                for k in range(32):
                    S.op("tensor", lambda e, p=p, w=w, k=k, t=t: e.matmul(p[:, 0:256], lhsT=MZ[:, k, t * 128:(t + 1) * 128], rhs=w[:, k, :],
                                                                            start=(k == 0), stop=(k == 31)),
                         reads=[b_MZ, bw], writes=[bp])
                S.op("vector", lambda e, p=p, tm=tm, f=f: e.tensor_tensor(out=tm[:], in0=p[:, 0:256], in1=gate[:, f * 256:(f + 1) * 256], op=ALU.mult),
                     reads=[bp, b_gate], writes=[btm])
                S.op("gpsimd", lambda e, tm=tm, t=t, f=f: e.tensor_tensor(out=H[:, t, f * 256:(f + 1) * 256], in0=H[:, t, f * 256:(f + 1) * 256], in1=tm[:], op=ALU.add),
                     reads=[btm, b_H[t]], writes=[b_H[t]])

    if do_norm or do_final:
        if do_outproj:
            sq = W[0][:, 16:24, :].rearrange("p a b -> p (a b)"); b_sq = b_W[0]
        else:
            sq = C.sb([128, 2048], BF16, "sq")[:]; b_sq = Buf()
        st = C.sb([128, 8, 4], F32, "st"); b_st = [Buf() for _ in range(8)]
        if do_norm:
            shift = C.sb([128, 2048], F32, "shift"); b_shift = Buf()
            gs = C.sb([128, 2048], F32, "gs"); b_gs = Buf()
            g_sb = C.sb([128, 2048], F32, "g_sb"); b_g = Buf()
            S.dma("sync", lambda e: e.dma_start(out=shift[:], in_=modn[:, 0:2048]), writes=[b_shift])
            S.dma("sync", lambda e: e.dma_start(out=gs[:], in_=modn[:, 2048:4096]), writes=[b_gs])
            S.dma("sync", lambda e: e.dma_start(out=g_sb[:], in_=gn[:, :]), writes=[b_g])
            S.op("vector", lambda e: e.scalar_tensor_tensor(out=gs[:], in0=gs[:], scalar=1.0, in1=g_sb[:], op0=ALU.add, op1=ALU.mult),
                 reads=[b_gs, b_g], writes=[b_gs])
            ident = C.sb([128, 128], BF16, "ident_sb"); b_id = Buf()
            idf = C.sb([128, 128], F32, "idf"); b_idf = Buf()
            idin = C.din("ident", [128, 128], F32)
            S.dma("sync", lambda e: e.dma_start(out=idf[:], in_=idin[:, :]), writes=[b_idf])
            S.op("vector", lambda e: e.tensor_copy(out=ident[:], in_=idf[:]), reads=[b_idf], writes=[b_id])
            if do_outproj:
                ub = [W[1][:, 0:8, :].rearrange("p a b -> p (a b)"), W[1][:, 8:16, :].rearrange("p a b -> p (a b)")]
                b_ub = [b_W[1], b_W[1]]
                uf = gate[:]; b_uf = b_gate
                UT = MZ[:, 0:16, :]; b_UT = [b_MZ]
            else:
                ub = [C.sb([128, 2048], BF16)[:] for _ in range(2)]; b_ub = [Buf(), Buf()]
                uf = C.sb([128, 2048], F32, "uf")[:]; b_uf = Buf()
                UT = C.sb([128, 16, 1024], BF16, "UT")[:]; b_UT = [Buf()]
            PT = [C.ps([128, 512], BF16) for _ in range(2)]; b_PT = [Buf(), Buf()]
            hov = h_out.rearrange("(t p) f -> p t f", p=128)
        else:
            g_sb = C.sb([128, 2048], F32, "g_sb"); b_g = Buf()
            S.dma("sync", lambda e: e.dma_start(out=g_sb[:], in_=gf[:, :]), writes=[b_g])
            yo = [C.sb([128, 2048], F32) for _ in range(2)]; b_yo = [Buf(), Buf()]
            yv = y_out.rearrange("(t p) f -> p t f", p=128)
        nt = 0
        for t in range(8):
            if do_norm:
                S.dma("sync", lambda e, t=t: e.dma_start(out=hov[:, t, :], in_=H[:, t, :]), reads=[b_H[t]], final=True)
            S.op("scalar", lambda e, t=t: e.activation(out=sq, in_=H[:, t, :], func=AF.Square, accum_out=st[:, t, 0:1]),
                 reads=[b_H[t]], writes=[b_sq, b_st[t]])
            S.op("scalar", lambda e, t=t: e.activation(out=st[:, t, 1:2], in_=st[:, t, 0:1], func=AF.Sqrt, scale=1.0 / 2048, bias=EPS),
                 reads=[b_st[t]], writes=[b_st[t]])
            S.op("vector", lambda e, t=t: e.reciprocal(out=st[:, t, 2:3], in_=st[:, t, 1:2]), reads=[b_st[t]], writes=[b_st[t]])
            if do_norm:
                u = ub[t % 2]; bu = b_ub[t % 2]
                S.op("vector", lambda e, t=t: e.scalar_tensor_tensor(out=uf, in0=H[:, t, :], scalar=st[:, t, 2:3], in1=gs[:], op0=ALU.mult, op1=ALU.mult),
                     reads=[b_H[t], b_st[t], b_gs], writes=[b_uf])
                S.op("gpsimd", lambda e, u=u: e.tensor_tensor(out=u, in0=uf, in1=shift[:], op=ALU.add), reads=[b_uf, b_shift], writes=[bu])
                for kq in range(4):
                    pt = PT[nt % 2]; bpt = b_PT[nt % 2]
                    nt += 1
                    for j in range(4):
                        k = kq * 4 + j
                        S.op("tensor", lambda e, pt=pt, u=u, k=k, j=j: e.transpose(out=pt[:, j * 128:(j + 1) * 128], in_=u[:, k * 128:(k + 1) * 128], identity=ident[:]),
                             reads=[bu, b_id], writes=[bpt])
                    S.op("scalar", lambda e, pt=pt, kq=kq, t=t: e.activation(out=UT[:, kq * 4:(kq + 1) * 4, t * 128:(t + 1) * 128],
                                                                              in_=pt[:, :].rearrange("p (a b) -> p a b", a=4), func=AF.Copy),
                         reads=[bpt], writes=[b_UT[0]])
            else:
                y = yo[t % 2]; by = b_yo[t % 2]
                S.op("vector", lambda e, t=t, y=y: e.scalar_tensor_tensor(out=y[:], in0=H[:, t, :], scalar=st[:, t, 2:3], in1=g_sb[:], op0=ALU.mult, op1=ALU.mult),
                     reads=[b_H[t], b_st[t], b_g], writes=[by])
                S.dma("sync", lambda e, t=t, y=y: e.dma_start(out=yv[:, t, :], in_=y[:]), reads=[by], final=True)
        if do_norm:
            uv = uT_out.rearrange("(k p) t -> p k t", p=128)
            for q in range(4):
                S.dma("sync", lambda e, q=q: e.dma_start(out=uv[:, q * 4:(q + 1) * 4, :], in_=UT[:, q * 4:(q + 1) * 4, :]), reads=b_UT, final=True)
    return C.finish()


GL = 3584
GW_A = 3072
GW_B = 2432
NSB = 8
PIPE_ODD = (1, 3)
PIPE_EVEN = (1, 2)


def build_L2(even, lam_init=0.0):
    C = Ctx()
    nc, S = C.nc, C.S
    uT = C.din("uT", [2048, 4096], BF16)
    wc = C.din("wc", [2048, 4096], F32)
    mz_out = C.dout("mz", [1024, 4096], BF16)
    jin = C.din("J", [128, 128], F32)

    PSB = [C.ps([128, 512], F32) for _ in range(8)]
    b_PSB = [Buf() for _ in range(8)]

    cst = C.sb([128, 128], F32, "cst"); b_cst = Buf()
    Jb = C.sb([128, 128], BF16, "Jb"); b_J = Buf()
    S.dma("sync", lambda e: e.dma_start(out=cst[:], in_=jin[:, :]), writes=[b_cst])
    S.op("vector", lambda e: e.tensor_copy(out=Jb[:], in_=cst[:]), reads=[b_cst], writes=[b_J])

    if not even:
        tri_in = C.din("TRI", [128, 128], F32)
        cm_in = C.din("CM", [128, 896], F32)
        TRIb = C.sb([128, 128], BF16, "TRIb"); b_TRI = Buf()
        NEGb = C.sb([128, 128], BF16, "NEGb"); b_NEG = Buf()
        CMb = C.sb([128, 896], BF16, "CMb"); b_CM = Buf()
        cmf = C.sb([128, 896], F32, "cmf"); b_cmf = Buf()
        S.dma("sync", lambda e: e.dma_start(out=cst[:], in_=tri_in[:, :]), writes=[b_cst])
        S.op("vector", lambda e: e.tensor_copy(out=TRIb[:], in_=cst[:]), reads=[b_cst], writes=[b_TRI])
        S.op("vector", lambda e: e.memset(NEGb[:], -1.0), writes=[b_NEG])
        S.dma("sync", lambda e: e.dma_start(out=cmf[:], in_=cm_in[:, :]), writes=[b_cmf])
        S.op("vector", lambda e: e.tensor_copy(out=CMb[:], in_=cmf[:]), reads=[b_cmf], writes=[b_CM])
    else:
        oh_in = C.din("OH", [32, GL], F32)
        ohb_in = C.din("OHB", [32, GL], F32)
        lma_in = C.din("LMA", [1, GL], F32)
        ngb_in = C.din("NGB", [1, GL], F32)
        rb_in = C.din("RB", [32, 8], F32)
        lam_in = C.din("lamv", [1, 256], F32)
        sg_in = C.din("sg", [128, 1], F32)
        frA = C.dscr("frA", [4, GL], BF16); b_frA = Buf()
        frB = C.dscr("frB", [4, GL], BF16); b_frB = Buf()
        ONESb = C.sb([128, 128], BF16, "ONESb"); b_ONES = Buf()
        S.op("vector", lambda e: e.memset(ONESb[:], 1.0), writes=[b_ONES])
        ones4 = C.sb([1, 4], F32, "ones4"); b_ones4 = Buf()
        S.op("vector", lambda e: e.memset(ones4[:], 1.0), writes=[b_ones4])
        RBs = C.sb([32, 8], F32, "RBs"); b_RB = Buf()
        S.dma("sync", lambda e: e.dma_start(out=RBs[:], in_=rb_in[:, :]), writes=[b_RB])
        FRA = C.sb([4, GL], BF16, "FRA"); b_FRA = Buf()
        FRB = C.sb([4, GL], BF16, "FRB"); b_FRB = Buf()
        ohc = [C.sb([32, 512], F32) for _ in range(2)]; b_ohc = [Buf(), Buf()]
        ohbc = [C.sb([32, 512], F32) for _ in range(2)]; b_ohbc = [Buf(), Buf()]
        lmc = [C.sb([1, 512], F32) for _ in range(2)]; b_lmc = [Buf(), Buf()]
        ngc = [C.sb([1, 512], F32) for _ in range(2)]; b_ngc = [Buf(), Buf()]
        for ch in range(GL // 512):
            i = ch % 2
            cs = slice(ch * 512, (ch + 1) * 512)
            S.dma("sync", lambda e, i=i, cs=cs: e.dma_start(out=ohc[i][:], in_=oh_in[:, cs]), writes=[b_ohc[i]])
            S.dma("sync", lambda e, i=i, cs=cs: e.dma_start(out=ohbc[i][:], in_=ohb_in[:, cs]), writes=[b_ohbc[i]])
            S.dma("sync", lambda e, i=i, cs=cs: e.dma_start(out=lmc[i][:], in_=lma_in[:, cs]), writes=[b_lmc[i]])
            S.dma("sync", lambda e, i=i, cs=cs: e.dma_start(out=ngc[i][:], in_=ngb_in[:, cs]), writes=[b_ngc[i]])
            S.op("tensor", lambda e, i=i: e.matmul(PSB[6][0:4, :], lhsT=RBs[:, 0:4], rhs=ohc[i][:], start=True, stop=False),
                 reads=[b_RB, b_ohc[i]], writes=[b_PSB[6]])
            S.op("tensor", lambda e, i=i: e.matmul(PSB[6][0:4, :], lhsT=ones4[0:1, 0:4], rhs=lmc[i][:], start=False, stop=True),
                 reads=[b_ones4, b_lmc[i]], writes=[b_PSB[6]])
            S.op("scalar", lambda e, cs=cs: e.activation(out=FRA[:, cs], in_=PSB[6][0:4, :], func=AF.Copy), reads=[b_PSB[6]], writes=[b_FRA])
            S.op("tensor", lambda e, i=i: e.matmul(PSB[7][0:4, :], lhsT=RBs[:, 4:8], rhs=ohbc[i][:], start=True, stop=False),
                 reads=[b_RB, b_ohbc[i]], writes=[b_PSB[7]])
            S.op("tensor", lambda e, i=i: e.matmul(PSB[7][0:4, :], lhsT=ones4[0:1, 0:4], rhs=ngc[i][:], start=False, stop=True),
                 reads=[b_ones4, b_ngc[i]], writes=[b_PSB[7]])
            S.op("scalar", lambda e, cs=cs: e.activation(out=FRB[:, cs], in_=PSB[7][0:4, :], func=AF.Copy), reads=[b_PSB[7]], writes=[b_FRB])
        S.dma("sync", lambda e: e.dma_start(out=frA[:, :], in_=FRA[:]), reads=[b_FRA], writes=[b_frA])
        S.dma("sync", lambda e: e.dma_start(out=frB[:, :], in_=FRB[:]), reads=[b_FRB], writes=[b_frB])
        GA = [C.sb([128, GW_A], BF16) for _ in range(4)]; b_GA = [Buf() for _ in range(4)]
        GB = [C.sb([128, GW_B], BF16) for _ in range(4)]; b_GB = [Buf() for _ in range(4)]
        for j in range(4):
            srcA = bass.AP(frA.tensor, frA.offset + j * GL, [[1, 128], [1, GW_A]])
            srcB = bass.AP(frB.tensor, frB.offset + j * GL, [[1, 128], [1, GW_B]])
            S.dma("sync", lambda e, j=j, srcA=srcA: e.dma_start(out=GA[j][:], in_=srcA), reads=[b_frA], writes=[b_GA[j]])
            S.dma("sync", lambda e, j=j, srcB=srcB: e.dma_start(out=GB[j][:], in_=srcB), reads=[b_frB], writes=[b_GB[j]])
        lam_sb = C.sb([128, 256], F32, "lam_sb"); b_lam = Buf()
        S.dma("sync", lambda e: e.dma_start(out=lam_sb[:], in_=lam_in[0:1, :].partition_broadcast(128)), writes=[b_lam])
        lt = C.sb([128, 2, 64], F32, "lt"); b_lt = Buf()
        ls = C.sb([128, 8], F32, "ls"); b_ls = Buf()
        S.op("vector", lambda e: e.tensor_tensor(out=lt[:, 0, :], in0=lam_sb[:, 0:64], in1=lam_sb[:, 64:128], op=ALU.mult), reads=[b_lam], writes=[b_lt])
        S.op("vector", lambda e: e.tensor_tensor(out=lt[:, 1, :], in0=lam_sb[:, 128:192], in1=lam_sb[:, 192:256], op=ALU.mult), reads=[b_lam], writes=[b_lt])
        S.op("scalar", lambda e: e.activation(out=lam_sb[:, 0:64], in_=lt[:, 0, :], func=AF.Copy, accum_out=ls[:, 0:1]), reads=[b_lt], writes=[b_lam, b_ls])
        S.op("scalar", lambda e: e.activation(out=lam_sb[:, 64:128], in_=lt[:, 1, :], func=AF.Copy, accum_out=ls[:, 1:2]), reads=[b_lt], writes=[b_lam, b_ls])
        S.op("scalar", lambda e: e.activation(out=ls[:, 2:4], in_=ls[:, 0:2], func=AF.Exp), reads=[b_ls], writes=[b_ls])
        S.op("vector", lambda e: e.tensor_tensor(out=ls[:, 4:5], in0=ls[:, 2:3], in1=ls[:, 3:4], op=ALU.subtract), reads=[b_ls], writes=[b_ls])
        S.op("vector", lambda e: e.tensor_scalar(out=ls[:, 5:6], in0=ls[:, 4:5], scalar1=float(lam_init), scalar2=-1.0, op0=ALU.add, op1=ALU.mult),
             reads=[b_ls], writes=[b_ls])
        neglam = ls[:, 5:6]
        sgs = C.sb([128, 2], F32, "sgs"); b_sg = Buf()
        S.dma("sync", lambda e: e.dma_start(out=sgs[:, 0:1], in_=sg_in[:, :]), writes=[b_sg])
        S.op("vector", lambda e: e.tensor_scalar(out=sgs[:, 1:2], in0=sgs[:, 0:1], scalar1=float(1.0 - lam_init), scalar2=None, op0=ALU.mult),
             reads=[b_sg], writes=[b_sg])
        gsc = sgs[:, 1:2]

    NSET = 2 if not even else 1
    QT = [C.sb([128, 4096], BF16) for _ in range(NSET)]
    KT = [C.sb([128, 4096], BF16) for _ in range(NSET)]
    SZ = [C.sb([128, 4096], BF16) for _ in range(NSET)]
    VV = [C.sb([128, 32, 128], BF16) for _ in range(NSET)]
    b_QT = [[Buf() for _ in range(8)] for _ in range(NSET)]
    b_KT = [[Buf() for _ in range(8)] for _ in range(NSET)]
    b_SZ = [[Buf() for _ in range(8)] for _ in range(NSET)]
    b_VV = [[Buf() for _ in range(8)] for _ in range(NSET)]
    Wh = [C.sb([128, 16, 512], BF16) for _ in range(2)]; b_Wh = [Buf(), Buf()]
    U = [C.sb([128, 16, 512], BF16) for _ in range(2)]; b_U = [Buf(), Buf()]
    wv = wc.rearrange("(k p) c -> p k c", p=128)
    uv = uT.rearrange("(k p) t -> p k t", p=128)
    MZo = [C.sb([128, 512], BF16) for _ in range(2)]; b_MZo = [Buf(), Buf()]
    state = {"u": 0, "pp": 0, "mzo": 0}

    def load_w(j):
        w = Wh[j % 2]
        for q in range(2):
            S.dma("gpsimd", lambda e, w=w, q=q, j=j: e.dma_start(out=w[:, q * 8:(q + 1) * 8, :], in_=wv[:, q * 8:(q + 1) * 8, j * 512:(j + 1) * 512]),
                  writes=[b_Wh[j % 2]])

    def proj(j, qscale):
        s = j % NSET
        w = Wh[j % 2]; bw = b_Wh[j % 2]
        for c in range(8):
            ui = state["u"] % 2
            state["u"] += 1
            u = U[ui]; bu = b_U[ui]
            for q in range(2):
                S.dma("sync", lambda e, u=u, q=q, c=c: e.dma_start(out=u[:, q * 8:(q + 1) * 8, :], in_=uv[:, q * 8:(q + 1) * 8, c * 512:(c + 1) * 512]),
                      writes=[bu])
            cs = slice(c * 512, (c + 1) * 512)
            for which in range(3):
                pi = 6 + state["pp"] % 2
                state["pp"] += 1
                p = PSB[pi]; bp = b_PSB[pi]
                col = (0, 128, 384)[which]
                for kk in range(16):
                    S.op("tensor", lambda e, p=p, w=w, u=u, kk=kk, col=col: e.matmul(p[:, :], lhsT=w[:, kk, col:col + 128], rhs=u[:, kk, :],
                                                                                      start=(kk == 0), stop=(kk == 15)),
                         reads=[bw, bu], writes=[bp])
                if which == 0:
                    S.op("scalar", lambda e, p=p, s=s, cs=cs: e.activation(out=QT[s][:, cs], in_=p[:, :], func=AF.Copy, scale=float(qscale)),
                         reads=[bp], writes=[b_QT[s][c]])
                elif which == 1:
                    S.op("vector", lambda e, p=p, s=s, cs=cs: e.tensor_copy(out=KT[s][:, cs], in_=p[:, :]), reads=[bp], writes=[b_KT[s][c]])
                else:
                    S.op("scalar", lambda e, p=p, s=s, cs=cs: e.activation(out=SZ[s][:, cs], in_=p[:, :], func=AF.Silu), reads=[bp], writes=[b_SZ[s][c]])
            pi = 6 + state["pp"] % 2
            state["pp"] += 1
            p = PSB[pi]; bp = b_PSB[pi]
            for sub in range(4):
                for kk in range(16):
                    S.op("tensor", lambda e, p=p, w=w, u=u, kk=kk, sub=sub: e.matmul(p[:, sub * 128:(sub + 1) * 128], lhsT=u[:, kk, sub * 128:(sub + 1) * 128],
                                                                                      rhs=w[:, kk, 256:384], start=(kk == 0), stop=(kk == 15)),
                         reads=[bw, bu], writes=[bp])
            S.op("vector", lambda e, p=p, s=s, c=c: e.tensor_copy(out=VV[s][:, 4 * c:4 * c + 4, :], in_=p[:, :].rearrange("p (a b) -> p a b", a=4)),
                 reads=[bp], writes=[b_VV[s][c]])

    def store_mz(j, qs, src_fn, reads):
        i = state["mzo"] % 2
        state["mzo"] += 1
        src_fn(MZo[i], b_MZo[i])
        S.dma("sync", lambda e, i=i: e.dma_start(out=mz_out[j * 128:(j + 1) * 128, qs * 512:(qs + 1) * 512], in_=MZo[i][:]),
              reads=[b_MZo[i]], final=True)

    if not even:
        E = [C.sb([128, 512], F32) for _ in range(2)]; b_E = [Buf(), Buf()]
        Lb = [C.sb([128, 512], BF16) for _ in range(3)]; b_Lb = [Buf() for _ in range(3)]
        LA = [C.sb([128, 512], BF16) for _ in range(3)]; b_LA = [Buf() for _ in range(3)]
        Ab = [C.sb([128, 512], BF16) for _ in range(3)]; b_Ab = [Buf() for _ in range(3)]

        def attn_odd(j):
            s = j % NSET
            units = [(qs, kb) for qs in range(NSB) for kb in range(4 * qs + 3, -1, -1)]
            n = len(units)
            lacc = {}

            def S0(u):
                qs, kb = units[u]
                z = PSB[u % 4]; bz = b_PSB[u % 4]
                diag = kb >= 4 * qs
                S.op("tensor", lambda e: e.matmul(z[:, :], lhsT=KT[s][:, kb * 128:(kb + 1) * 128], rhs=QT[s][:, qs * 512:(qs + 1) * 512],
                                                  start=True, stop=False),
                     reads=[b_KT[s][kb // 4], b_QT[s][qs]], writes=[bz])
                if diag:
                    x0 = 512 * qs - 128 * kb + 384
                    S.op("tensor", lambda e: e.matmul(z[:, :], lhsT=Jb[:], rhs=CMb[:, x0:x0 + 512], start=False, stop=False),
                         reads=[b_J, b_CM], writes=[bz])

            def S1(u):
                z = PSB[u % 4]; bz = b_PSB[u % 4]
                ee = E[u % 2]; be = b_E[u % 2]
                S.op("scalar", lambda e: e.activation(out=ee[:], in_=z[:, :], func=AF.Exp), reads=[bz], writes=[be])
                S.op("scalar", lambda e: e.activation(out=Lb[u % 3][:], in_=ee[:], func=AF.Ln, bias=1.0), reads=[be], writes=[b_Lb[u % 3]])

            def S23(u):
                qs, kb = units[u]
                z = PSB[u % 4]; bz = b_PSB[u % 4]
                first = kb == 4 * qs + 3
                last = kb == 0
                S.op("tensor", lambda e: e.matmul(z[:, :], lhsT=TRIb[:], rhs=Lb[u % 3][:], start=False, stop=first),
                     reads=[b_TRI, b_Lb[u % 3]], writes=[bz])
                if not first:
                    pa, pb = lacc[u - 1]
                    S.op("tensor", lambda e: e.matmul(z[:, :], lhsT=NEGb[:], rhs=pa[:], start=False, stop=True),
                         reads=[b_NEG, pb], writes=[bz])
                if not last:
                    if first:
                        lacc[u] = (Lb[u % 3], b_Lb[u % 3])
                    else:
                        pa, pb = lacc[u - 1]
                        S.op("vector", lambda e: e.tensor_tensor(out=LA[u % 3][:], in0=pa[:], in1=Lb[u % 3][:], op=ALU.add),
                             reads=[pb, b_Lb[u % 3]], writes=[b_LA[u % 3]])
                        lacc[u] = (LA[u % 3], b_LA[u % 3])
                lacc.pop(u - 2, None)

            def S45(u):
                qs, kb = units[u]
                z = PSB[u % 4]; bz = b_PSB[u % 4]
                first = kb == 4 * qs + 3
                last = kb == 0
                o = PSB[4 + qs % 2]; bo = b_PSB[4 + qs % 2]
                S.op("scalar", lambda e: e.activation(out=Ab[u % 3][:], in_=z[:, :], func=AF.Exp), reads=[bz], writes=[b_Ab[u % 3]])
                S.op("tensor", lambda e: e.matmul(o[:, :], lhsT=VV[s][:, kb, :], rhs=Ab[u % 3][:], start=first, stop=last),
                     reads=[b_VV[s][kb // 4], b_Ab[u % 3]], writes=[bo])
                if last:
                    def fin(dst, bdst):
                        S.op("vector", lambda e: e.tensor_tensor(out=dst[:], in0=o[:, :], in1=SZ[s][:, qs * 512:(qs + 1) * 512], op=ALU.mult),
                             reads=[bo, b_SZ[s][qs]], writes=[bdst])
                    store_mz(j, qs, fin, None)

            d1, d2 = PIPE_ODD
            for i in range(n + d2):
                if i < n:
                    S0(i)
                if 0 <= i - d1 < n:
                    S1(i - d1)
                    S23(i - d1)
                if 0 <= i - d2 < n:
                    S45(i - d2)

    else:
        Pb = [C.sb([128, 512], BF16) for _ in range(3)]; b_Pb = [Buf() for _ in range(3)]
        Rr = [C.sb([128, 512], F32) for _ in range(2)]; b_Rr = [Buf(), Buf()]
        On = [C.sb([128, 512], F32) for _ in range(3)]; b_On = [Buf() for _ in range(3)]
        Dd = C.sb([128, 512], F32, "Dd"); b_Dd = Buf()
        Dq = C.sb([128, 512], BF16, "Dq"); b_Dq = Buf()
        Sd = C.sb([128, 512], F32, "Sd"); b_Sd = Buf()
        cnt = {"u": 0, "acc": 0, "on": 0, "rr": 0}

        def softmax_sb(s, qs, rows, G, bG, is_A):
            if is_A:
                kbs = [kb for kb in range(4 * qs + 3, -1, -1) if 512 * qs - 128 * kb <= 2176]
            else:
                kbs = list(range(4 * qs + 3, -1, -1))
            ai = cnt["acc"] % 2
            cnt["acc"] += 1
            o = PSB[2 + 2 * ai]; bo = b_PSB[2 + 2 * ai]
            l = PSB[3 + 2 * ai]; bl = b_PSB[3 + 2 * ai]
            n = len(kbs)
            ids = []

            def S0(i):
                kb = kbs[i]
                u = cnt["u"]; cnt["u"] += 1
                ids.append(u)
                sp = PSB[u % 2]; bs = b_PSB[u % 2]
                D = 512 * qs - 128 * kb
                biased = is_A or D <= 1536
                S.op("tensor", lambda e: e.matmul(sp[:, :], lhsT=KT[s][rows, kb * 128:(kb + 1) * 128], rhs=QT[s][rows, qs * 512:(qs + 1) * 512],
                                                  start=True, stop=not biased),
                     reads=[b_KT[s][kb // 4], b_QT[s][qs]], writes=[bs])
                if biased:
                    x0 = D + 384
                    S.op("tensor", lambda e: e.matmul(sp[:, :], lhsT=Jb[:], rhs=G[:, x0:x0 + 512], start=False, stop=True),
                         reads=[b_J, bG], writes=[bs])

            def S1(i):
                u = ids[i]
                sp = PSB[u % 2]; bs = b_PSB[u % 2]
                S.op("scalar", lambda e: e.activation(out=Pb[u % 3][:], in_=sp[:, :], func=AF.Exp), reads=[bs], writes=[b_Pb[u % 3]])

            def S2(i):
                u = ids[i]
                kb = kbs[i]
                S.op("tensor", lambda e: e.matmul(o[:, :], lhsT=VV[s][:, kb, :], rhs=Pb[u % 3][:], start=(i == 0), stop=(i == n - 1)),
                     reads=[b_VV[s][kb // 4], b_Pb[u % 3]], writes=[bo])
                S.op("tensor", lambda e: e.matmul(l[:, :], lhsT=ONESb[:], rhs=Pb[u % 3][:], start=(i == 0), stop=(i == n - 1)),
                     reads=[b_ONES, b_Pb[u % 3]], writes=[bl])

            d1, d2 = PIPE_EVEN
            for i in range(n + d2):
                if i < n:
                    S0(i)
                if 0 <= i - d1 < n:
                    S1(i - d1)
                if 0 <= i - d2 < n:
                    S2(i - d2)
            ri = cnt["rr"] % 2; cnt["rr"] += 1
            oi = cnt["on"] % 3; cnt["on"] += 1
            S.op("vector", lambda e: e.reciprocal(out=Rr[ri][:], in_=l[:, :]), reads=[bl], writes=[b_Rr[ri]])
            S.op("vector", lambda e: e.tensor_tensor(out=On[oi][:], in0=o[:, :], in1=Rr[ri][:], op=ALU.mult), reads=[bo, b_Rr[ri]], writes=[b_On[oi]])
            return On[oi], b_On[oi]

        def attn_A(j):
            s = j % NSET
            for qs in range(NSB):
                on, bon = softmax_sb(s, qs, slice(0, 128), GA[j], b_GA[j], True)

                def fin(dst, bdst, on=on, bon=bon, qs=qs):
                    S.op("vector", lambda e: e.tensor_tensor(out=dst[:], in0=on[:], in1=SZ[s][:, qs * 512:(qs + 1) * 512], op=ALU.mult),
                         reads=[bon, b_SZ[s][qs]], writes=[bdst])
                store_mz(j, qs, fin, None)

        def attn_B(j):
            s = j % NSET
            jb = j - 4
            for qs in range(NSB):
                o1, bo1 = softmax_sb(s, qs, slice(0, 64), GB[jb], b_GB[jb], False)
                o2, bo2 = softmax_sb(s, qs, slice(64, 128), GB[jb], b_GB[jb], False)
                S.op("vector", lambda e, o1=o1, o2=o2: e.scalar_tensor_tensor(out=Dd[:], in0=o2[:], scalar=neglam, in1=o1[:], op0=ALU.mult, op1=ALU.add),
                     reads=[bo1, bo2, b_ls], writes=[b_Dd])
                S.op("vector", lambda e: e.tensor_tensor(out=Dq[:], in0=Dd[:], in1=Dd[:], op=ALU.mult), reads=[b_Dd], writes=[b_Dq])
                S.op("tensor", lambda e: e.matmul(PSB[6][:, :], lhsT=ONESb[:], rhs=Dq[:], start=True, stop=True), reads=[b_ONES, b_Dq], writes=[b_PSB[6]])
                S.op("scalar", lambda e: e.activation(out=Sd[:], in_=PSB[6][:, :], func=AF.Sqrt, scale=1.0 / 128, bias=EPS), reads=[b_PSB[6]], writes=[b_Sd])
                S.op("vector", lambda e: e.reciprocal(out=Sd[:], in_=Sd[:]), reads=[b_Sd], writes=[b_Sd])
                S.op("vector", lambda e: e.tensor_tensor(out=Dd[:], in0=Dd[:], in1=Sd[:], op=ALU.mult), reads=[b_Dd, b_Sd], writes=[b_Dd])

                def fin(dst, bdst, qs=qs):
                    S.op("vector", lambda e: e.scalar_tensor_tensor(out=dst[:], in0=Dd[:], scalar=gsc, in1=SZ[s][:, qs * 512:(qs + 1) * 512],
                                                                     op0=ALU.mult, op1=ALU.mult),
                         reads=[b_Dd, b_sg, b_SZ[s][qs]], writes=[bdst])
                store_mz(j, qs, fin, None)

    load_w(0)
    for j in range(8):
        if j + 1 < 8:
            load_w(j + 1)
        if not even:
            proj(j, 128 ** -0.5)
            attn_odd(j)
        elif j < 4:
            proj(j, 128 ** -0.5)
            attn_A(j)
        else:
            proj(j, 64 ** -0.5)
            attn_B(j)
    return C.finish()


def _t5_bucket_np(dist):
    dist = np.asarray(dist, dtype=np.int64)
    d = np.maximum(dist, 1).astype(np.float32)
    ratio = (np.log(d / np.float32(16.0)) / np.float32(math.log(2048 / 16)) * np.float32(16.0)).astype(np.float32)
    large = 16 + ratio.astype(np.int32)
    large = np.minimum(large, 31)
    return np.where(dist < 16, dist, large).astype(np.int64)


_CONST_CACHE = {}


def host_consts():
    if _CONST_CACHE:
        return _CONST_CACHE
    J = np.zeros((128, 128), np.float32)
    J[np.arange(128), 127 - np.arange(128)] = 1.0
    jj, ss = np.meshgrid(np.arange(128), np.arange(128), indexing="ij")
    TRI = np.where(jj >= ss, -1.0, 0.0).astype(np.float32)
    kk, xx = np.meshgrid(np.arange(128), np.arange(896), indexing="ij")
    CM = np.where(xx + kk - 511 <= 0, NEG, 0.0).astype(np.float32)
    idx = np.arange(GL)
    delta = idx - 511
    valid = delta >= 0
    bk = _t5_bucket_np(np.maximum(delta, 0))
    OH = np.zeros((32, GL), np.float32)
    OH[bk[valid], idx[valid]] = 1.0
    OHB = OH.copy()
    OHB[31, valid] -= 1.0
    mult = ((delta <= 128).astype(np.int64) + ((delta % 4 == 0) & (delta <= 512)).astype(np.int64)
            + ((delta % 16 == 0) & (delta <= 2048)).astype(np.int64))
    mult = np.where(valid, mult, 0)
    LMA = np.where(mult > 0, np.log(np.maximum(mult, 1).astype(np.float64)), NEG).astype(np.float32).reshape(1, GL)
    NGB = np.where(valid, 0.0, NEG).astype(np.float32).reshape(1, GL)
    _CONST_CACHE.update(J=J, TRI=TRI, CM=CM, OH=OH, OHB=OHB, LMA=LMA, NGB=NGB)
    return _CONST_CACHE


def head_cols(g, j, even):
    if not even:
        h = 8 * g + j
        return [(0, h * 128, 128), (128, 4096 + h * 128, 128), (256, 8192 + h * 128, 128), (384, 12288 + h * 128, 128)]
    if j < 4:
        h = 4 * g + j
        return [(0, h * 128, 128), (128, 2048 + h * 128, 128), (256, 4096 + h * 128, 128), (384, 12288 + h * 128, 128)]
    h = 4 * g + j - 4
    return [(0, 6144 + h * 64, 64), (64, 7168 + h * 64, 64), (128, 8192 + h * 64, 64), (192, 9216 + h * 64, 64),
            (256, 10240 + h * 128, 128), (384, 12288 + 2048 + h * 128, 128)]


def make_wc(w_in_l, g, even):
    wc = np.empty((2048, 4096), np.float32)
    for j in range(8):
        for d, s, w in head_cols(g, j, even):
            wc[:, j * 512 + d:j * 512 + d + w] = w_in_l[:, s:s + w]
    return wc


def inner_row(g, j, even):
    if not even:
        return (8 * g + j) * 128
    if j < 4:
        return (4 * g + j) * 128
    return 2048 + (4 * g + j - 4) * 128


U8 = mybir.dt.uint8
ARENA_KB = 200
GROUPS4 = [[0, 1, 2, 3], [4, 5, 6, 7]]


def _dtsize(dt):
    return 4 if dt == F32 else 2


class Arena:
    def __init__(self, C):
        self.t = C.sb([128, ARENA_KB * 1024], U8, "arena")
        self.base = 0
        self.off = 0

    h_free = False
    hoff = 0
    H_BYTES = 8 * 2048 * 4

    def alloc(self, shape, dt):
        n = _dtsize(dt)
        for d in shape[1:]:
            n *= d
        n = (n + 63) // 64 * 64
        if self.h_free and self.hoff + n <= self.H_BYTES:
            v = self.t[:, self.hoff:self.hoff + n].bitcast(dt)
            self.hoff += n
        else:
            assert self.off + n <= ARENA_KB * 1024, ("arena overflow", self.off, n)
            v = self.t[:, self.off:self.off + n].bitcast(dt)
            self.off += n
        nel = 1
        for d in shape[1:]:
            nel *= d
        v = v[:, 0:nel]
        if len(shape) == 3:
            v = v.rearrange("p (a b) -> p a b", a=shape[1])
        if shape[0] < 128:
            v = v[0:shape[0]]
        return v

    def persist(self):
        self.base = self.off

    def reset(self):
        self.off = self.base


def build_fused(nphase=None, dbg=False):
    C = Ctx()
    nc, S = C.nc, C.S
    A = Arena(C)
    hc_names = {}
    x_in = C.din("x", [1024, 2048], F32)
    cT = C.din("cT", [128, 16], F32)
    wm = C.din("wm", [2048, 6144], F32)
    bm = C.din("bm", [1, 6144], F32)
    wc = C.din("wc", [DEPTH, 2048, 4096], F32) if (nphase is None or nphase >= 4) else None
    wout = C.din("wout", [DEPTH, 4096, 2048], F32) if (nphase is None or nphase >= 5) else None
    gn = C.din("gn", [DEPTH, 2048], F32)
    gf = C.din("gf", [1, 2048], F32)
    id_in = C.din("ident", [128, 128], F32)
    j_in = C.din("J", [128, 128], F32)
    tri_in = C.din("TRI", [128, 128], F32)
    cm_in = C.din("CM", [128, 896], F32)
    oh_in = C.din("OH", [32, GL], F32)
    ohb_in = C.din("OHB", [32, GL], F32)
    lma_in = C.din("LMA", [1, GL], F32)
    ngb_in = C.din("NGB", [1, GL], F32)
    rb_in = C.din("RB", [32, 8], F32)
    lam_in = C.din("lamv", [2, 256], F32)
    sg_in = C.din("sg", [2, 128, 1], F32)
    y_out = C.dout("y", [1024, 2048], F32)

    mod_loc = C.dscr("mod_loc", [1, 6144], F32); b_modloc = Buf()
    mod_all = C.dscr("mod_all", [4, 6144], F32); b_modall = Buf()
    uT_loc = C.dscr("uT_loc", [2048, 1024], BF16); b_uTloc = [Buf() for _ in range(4)]
    uT_all = C.dscr("uT_all", [4 * 2048, 1024], BF16); b_uTall = [Buf() for _ in range(4)]
    mz_loc = C.dscr("mz_loc", [1024, 4096], BF16); b_mzloc = [Buf() for _ in range(8)]
    mz_all = C.dscr("mz_all", [4 * 1024, 4096], BF16); b_mzall = Buf()
    frA = C.dscr("frA", [4, GL], BF16); b_frA = Buf()
    frB = C.dscr("frB", [4, GL], BF16); b_frB = Buf()
    h_spill = C.dscr("h_spill", [1024, 2048], F32); b_hsp = Buf()
    spill = {"on": False}

    PSB = [C.ps([128, 512], F32) for _ in range(8)]
    b_PSB = [Buf() for _ in range(8)]

    H = A.alloc([128, 8, 2048], F32); b_H = [Buf() for _ in range(8)]
    ident = A.alloc([128, 128], BF16); b_id = Buf()
    Jb = A.alloc([128, 128], BF16); b_J = Buf()
    TRIb = A.alloc([128, 128], BF16); b_TRI = Buf()
    NEGb = A.alloc([128, 128], BF16); b_NEG = Buf()
    ONESb = A.alloc([128, 128], BF16); b_ONES = Buf()
    CMb = A.alloc([128, 896], BF16); b_CM = Buf()
    A.persist()
    S.dma("gpsimd", lambda e: e.dma_start(out=ident, in_=id_in[:, :]), writes=[b_id])
    S.dma("gpsimd", lambda e: e.dma_start(out=Jb, in_=j_in[:, :]), writes=[b_J])
    S.dma("gpsimd", lambda e: e.dma_start(out=TRIb, in_=tri_in[:, :]), writes=[b_TRI])
    S.dma("gpsimd", lambda e: e.dma_start(out=CMb, in_=cm_in[:, :]), writes=[b_CM])
    S.op("vector", lambda e: e.memset(NEGb, -1.0), writes=[b_NEG])
    S.op("vector", lambda e: e.memset(ONESb, 1.0), writes=[b_ONES])
    hv = x_in.rearrange("(t p) f -> p t f", p=128)
    for t in range(8):
        S.dma("sync", lambda e, t=t: e.dma_start(out=H[:, t, :], in_=hv[:, t, :]), writes=[b_H[t]])

    def phase_M():
        c_sb = A.alloc([128, 16], F32); b_c = Buf()
        e_sb = A.alloc([128, 16], F32); b_e = Buf()
        ca = A.alloc([128, 16], F32); b_ca = Buf()
        ones = A.alloc([128, 128], F32); b_ones = Buf()
        L = A.alloc([128, 16, 128], F32); b_L = Buf()
        bmr = A.alloc([1, 6144], F32); b_bmr = Buf()
        W = [A.alloc([128, 16, 512], F32) for _ in range(2)]; b_W = [Buf(), Buf()]
        res = A.alloc([1, 6144], F32); b_res = Buf()
        S.dma("sync", lambda e: e.dma_start(out=c_sb, in_=cT[:, :]), writes=[b_c])
        S.dma("sync", lambda e: e.dma_start(out=bmr, in_=bm[:, :]), writes=[b_bmr])
        S.op("vector", lambda e: e.memset(ones, 1.0), writes=[b_ones])
        S.op("scalar", lambda e: e.activation(out=e_sb, in_=c_sb, func=AF.Exp, scale=-1.0), reads=[b_c], writes=[b_e])
        S.op("vector", lambda e: e.tensor_scalar(out=e_sb, in0=e_sb, scalar1=1.0, scalar2=None, op0=ALU.add), reads=[b_e], writes=[b_e])
        S.op("vector", lambda e: e.reciprocal(out=e_sb, in_=e_sb), reads=[b_e], writes=[b_e])
        S.op("vector", lambda e: e.tensor_tensor(out=ca, in0=c_sb, in1=e_sb, op=ALU.mult), reads=[b_c, b_e], writes=[b_ca])
        for k in range(16):
            S.op("vector", lambda e, k=k: e.tensor_scalar(out=L[:, k, :], in0=ones, scalar1=ca[:, k:k + 1], scalar2=None, op0=ALU.mult),
                 reads=[b_ones, b_ca], writes=[b_L])
        wv = wm.rearrange("(k p) c -> p k c", p=128)
        for n in range(12):
            w = W[n % 2]; bw = b_W[n % 2]; p = PSB[n % 2]; bp = b_PSB[n % 2]
            S.dma("sync", lambda e, w=w, n=n: e.dma_start(out=w, in_=wv[:, :, n * 512:(n + 1) * 512]), writes=[bw])
            for k in range(16):
                S.op("tensor", lambda e, w=w, p=p, k=k: e.matmul(p[0:1, :], lhsT=L[:, k, 0:1], rhs=w[:, k, :], start=(k == 0), stop=False),
                     reads=[b_L, bw], writes=[bp])
            S.op("tensor", lambda e, p=p, n=n: e.matmul(p[0:1, :], lhsT=ones[0:1, 0:1], rhs=bmr[0:1, n * 512:(n + 1) * 512], start=False, stop=True),
                 reads=[b_ones, b_bmr], writes=[bp])
            S.op("scalar", lambda e, p=p, n=n: e.activation(out=res[0:1, n * 512:(n + 1) * 512], in_=p[0:1, :], func=AF.Copy), reads=[bp], writes=[b_res])
        S.dma("sync", lambda e: e.dma_start(out=mod_loc[:, :], in_=res), reads=[b_res], writes=[b_modloc])
        S.cc(lambda e: e.collective_compute("AllGather", ALU.bypass, replica_groups=GROUPS4, ins=[mod_loc[:, :]], outs=[mod_all[:, :]]),
             reads=[b_modloc], writes=[b_modall])

    def phase_G():
        ones4 = A.alloc([1, 4], F32); b_ones4 = Buf()
        S.op("vector", lambda e: e.memset(ones4, 1.0), writes=[b_ones4])
        RBs = A.alloc([32, 8], F32); b_RB = Buf()
        S.dma("sync", lambda e: e.dma_start(out=RBs, in_=rb_in[:, :]), writes=[b_RB])
        FRA = A.alloc([4, GL], BF16); b_FRA = Buf()
        FRB = A.alloc([4, GL], BF16); b_FRB = Buf()
        ohc = [A.alloc([32, 512], F32) for _ in range(2)]; b_ohc = [Buf(), Buf()]
        ohbc = [A.alloc([32, 512], F32) for _ in range(2)]; b_ohbc = [Buf(), Buf()]
        lmc = [A.alloc([1, 512], F32) for _ in range(2)]; b_lmc = [Buf(), Buf()]
        ngc = [A.alloc([1, 512], F32) for _ in range(2)]; b_ngc = [Buf(), Buf()]
        for ch in range(GL // 512):
            i = ch % 2
            cs = slice(ch * 512, (ch + 1) * 512)
            S.dma("sync", lambda e, i=i, cs=cs: e.dma_start(out=ohc[i], in_=oh_in[:, cs]), writes=[b_ohc[i]])
            S.dma("sync", lambda e, i=i, cs=cs: e.dma_start(out=ohbc[i], in_=ohb_in[:, cs]), writes=[b_ohbc[i]])
            S.dma("sync", lambda e, i=i, cs=cs: e.dma_start(out=lmc[i], in_=lma_in[:, cs]), writes=[b_lmc[i]])
            S.dma("sync", lambda e, i=i, cs=cs: e.dma_start(out=ngc[i], in_=ngb_in[:, cs]), writes=[b_ngc[i]])
            S.op("tensor", lambda e, i=i: e.matmul(PSB[6][0:4, :], lhsT=RBs[:, 0:4], rhs=ohc[i], start=True, stop=False),
                 reads=[b_RB, b_ohc[i]], writes=[b_PSB[6]])
            S.op("tensor", lambda e, i=i: e.matmul(PSB[6][0:4, :], lhsT=ones4[0:1, 0:4], rhs=lmc[i], start=False, stop=True),
                 reads=[b_ones4, b_lmc[i]], writes=[b_PSB[6]])
            S.op("scalar", lambda e, cs=cs: e.activation(out=FRA[:, cs], in_=PSB[6][0:4, :], func=AF.Copy), reads=[b_PSB[6]], writes=[b_FRA])
            S.op("tensor", lambda e, i=i: e.matmul(PSB[7][0:4, :], lhsT=RBs[:, 4:8], rhs=ohbc[i], start=True, stop=False),
                 reads=[b_RB, b_ohbc[i]], writes=[b_PSB[7]])
            S.op("tensor", lambda e, i=i: e.matmul(PSB[7][0:4, :], lhsT=ones4[0:1, 0:4], rhs=ngc[i], start=False, stop=True),
                 reads=[b_ones4, b_ngc[i]], writes=[b_PSB[7]])
            S.op("scalar", lambda e, cs=cs: e.activation(out=FRB[:, cs], in_=PSB[7][0:4, :], func=AF.Copy), reads=[b_PSB[7]], writes=[b_FRB])
        S.dma("sync", lambda e: e.dma_start(out=frA[:, :], in_=FRA), reads=[b_FRA], writes=[b_frA])
        S.dma("sync", lambda e: e.dma_start(out=frB[:, :], in_=FRB), reads=[b_FRB], writes=[b_frB])

    def phase_L31(l_prev, l_next, final):
        do_outproj = l_prev is not None
        do_norm = l_next is not None
        if spill["on"]:
            spill["on"] = False
            hsv = h_spill.rearrange("(t p) f -> p t f", p=128)
            for t in range(8):
                S.dma("sync", lambda e, t=t: e.dma_start(out=H[:, t, :], in_=hsv[:, t, :]), reads=[b_hsp], writes=[b_H[t]])
        if do_outproj:
            even = l_prev % 2 == 0
            MZ = A.alloc([128, 32, 1024], BF16); b_MZ = Buf()
            W = [A.alloc([128, 32, 256], BF16) for _ in range(2)]; b_W = [Buf(), Buf()]
            gate = A.alloc([128, 2048], F32); b_gate = Buf()
            tmp = [A.alloc([128, 256], F32) for _ in range(2)]; b_tmp = [Buf(), Buf()]
            mzv = mz_all.rearrange("(k p) t -> p k t", p=128)
            for q in range(4):
                S.dma("sync", lambda e, q=q: e.dma_start(out=MZ[:, q * 8:(q + 1) * 8, :],
                                                          in_=mzv[:, q * 8:(q + 1) * 8, bass.ds((S.rt["pid"] % 4) * 1024, 1024)]),
                      reads=[b_mzall], writes=[b_MZ])
            S.dma("sync", lambda e: e.dma_start(out=gate, in_=mod_all[l_prev:l_prev + 1, 4096:6144].partition_broadcast(128)),
                  reads=[b_modall], writes=[b_gate])
            if not even:
                wsrc = [wout[l_prev].rearrange("(g j p) c -> p j g c", g=4, j=8)[:, j] for j in range(8)]
            else:
                wa = wout[l_prev][0:2048, :].rearrange("(g j p) c -> p j g c", g=4, j=4)
                wb = wout[l_prev][2048:4096, :].rearrange("(g j p) c -> p j g c", g=4, j=4)
                wsrc = [wa[:, j] for j in range(4)] + [wb[:, j] for j in range(4)]
            def load_wout(f):
                w = W[f % 2]; bw = b_W[f % 2]
                fs = slice(f * 256, (f + 1) * 256)
                for j in range(8):
                    S.dma("gpsimd", lambda e, w=w, fs=fs, j=j: e.dma_start(out=w[:, j * 4:(j + 1) * 4, :], in_=wsrc[j][:, :, fs]), writes=[bw])

            it = 0
            load_wout(0)
            for f in range(8):
                w = W[f % 2]; bw = b_W[f % 2]
                fs = slice(f * 256, (f + 1) * 256)
                if f + 1 < 8:
                    load_wout(f + 1)
                for t in range(8):
                    p = PSB[it % 4]; bp = b_PSB[it % 4]; tm = tmp[it % 2]; btm = b_tmp[it % 2]
                    it += 1
                    for k in range(32):
                        S.op("tensor", lambda e, p=p, w=w, k=k, t=t: e.matmul(p[:, 0:256], lhsT=MZ[:, k, t * 128:(t + 1) * 128], rhs=w[:, k, :],
                                                                                start=(k == 0), stop=(k == 31)),
                             reads=[b_MZ, bw], writes=[bp])
                    S.op("vector", lambda e, p=p, tm=tm, fs=fs: e.tensor_tensor(out=tm, in0=p[:, 0:256], in1=gate[:, fs], op=ALU.mult),
                         reads=[bp, b_gate], writes=[btm])
                    S.op("vector", lambda e, tm=tm, t=t, fs=fs: e.tensor_tensor(out=H[:, t, fs], in0=H[:, t, fs], in1=tm, op=ALU.add),
                         reads=[btm, b_H[t]], writes=[b_H[t]])
            sq = W[0][:, 16:24, :].rearrange("p a b -> p (a b)"); b_sq = b_W[0]
            ub = [W[1][:, 0:8, :].rearrange("p a b -> p (a b)"), W[1][:, 8:16, :].rearrange("p a b -> p (a b)")]
            b_ub = [b_W[1], b_W[1]]
            uf = gate; b_uf = b_gate
            UT = MZ[:, 0:16, :]; b_UT = b_MZ
        else:
            sq = A.alloc([128, 2048], BF16); b_sq = Buf()
            ub = [A.alloc([128, 2048], BF16) for _ in range(2)]; b_ub = [Buf(), Buf()]
            uf = A.alloc([128, 2048], F32); b_uf = Buf()
            UT = A.alloc([128, 16, 1024], BF16); b_UT = Buf()
        st = A.alloc([128, 8, 4], F32); b_st = [Buf() for _ in range(8)]
        gs = A.alloc([128, 2048], F32); b_gs = Buf()
        if do_norm:
            shift = A.alloc([128, 2048], F32); b_shift = Buf()
            S.dma("sync", lambda e: e.dma_start(out=shift, in_=mod_all[l_next:l_next + 1, 0:2048].partition_broadcast(128)), reads=[b_modall], writes=[b_shift])
            S.dma("sync", lambda e: e.dma_start(out=gs, in_=mod_all[l_next:l_next + 1, 2048:4096].partition_broadcast(128)), reads=[b_modall], writes=[b_gs])
            S.dma("sync", lambda e: e.dma_start(out=uf, in_=gn[l_next:l_next + 1, :].partition_broadcast(128)), writes=[b_uf])
            S.op("vector", lambda e: e.scalar_tensor_tensor(out=gs, in0=gs, scalar=1.0, in1=uf, op0=ALU.add, op1=ALU.mult),
                 reads=[b_gs, b_uf], writes=[b_gs])
        else:
            S.dma("sync", lambda e: e.dma_start(out=gs, in_=gf[0:1, :].partition_broadcast(128)), writes=[b_gs])
            yo = [uf, A.alloc([128, 2048], F32)]; b_yo = [b_uf, Buf()]
            yv = y_out.rearrange("(t p) f -> p t f", p=128)
        nt = 0
        for t in range(8):
            S.op("scalar", lambda e, t=t: e.activation(out=sq, in_=H[:, t, :], func=AF.Square, accum_out=st[:, t, 0:1]),
                 reads=[b_H[t]], writes=[b_sq, b_st[t]])
            S.op("scalar", lambda e, t=t: e.activation(out=st[:, t, 1:2], in_=st[:, t, 0:1], func=AF.Sqrt, scale=1.0 / 2048, bias=EPS),
                 reads=[b_st[t]], writes=[b_st[t]])
            S.op("vector", lambda e, t=t: e.reciprocal(out=st[:, t, 2:3], in_=st[:, t, 1:2]), reads=[b_st[t]], writes=[b_st[t]])
            if do_norm:
                u = ub[t % 2]; bu = b_ub[t % 2]
                S.op("vector", lambda e, t=t: e.scalar_tensor_tensor(out=uf, in0=H[:, t, :], scalar=st[:, t, 2:3], in1=gs, op0=ALU.mult, op1=ALU.mult),
                     reads=[b_H[t], b_st[t], b_gs], writes=[b_uf])
                S.op("gpsimd", lambda e, u=u: e.tensor_tensor(out=u, in0=uf, in1=shift, op=ALU.add), reads=[b_uf, b_shift], writes=[bu])
                for kq in range(4):
                    pi = 4 + nt % 2
                    nt += 1
                    pt = PSB[pi][:, 0:256].bitcast(BF16); bpt = b_PSB[pi]
                    for j in range(4):
                        k = kq * 4 + j
                        S.op("tensor", lambda e, pt=pt, u=u, k=k, j=j: e.transpose(out=pt[:, j * 128:(j + 1) * 128], in_=u[:, k * 128:(k + 1) * 128], identity=ident),
                             reads=[bu, b_id], writes=[bpt])
                    S.op("scalar", lambda e, pt=pt, kq=kq, t=t: e.activation(out=UT[:, kq * 4:(kq + 1) * 4, t * 128:(t + 1) * 128],
                                                                              in_=pt.rearrange("p (a b) -> p a b", a=4), func=AF.Copy),
                         reads=[bpt], writes=[b_UT])
            else:
                y = yo[t % 2]; by = b_yo[t % 2]
                S.op("vector", lambda e, t=t, y=y: e.scalar_tensor_tensor(out=y, in0=H[:, t, :], scalar=st[:, t, 2:3], in1=gs, op0=ALU.mult, op1=ALU.mult),
                     reads=[b_H[t], b_st[t], b_gs], writes=[by])
                S.dma("sync", lambda e, t=t, y=y: e.dma_start(out=yv[:, t, :], in_=y), reads=[by], final=True)
        if do_norm:
            uv = uT_loc.rearrange("(k p) t -> p k t", p=128)
            for q in range(4):
                S.dma("sync", lambda e, q=q: e.dma_start(out=uv[:, q * 4:(q + 1) * 4, :], in_=UT[:, q * 4:(q + 1) * 4, :]), reads=[b_UT], writes=[b_uTloc[q]])
                S.cc(lambda e, q=q: e.collective_compute("AllGather", ALU.bypass, replica_groups=GROUPS4, ins=[uT_loc[q * 512:(q + 1) * 512, :]],
                                                         outs=[uT_all[q * 2048:(q + 1) * 2048, :]]),
                     reads=[b_uTloc[q]], writes=[b_uTall[q]])

    def phase_L2(l):
        even = l % 2 == 0
        lam_init = 0.8 - 0.6 * math.exp(-0.3 * l)
        NSET = 2
        CW = 512 if even else 256
        if even:
            hsv = h_spill.rearrange("(t p) f -> p t f", p=128)
            for t in range(8):
                S.dma("sync", lambda e, t=t: e.dma_start(out=hsv[:, t, :], in_=H[:, t, :]), reads=[b_H[t]], writes=[b_hsp])
            S.barrier()
            spill["on"] = True
            A.h_free = True
            A.hoff = 0
        QTs = [A.alloc([128, 4096], BF16) for _ in range(NSET)]; KTs = [A.alloc([128, 4096], BF16) for _ in range(NSET)]
        SZs = [A.alloc([128, 4096], BF16) for _ in range(NSET)]; VVs = [A.alloc([128, 32, 128], BF16) for _ in range(NSET)]
        b_QTs = [[Buf() for _ in range(8)] for _ in range(NSET)]; b_KTs = [[Buf() for _ in range(8)] for _ in range(NSET)]
        b_SZs = [[Buf() for _ in range(8)] for _ in range(NSET)]; b_VVs = [[Buf() for _ in range(8)] for _ in range(NSET)]
        QT, KT, SZ, VV = QTs[0], KTs[0], SZs[0], VVs[0]
        b_QT, b_KT, b_SZ, b_VV = b_QTs[0], b_KTs[0], b_SZs[0], b_VVs[0]
        Wh = [A.alloc([128, 16, 512], BF16) for _ in range(2)]; b_Wh = [Buf(), Buf()]
        U = [A.alloc([128, 16, CW], BF16) for _ in range(2)]; b_U = [Buf(), Buf()]
        MZo = [A.alloc([128, 512], BF16) for _ in range(2)]; b_MZo = [Buf(), Buf()]
        wv = wc[l].rearrange("(k p) c -> p k c", p=128)
        state = {"u": 0, "pp": 0, "mzo": 0}

        def load_w(j):
            w = Wh[j % 2]
            for q in range(2):
                S.dma("gpsimd", lambda e, w=w, q=q, j=j: e.dma_start(out=w[:, q * 8:(q + 1) * 8, :], in_=wv[:, q * 8:(q + 1) * 8, j * 512:(j + 1) * 512]),
                      writes=[b_Wh[j % 2]])

        def proj_items(j, qscale):
            s = j % NSET
            w = Wh[j % 2]; bw = b_Wh[j % 2]
            items = []
            NCH = 4096 // CW
            ubuf = {}

            def load_u(c):
                ui = state["u"] % 2
                state["u"] += 1
                u = U[ui]; bu = b_U[ui]
                ubuf[c] = (u, bu)
                r = (c * CW) // 1024
                co = (c * CW) % 1024
                for q in range(4):
                    src = uT_all[q * 2048 + r * 512:q * 2048 + (r + 1) * 512, :].rearrange("(k p) t -> p k t", p=128)
                    S.dma("sync", lambda e, u=u, q=q, src=src, co=co: e.dma_start(out=u[:, q * 4:(q + 1) * 4, :], in_=src[:, :, co:co + CW]),
                          reads=[b_uTall[q]], writes=[bu])

            items.append(lambda: load_u(0))
            for c in range(NCH):
                if c + 1 < NCH:
                    items.append(lambda c=c: load_u(c + 1))
                cs = slice(c * CW, (c + 1) * CW)
                c8 = (c * CW) // 512
                for which in range(3):
                    pi = 6 + state["pp"] % 2
                    state["pp"] += 1
                    col = (0, 128, 384)[which]
                    for k4 in range(4):
                        def it(c=c, pi=pi, col=col, k4=k4, which=which, cs=cs, c8=c8):
                            u, bu = ubuf[c]
                            p = PSB[pi]; bp = b_PSB[pi]
                            for kk in range(4 * k4, 4 * k4 + 4):
                                S.op("tensor", lambda e, kk=kk: e.matmul(p[:, 0:CW], lhsT=w[:, kk, col:col + 128], rhs=u[:, kk, :],
                                                                          start=(kk == 0), stop=(kk == 15)),
                                     reads=[bw, bu], writes=[bp])
                            if k4 == 3:
                                if which == 0:
                                    S.op("scalar", lambda e: e.activation(out=QTs[s][:, cs], in_=p[:, 0:CW], func=AF.Copy, scale=float(qscale)),
                                         reads=[bp], writes=[b_QTs[s][c8]])
                                elif which == 1:
                                    S.op("vector", lambda e: e.tensor_copy(out=KTs[s][:, cs], in_=p[:, 0:CW]), reads=[bp], writes=[b_KTs[s][c8]])
                                else:
                                    S.op("vector", lambda e: e.tensor_copy(out=SZs[s][:, cs], in_=p[:, 0:CW]), reads=[bp], writes=[b_SZs[s][c8]])
                        items.append(it)
                pi = 6 + state["pp"] % 2
                state["pp"] += 1
                nsub = CW // 128
                for sub in range(nsub):
                    for k4 in range(4):
                        def it(c=c, pi=pi, sub=sub, k4=k4, c8=c8, nsub=nsub):
                            u, bu = ubuf[c]
                            p = PSB[pi]; bp = b_PSB[pi]
                            for kk in range(4 * k4, 4 * k4 + 4):
                                S.op("tensor", lambda e, kk=kk: e.matmul(p[:, sub * 128:(sub + 1) * 128], lhsT=u[:, kk, sub * 128:(sub + 1) * 128],
                                                                          rhs=w[:, kk, 256:384], start=(kk == 0), stop=(kk == 15)),
                                     reads=[bw, bu], writes=[bp])
                            if sub == nsub - 1 and k4 == 3:
                                b0 = (c * CW) // 128
                                S.op("vector", lambda e: e.tensor_copy(out=VVs[s][:, b0:b0 + nsub, :],
                                                                        in_=p[:, 0:CW].rearrange("p (a b) -> p a b", a=nsub)),
                                     reads=[bp], writes=[b_VVs[s][c8]])
                        items.append(it)
            return items

        def silu_sz(j):
            s = j % NSET
            for c in range(8):
                cs = slice(c * 512, (c + 1) * 512)
                S.op("scalar", lambda e, cs=cs: e.activation(out=SZs[s][:, cs], in_=SZs[s][:, cs], func=AF.Silu), reads=[b_SZs[s][c]], writes=[b_SZs[s][c]])

        def proj(j, qscale):
            w = Wh[j % 2]; bw = b_Wh[j % 2]
            for c in range(8):
                ui = state["u"] % 2
                state["u"] += 1
                u = U[ui]; bu = b_U[ui]
                r = c // 2
                co = (c % 2) * 512
                for q in range(4):
                    src = uT_all[q * 2048 + r * 512:q * 2048 + (r + 1) * 512, :].rearrange("(k p) t -> p k t", p=128)
                    S.dma("sync", lambda e, u=u, q=q, src=src, co=co: e.dma_start(out=u[:, q * 4:(q + 1) * 4, :], in_=src[:, :, co:co + 512]),
                          reads=[b_uTall[q]], writes=[bu])
                cs = slice(c * 512, (c + 1) * 512)
                for which in range(3):
                    pi = 6 + state["pp"] % 2
                    state["pp"] += 1
                    p = PSB[pi]; bp = b_PSB[pi]
                    col = (0, 128, 384)[which]
                    for kk in range(16):
                        S.op("tensor", lambda e, p=p, w=w, u=u, kk=kk, col=col: e.matmul(p[:, :], lhsT=w[:, kk, col:col + 128], rhs=u[:, kk, :],
                                                                                          start=(kk == 0), stop=(kk == 15)),
                             reads=[bw, bu], writes=[bp])
                    if which == 0:
                        S.op("scalar", lambda e, p=p, cs=cs: e.activation(out=QT[:, cs], in_=p[:, :], func=AF.Copy, scale=float(qscale)),
                             reads=[bp], writes=[b_QT[c]])
                    elif which == 1:
                        S.op("vector", lambda e, p=p, cs=cs: e.tensor_copy(out=KT[:, cs], in_=p[:, :]), reads=[bp], writes=[b_KT[c]])
                    else:
                        S.op("scalar", lambda e, p=p, cs=cs: e.activation(out=SZ[:, cs], in_=p[:, :], func=AF.Silu), reads=[bp], writes=[b_SZ[c]])
                pi = 6 + state["pp"] % 2
                state["pp"] += 1
                p = PSB[pi]; bp = b_PSB[pi]
                for sub in range(4):
                    for kk in range(16):
                        S.op("tensor", lambda e, p=p, w=w, u=u, kk=kk, sub=sub: e.matmul(p[:, sub * 128:(sub + 1) * 128], lhsT=u[:, kk, sub * 128:(sub + 1) * 128],
                                                                                          rhs=w[:, kk, 256:384], start=(kk == 0), stop=(kk == 15)),
                             reads=[bw, bu], writes=[bp])
                S.op("vector", lambda e, p=p, c=c: e.tensor_copy(out=VV[:, 4 * c:4 * c + 4, :], in_=p[:, :].rearrange("p (a b) -> p a b", a=4)),
                     reads=[bp], writes=[b_VV[c]])

        def store_mz(j, qs, src_fn):
            i = state["mzo"] % 2
            state["mzo"] += 1
            src_fn(MZo[i], b_MZo[i])
            S.dma("sync", lambda e, i=i: e.dma_start(out=mz_loc[j * 128:(j + 1) * 128, qs * 512:(qs + 1) * 512], in_=MZo[i]),
                  reads=[b_MZo[i]], writes=[b_mzloc[j]])

        if not even:
            E = [A.alloc([128, 512], F32) for _ in range(2)]; b_E = [Buf(), Buf()]
            Lb = [A.alloc([128, 512], BF16) for _ in range(3)]; b_Lb = [Buf() for _ in range(3)]
            LA = [A.alloc([128, 512], BF16) for _ in range(3)]; b_LA = [Buf() for _ in range(3)]
            Ab = [A.alloc([128, 512], BF16) for _ in range(3)]; b_Ab = [Buf() for _ in range(3)]

            def attn_odd(j, bg):
                s_ = j % NSET
                QT, KT, SZ, VV = QTs[s_], KTs[s_], SZs[s_], VVs[s_]
                b_QT, b_KT, b_SZ, b_VV = b_QTs[s_], b_KTs[s_], b_SZs[s_], b_VVs[s_]
                units = [(qs, kb) for qs in range(NSB) for kb in range(4 * qs + 3, -1, -1)]
                n = len(units)
                lacc = {}
                nbg = len(bg)
                done = [0]

                def S0(u):
                    qs, kb = units[u]
                    z = PSB[u % 4]; bz = b_PSB[u % 4]
                    diag = kb >= 4 * qs
                    S.op("tensor", lambda e: e.matmul(z[:, :], lhsT=KT[:, kb * 128:(kb + 1) * 128], rhs=QT[:, qs * 512:(qs + 1) * 512], start=True, stop=False),
                         reads=[b_KT[kb // 4], b_QT[qs]], writes=[bz])
                    if diag:
                        x0 = 512 * qs - 128 * kb + 384
                        S.op("tensor", lambda e: e.matmul(z[:, :], lhsT=Jb, rhs=CMb[:, x0:x0 + 512], start=False, stop=False),
                             reads=[b_J, b_CM], writes=[bz])

                def S1a(u):
                    z = PSB[u % 4]; bz = b_PSB[u % 4]
                    ee = E[u % 2]; be = b_E[u % 2]
                    S.op("scalar", lambda e: e.activation(out=ee, in_=z[:, :], func=AF.Exp), reads=[bz], writes=[be])

                def S1b(u):
                    ee = E[u % 2]; be = b_E[u % 2]
                    S.op("scalar", lambda e: e.activation(out=Lb[u % 3], in_=ee, func=AF.Ln, bias=1.0), reads=[be], writes=[b_Lb[u % 3]])

                def S23(u):
                    qs, kb = units[u]
                    z = PSB[u % 4]; bz = b_PSB[u % 4]
                    first = kb == 4 * qs + 3
                    last = kb == 0
                    S.op("tensor", lambda e: e.matmul(z[:, :], lhsT=TRIb, rhs=Lb[u % 3], start=False, stop=first),
                         reads=[b_TRI, b_Lb[u % 3]], writes=[bz])
                    if not first:
                        pa, pb = lacc[u - 1]
                        S.op("tensor", lambda e: e.matmul(z[:, :], lhsT=NEGb, rhs=pa, start=False, stop=True), reads=[b_NEG, pb], writes=[bz])
                    if not last:
                        if first:
                            lacc[u] = (Lb[u % 3], b_Lb[u % 3])
                        else:
                            pa, pb = lacc[u - 1]
                            S.op("vector", lambda e: e.tensor_tensor(out=LA[u % 3], in0=pa, in1=Lb[u % 3], op=ALU.add),
                                 reads=[pb, b_Lb[u % 3]], writes=[b_LA[u % 3]])
                            lacc[u] = (LA[u % 3], b_LA[u % 3])
                    lacc.pop(u - 2, None)

                def S4(u):
                    z = PSB[u % 4]; bz = b_PSB[u % 4]
                    S.op("scalar", lambda e: e.activation(out=Ab[u % 3], in_=z[:, :], func=AF.Exp), reads=[bz], writes=[b_Ab[u % 3]])

                def S5(u):
                    qs, kb = units[u]
                    first = kb == 4 * qs + 3
                    last = kb == 0
                    o = PSB[4 + qs % 2]; bo = b_PSB[4 + qs % 2]
                    S.op("tensor", lambda e: e.matmul(o[:, :], lhsT=VV[:, kb, :], rhs=Ab[u % 3], start=first, stop=last),
                         reads=[b_VV[kb // 4], b_Ab[u % 3]], writes=[bo])
                    if last:
                        def fin(dst, bdst):
                            S.op("vector", lambda e: e.tensor_tensor(out=dst, in0=o[:, :], in1=SZ[:, qs * 512:(qs + 1) * 512], op=ALU.mult),
                                 reads=[bo, b_SZ[qs]], writes=[bdst])
                        store_mz(j, qs, fin)

                d1, d2 = PIPE_ODD
                for i in range(n + d2):
                    tgt = min(nbg, ((i + 1) * nbg + n - 1) // n) if i < n else nbg
                    while done[0] < tgt:
                        bg[done[0]]()
                        done[0] += 1
                    if i < n:
                        S0(i)
                    if 0 <= i - d1 < n:
                        S1a(i - d1)
                    if 0 <= i - d2 < n:
                        S4(i - d2)
                    if 0 <= i - d1 < n:
                        S1b(i - d1)
                        S23(i - d1)
                    if 0 <= i - d2 < n:
                        S5(i - d2)
        else:
            e_idx = l // 2
            G = [A.alloc([128, GW_A], BF16) for _ in range(2)]; b_G = [Buf(), Buf()]
            Pb = [A.alloc([128, 512], BF16) for _ in range(3)]; b_Pb = [Buf() for _ in range(3)]
            Rr = [A.alloc([128, 512], F32) for _ in range(1)]; b_Rr = [Buf()]
            On = [A.alloc([128, 512], F32) for _ in range(2)]; b_On = [Buf() for _ in range(2)]
            Pacc = [[A.alloc([128, 512], BF16) for _ in range(2)] for _ in range(2)]; b_Pacc = [[Buf(), Buf()], [Buf(), Buf()]]
            Qp = [[A.alloc([128, 512], BF16) for _ in range(2)] for _ in range(2)]; b_Qp = [[Buf(), Buf()], [Buf(), Buf()]]
            for m_ in range(2):
                for r_ in range(2):
                    S.op("vector", lambda e, m_=m_, r_=r_: e.memset(Qp[m_][r_], 0.0), writes=[b_Qp[m_][r_]])
            Dd = A.alloc([128, 512], F32); b_Dd = Buf()
            Dq = A.alloc([128, 512], BF16); b_Dq = Buf()
            lam_sb = A.alloc([128, 256], F32); b_lam = Buf()
            lt = A.alloc([128, 2, 64], F32); b_lt = Buf()
            ls = A.alloc([128, 8], F32); b_ls = Buf()
            sgs = A.alloc([128, 2], F32); b_sg = Buf()
            S.dma("sync", lambda e: e.dma_start(out=lam_sb, in_=lam_in[e_idx:e_idx + 1, :].partition_broadcast(128)), writes=[b_lam])
            S.op("vector", lambda e: e.tensor_tensor(out=lt[:, 0, :], in0=lam_sb[:, 0:64], in1=lam_sb[:, 64:128], op=ALU.mult), reads=[b_lam], writes=[b_lt])
            S.op("vector", lambda e: e.tensor_tensor(out=lt[:, 1, :], in0=lam_sb[:, 128:192], in1=lam_sb[:, 192:256], op=ALU.mult), reads=[b_lam], writes=[b_lt])
            S.op("scalar", lambda e: e.activation(out=lam_sb[:, 0:64], in_=lt[:, 0, :], func=AF.Copy, accum_out=ls[:, 0:1]), reads=[b_lt], writes=[b_lam, b_ls])
            S.op("scalar", lambda e: e.activation(out=lam_sb[:, 64:128], in_=lt[:, 1, :], func=AF.Copy, accum_out=ls[:, 1:2]), reads=[b_lt], writes=[b_lam, b_ls])
            S.op("scalar", lambda e: e.activation(out=ls[:, 2:4], in_=ls[:, 0:2], func=AF.Exp), reads=[b_ls], writes=[b_ls])
            S.op("vector", lambda e: e.tensor_tensor(out=ls[:, 4:5], in0=ls[:, 2:3], in1=ls[:, 3:4], op=ALU.subtract), reads=[b_ls], writes=[b_ls])
            S.op("vector", lambda e: e.tensor_scalar(out=ls[:, 5:6], in0=ls[:, 4:5], scalar1=float(lam_init), scalar2=-1.0, op0=ALU.add, op1=ALU.mult),
                 reads=[b_ls], writes=[b_ls])
            neglam = ls[:, 5:6]
            S.dma("sync", lambda e: e.dma_start(out=sgs[:, 0:1], in_=sg_in[e_idx]), writes=[b_sg])
            S.op("vector", lambda e: e.tensor_scalar(out=sgs[:, 1:2], in0=sgs[:, 0:1], scalar1=float(1.0 - lam_init), scalar2=None, op0=ALU.mult),
                 reads=[b_sg], writes=[b_sg])
            gsc = sgs[:, 1:2]
            cnt = {"u": 0, "acc": 0, "on": 0, "rr": 0, "qp": 0}

            def load_G(j):
                gi = j % 2
                if j < 4:
                    src = bass.AP(frA.tensor, frA.offset + j * GL, [[1, 128], [1, GW_A]])
                    S.dma("sync", lambda e: e.dma_start(out=G[gi], in_=src), reads=[b_frA], writes=[b_G[gi]])
                else:
                    src = bass.AP(frB.tensor, frB.offset + (j - 4) * GL, [[1, 128], [1, GW_B]])
                    S.dma("sync", lambda e: e.dma_start(out=G[gi][:, 0:GW_B], in_=src), reads=[b_frB], writes=[b_G[gi]])

            def softmax_sb(qs, qap, bq, Gt, bG, is_A, deferred=None):
                if is_A:
                    kbs = [kb for kb in range(4 * qs + 3, -1, -1) if 512 * qs - 128 * kb <= 2176]
                else:
                    kbs = list(range(4 * qs + 3, -1, -1))
                ai = cnt["acc"] % 2
                cnt["acc"] += 1
                o = PSB[3 + ai]; bo = b_PSB[3 + ai]
                lq = PSB[5]; bl = b_PSB[5]
                pac = Pacc[ai]; bpac = b_Pacc[ai]
                KT, VV = cur["KT"], cur["VV"]
                b_KT, b_VV = cur["b_KT"], cur["b_VV"]
                n = len(kbs)
                ids = []
                used = [False, False]

                def S0(i):
                    kb = kbs[i]
                    u = cnt["u"]; cnt["u"] += 1
                    ids.append(u)
                    sp = PSB[u % 3]; bs = b_PSB[u % 3]
                    D = 512 * qs - 128 * kb
                    biased = is_A or D <= 1536
                    S.op("tensor", lambda e: e.matmul(sp[:, :], lhsT=KT[:, kb * 128:(kb + 1) * 128], rhs=qap, start=True, stop=not biased),
                         reads=[b_KT[kb // 4], bq], writes=[bs])
                    if biased:
                        x0 = D + 384
                        S.op("tensor", lambda e: e.matmul(sp[:, :], lhsT=Jb, rhs=Gt[:, x0:x0 + 512], start=False, stop=True),
                             reads=[b_J, bG], writes=[bs])

                def S1(i):
                    u = ids[i]
                    sp = PSB[u % 3]; bs = b_PSB[u % 3]
                    S.op("scalar", lambda e: e.activation(out=Pb[u % 3], in_=sp[:, :], func=AF.Exp), reads=[bs], writes=[b_Pb[u % 3]])

                def S2(i):
                    u = ids[i]
                    kb = kbs[i]
                    S.op("tensor", lambda e: e.matmul(o[:, :], lhsT=VV[:, kb, :], rhs=Pb[u % 3], start=(i == 0), stop=(i == n - 1)),
                         reads=[b_VV[kb // 4], b_Pb[u % 3]], writes=[bo])
                    w_ = i % 2
                    eng = ("vector", "gpsimd")[w_]
                    if not used[w_]:
                        used[w_] = True
                        S.op(eng, lambda e: e.tensor_copy(out=pac[w_], in_=Pb[u % 3]), reads=[b_Pb[u % 3]], writes=[bpac[w_]])
                    else:
                        S.op(eng, lambda e: e.tensor_tensor(out=pac[w_], in0=pac[w_], in1=Pb[u % 3], op=ALU.add),
                             reads=[b_Pb[u % 3], bpac[w_]], writes=[bpac[w_]])

                d1, d2 = PIPE_EVEN
                tot = n + d2
                dpos = min(3, tot - 1)
                for i in range(tot):
                    if i < n:
                        bgs = cur["bg"]
                        left = len(bgs["items"]) - bgs["done"]
                        take = -(-left // max(1, bgs["units"]))
                        bgs["units"] -= 1
                        for _ in range(take):
                            bgs["items"][bgs["done"]]()
                            bgs["done"] += 1
                        S0(i)
                    if 0 <= i - d1 < n:
                        S1(i - d1)
                    if 0 <= i - d2 < n:
                        S2(i - d2)
                    if i == dpos and deferred is not None:
                        deferred()

                def norm():
                    S.op("tensor", lambda e: e.matmul(lq[:, :], lhsT=ONESb, rhs=pac[0], start=True, stop=not used[1]),
                         reads=[b_ONES, bpac[0]], writes=[bl])
                    if used[1]:
                        S.op("tensor", lambda e: e.matmul(lq[:, :], lhsT=ONESb, rhs=pac[1], start=False, stop=True),
                             reads=[b_ONES, bpac[1]], writes=[bl])
                    oi = cnt["on"] % 2; cnt["on"] += 1
                    S.op("vector", lambda e: e.reciprocal(out=Rr[0], in_=lq[:, :]), reads=[bl], writes=[b_Rr[0]])
                    S.op("vector", lambda e: e.tensor_tensor(out=On[oi], in0=o[:, :], in1=Rr[0], op=ALU.mult), reads=[bo, b_Rr[0]], writes=[b_On[oi]])
                    return On[oi], b_On[oi]
                return norm

            def attn_A(j):
                QT, SZ, b_QT, b_SZ = cur["QT"], cur["SZ"], cur["b_QT"], cur["b_SZ"]
                pending = [None]
                for qs in range(NSB):
                    nrm = softmax_sb(qs, QT[:, qs * 512:(qs + 1) * 512], b_QT[qs], G[j % 2], b_G[j % 2], True, deferred=pending[0])

                    def fin_all(nrm=nrm, qs=qs):
                        on, bon = nrm()

                        def fin(dst, bdst):
                            S.op("vector", lambda e: e.tensor_tensor(out=dst, in0=on, in1=SZ[:, qs * 512:(qs + 1) * 512], op=ALU.mult),
                                 reads=[bon, b_SZ[qs]], writes=[bdst])
                        store_mz(j, qs, fin)
                    pending[0] = fin_all
                pending[0]()

            def attn_B(j):
                QT, SZ, b_QT, b_SZ = cur["QT"], cur["SZ"], cur["b_QT"], cur["b_SZ"]
                pending = [None]
                for qs in range(NSB):
                    r_ = cnt["qp"] % 2; cnt["qp"] += 1
                    S.op("gpsimd", lambda e, r_=r_, qs=qs: e.tensor_copy(out=Qp[0][r_][0:64, :], in_=QT[0:64, qs * 512:(qs + 1) * 512]),
                         reads=[b_QT[qs]], writes=[b_Qp[0][r_]])
                    S.op("gpsimd", lambda e, r_=r_, qs=qs: e.tensor_copy(out=Qp[1][r_][64:128, :], in_=QT[64:128, qs * 512:(qs + 1) * 512]),
                         reads=[b_QT[qs]], writes=[b_Qp[1][r_]])
                    n1 = softmax_sb(qs, Qp[0][r_], b_Qp[0][r_], G[j % 2], b_G[j % 2], False, deferred=pending[0])
                    res1 = {}

                    def d1(n1=n1, res1=res1):
                        res1["o"] = n1()
                    n2 = softmax_sb(qs, Qp[1][r_], b_Qp[1][r_], G[j % 2], b_G[j % 2], False, deferred=d1)

                    def fin_all(n2=n2, res1=res1, qs=qs):
                        o1, bo1 = res1["o"]
                        o2, bo2 = n2()
                        S.op("vector", lambda e: e.scalar_tensor_tensor(out=Dd, in0=o2, scalar=neglam, in1=o1, op0=ALU.mult, op1=ALU.add),
                             reads=[bo1, bo2, b_ls], writes=[b_Dd])
                        S.op("vector", lambda e: e.tensor_tensor(out=Dq, in0=Dd, in1=Dd, op=ALU.mult), reads=[b_Dd], writes=[b_Dq])
                        S.op("tensor", lambda e: e.matmul(PSB[5][:, :], lhsT=ONESb, rhs=Dq, start=True, stop=True), reads=[b_ONES, b_Dq], writes=[b_PSB[5]])
                        S.op("scalar", lambda e: e.activation(out=Rr[0], in_=PSB[5][:, :], func=AF.Ln, scale=1.0 / 128, bias=EPS), reads=[b_PSB[5]], writes=[b_Rr[0]])
                        S.op("scalar", lambda e: e.activation(out=Rr[0], in_=Rr[0], func=AF.Exp, scale=-0.5), reads=[b_Rr[0]], writes=[b_Rr[0]])
                        S.op("vector", lambda e: e.tensor_tensor(out=Dd, in0=Dd, in1=Rr[0], op=ALU.mult), reads=[b_Dd, b_Rr[0]], writes=[b_Dd])

                        def fin(dst, bdst):
                            S.op("vector", lambda e: e.scalar_tensor_tensor(out=dst, in0=Dd, scalar=gsc, in1=SZ[:, qs * 512:(qs + 1) * 512], op0=ALU.mult, op1=ALU.mult),
                                 reads=[b_Dd, b_sg, b_SZ[qs]], writes=[bdst])
                        store_mz(j, qs, fin)
                    pending[0] = fin_all
                pending[0]()

        load_w(0)
        if not even:
            load_w(1)
            for itf in proj_items(0, 128 ** -0.5):
                itf()
        cur = {}
        if even:
            load_w(1)
            for itf in proj_items(0, 128 ** -0.5):
                itf()
        for j in range(8):
            if even:
                if j + 2 < 8:
                    load_w(j + 2)
                load_G(j)
                silu_sz(j)
                s_ = j % NSET
                nxt_scale = (128 ** -0.5) if (j + 1) < 4 else (64 ** -0.5)
                cur.update(QT=QTs[s_], KT=KTs[s_], SZ=SZs[s_], VV=VVs[s_], b_QT=b_QTs[s_], b_KT=b_KTs[s_], b_SZ=b_SZs[s_], b_VV=b_VVs[s_])
                units = sum(len([kb for kb in range(4 * qs + 3, -1, -1) if 512 * qs - 128 * kb <= 2176]) for qs in range(NSB)) if j < 4 \
                    else 2 * sum(4 * qs + 4 for qs in range(NSB))
                cur["bg"] = {"items": proj_items(j + 1, nxt_scale) if j + 1 < 8 else [], "done": 0, "units": units}
                if j < 4:
                    attn_A(j)
                else:
                    attn_B(j)
                bgs = cur["bg"]
                while bgs["done"] < len(bgs["items"]):
                    bgs["items"][bgs["done"]]()
                    bgs["done"] += 1
                S.cc(lambda e, j=j: e.collective_compute("AllGather", ALU.bypass, replica_groups=GROUPS4, ins=[mz_loc[j * 128:(j + 1) * 128, :]],
                                                         outs=[mz_all[j * 512:(j + 1) * 512, :]]),
                     reads=[b_mzloc[j]], writes=[b_mzall])
                continue
            if not even:
                if j + 2 < 8:
                    load_w(j + 2)
                silu_sz(j)
                bg = proj_items(j + 1, 128 ** -0.5) if j + 1 < 8 else []
                attn_odd(j, bg)
            elif j < 4:
                load_G(j)
                proj(j, 128 ** -0.5)
                attn_A(j)
            else:
                load_G(j)
                proj(j, 64 ** -0.5)
                attn_B(j)
            S.cc(lambda e, j=j: e.collective_compute("AllGather", ALU.bypass, replica_groups=GROUPS4, ins=[mz_loc[j * 128:(j + 1) * 128, :]],
                                                     outs=[mz_all[j * 512:(j + 1) * 512, :]]),
                 reads=[b_mzloc[j]], writes=[b_mzall])

    _pl2 = phase_L2

    def phase_L2(l):
        _pl2(l)
        A.h_free = False

    phases = [phase_M, phase_G, lambda: phase_L31(None, 0, False)]
    for l in range(DEPTH):
        last = l == DEPTH - 1
        phases.append(lambda l=l: phase_L2(l))
        phases.append(lambda l=l, last=last: phase_L31(l, None if last else l + 1, last))
    for i, ph in enumerate(phases):
        if nphase is not None and i >= nphase:
            break
        ph()
        S.barrier(); A.reset()
    if nphase is not None and nphase < len(phases):
        yv = y_out.rearrange("(t p) f -> p t f", p=128)
        for t in range(8):
            S.dma("sync", lambda e, t=t: e.dma_start(out=yv[:, t, :], in_=H[:, t, :]), reads=[b_H[t]], final=True)
    if dbg:
        d1 = C.dout("dbg_uT", [4 * 2048, 1024], BF16)
        d2 = C.dout("dbg_mz", [4 * 1024, 4096], BF16)
        d3 = C.dout("dbg_mod", [4, 6144], F32)
        for q in range(16 if (nphase is None or nphase >= 3) else 0):
            S.dma("sync", lambda e, q=q: e.dma_start(out=d1[q * 512:(q + 1) * 512, :], in_=uT_all[q * 512:(q + 1) * 512, :]), reads=b_uTall, final=True)
        if nphase is None or nphase >= 4:
            for q in range(32):
                S.dma("sync", lambda e, q=q: e.dma_start(out=d2[q * 128:(q + 1) * 128, :], in_=mz_all[q * 128:(q + 1) * 128, :]), reads=[b_mzall], final=True)
        S.dma("sync", lambda e: e.dma_start(out=d3[:, :], in_=mod_all[:, :]), reads=[b_modall], final=True)
    return C.finish()


_PROGS = {}


def kernel(x, c, norm_g, w_mod, b_mod, w_in, w_out, rel_bias, diff_lambda, diff_subln_g, final_norm_g):
    f32 = lambda a: np.ascontiguousarray(np.asarray(a, np.float32))
    x = f32(x); c = f32(c); norm_g = f32(norm_g); w_mod = f32(w_mod); b_mod = f32(b_mod); w_in = f32(w_in); w_out = f32(w_out)
    rel_bias = f32(rel_bias); diff_lambda = f32(diff_lambda); diff_subln_g = f32(diff_subln_g); final_norm_g = f32(final_norm_g)
    if "fused" not in _PROGS:
        _PROGS["fused"] = build_fused()
    nc = _PROGS["fused"]
    ims = make_inputs(x, c, norm_g, w_mod, b_mod, w_in, w_out, rel_bias, diff_lambda, diff_subln_g, final_norm_g)
    res = run_bass_kernel_spmd(nc, ims, core_ids=list(range(8))).results
    out = np.concatenate([np.asarray(res[i]["y"]) for i in range(8)], axis=0)
    return out.reshape(BATCH, SEQ, D_MODEL).astype(np.float32)


def make_inputs(x, c, norm_g, w_mod, b_mod, w_in, w_out, rel_bias, diff_lambda, diff_subln_g, final_norm_g):
    hc = host_consts()
    xf = x.reshape(8192, 2048)
    ident = np.eye(128, dtype=np.float32)
    ims = []
    for i in range(8):
        b, g = i // 4, i % 4
        lm = i % 4
        wcs = np.stack([make_wc(w_in[l], g, l % 2 == 0) for l in range(DEPTH)])
        cols = [4 * g + j for j in range(4)] + [16 + 4 * g + j for j in range(4)]
        ims.append({
            "x": np.ascontiguousarray(xf[i * 1024:(i + 1) * 1024]),
            "cT": np.ascontiguousarray(c[b].reshape(16, 128).T),
            "wm": w_mod[lm], "bm": np.ascontiguousarray(b_mod[lm].reshape(1, 6144)),
            "wc": wcs, "wout": w_out, "gn": norm_g, "gf": final_norm_g.reshape(1, 2048),
            "ident": ident, "J": hc["J"], "TRI": hc["TRI"], "CM": hc["CM"],
            "OH": hc["OH"], "OHB": hc["OHB"], "LMA": hc["LMA"], "NGB": hc["NGB"],
            "RB": np.ascontiguousarray(rel_bias[:, cols]),
            "lamv": np.ascontiguousarray(diff_lambda.reshape(2, 256)),
            "sg": np.ascontiguousarray(diff_subln_g.reshape(2, 128, 1)),
        })
    return ims
```

```python
import math
from contextlib import ExitStack

import numpy as np
import ml_dtypes
import concourse.bass as bass
import concourse.mybir as mybir
from concourse.bass_utils import run_bass_kernel_spmd

F32 = mybir.dt.float32
BF16 = mybir.dt.bfloat16
AF = mybir.ActivationFunctionType
ALU = mybir.AluOpType

D_MODEL = 2048
SEQ = 4096
BATCH = 2
DEPTH = 4
D_INNER = 4096
EPS = 1e-6
NEG = -30000.0

ENGS = ("sync", "scalar", "vector", "gpsimd", "tensor")
DMA_K = 6


class Buf:
    __slots__ = ("name", "last_w", "readers")

    def __init__(self, name=""):
        self.name = name
        self.last_w = None
        self.readers = {}


class Sched:
    def __init__(self, nc, stack):
        self.nc = nc
        self.streams = {e: [] for e in ENGS}
        self.count = {e: 0 for e in ENGS}
        self.seen = {e: {} for e in ENGS}
        self.sems = {}
        for e in ENGS:
            self.sems[e] = stack.enter_context(nc.semaphore("s_" + e))
        self.dma_n = {e: 0 for e in ("sync", "scalar", "gpsimd")}
        for e in ("sync", "scalar", "gpsimd"):
            for k in range(DMA_K):
                key = ("d", e, k)
                self.sems[key] = stack.enter_context(nc.semaphore(f"d_{e}_{k}"))
                self.count[key] = 0
        self.final_events = []
        self.sems["cc"] = stack.enter_context(nc.semaphore("s_cc"))
        self.count["cc"] = 0
        self.pending = {}
        self.rt = {}

    def barrier(self):
        snap = {k: v for k, v in self.count.items() if v > 0}
        for e in ENGS:
            p = self.pending.setdefault(e, {})
            for k, v in snap.items():
                if p.get(k, 0) < v:
                    p[k] = v

    def cc(self, fn, reads=(), writes=(), serialize=True):
        deps = self._deps(reads, writes)
        prev = self.count["cc"]
        if serialize and prev > 0 and deps.get("cc", 0) < prev:
            deps["cc"] = prev
        self.count["cc"] += 1
        ev = ("cc", self.count["cc"])
        self._finish("gpsimd", deps, fn, ev, None, reads, writes)
        return ev

    def _deps(self, reads, writes):
        deps = {}
        for b in reads:
            ev = b.last_w
            if ev is not None and deps.get(ev[0], 0) < ev[1]:
                deps[ev[0]] = ev[1]
        for b in writes:
            ev = b.last_w
            if ev is not None and deps.get(ev[0], 0) < ev[1]:
                deps[ev[0]] = ev[1]
            for k, v in b.readers.items():
                if deps.get(k, 0) < v:
                    deps[k] = v
        return deps

    def _finish(self, eng, deps, fn, ev, inc, reads, writes):
        pend = self.pending.pop(eng, None)
        if pend:
            for k, v in pend.items():
                if deps.get(k, 0) < v:
                    deps[k] = v
        waits = []
        seen = self.seen[eng]
        for k, v in deps.items():
            if k == "tensor" and eng == "tensor":
                continue
            if seen.get(k, 0) >= v:
                continue
            seen[k] = v
            waits.append((k, v))
        self.streams[eng].append((waits, fn, (ev[0], inc)))
        for b in reads:
            if b.readers.get(ev[0], 0) < ev[1]:
                b.readers[ev[0]] = ev[1]
        for b in writes:
            b.last_w = ev
            b.readers = {}

    def op(self, eng, fn, reads=(), writes=()):
        deps = self._deps(reads, writes)
        self.count[eng] += 1
        ev = (eng, self.count[eng])
        self._finish(eng, deps, fn, ev, 1, reads, writes)
        return ev

    def dma(self, eng, fn, reads=(), writes=(), final=False):
        deps = self._deps(reads, writes)
        n = self.dma_n[eng]
        self.dma_n[eng] += 1
        key = ("d", eng, n % DMA_K)
        prev = self.count[key]
        if prev > 0 and deps.get(key, 0) < prev:
            deps[key] = prev
        self.count[key] += 16
        ev = (key, self.count[key])
        self._finish(eng, deps, fn, ev, 16, reads, writes)
        if final:
            self.final_events.append(ev)
        return ev

    def emit(self, final_eng="sync"):
        nc = self.nc
        fw = {}
        for k, v in self.final_events:
            fw[k] = max(fw.get(k, 0), v)
        streams = self.streams
        sems = self.sems
        with nc.Block() as block:
            def mk(ename):
                def body(eng):
                    if ename == "sync":
                        self.rt["pid"] = nc.partition_id([eng.engine])
                    for waits, fn, (sk, inc) in streams[ename]:
                        for k, v in waits:
                            eng.wait_ge(sems[k], v)
                        inst = fn(eng)
                        if inc is None:
                            inst.then_inc(sems[sk])
                        else:
                            inst.then_inc(sems[sk], inc)
                    if ename == final_eng:
                        for k, v in fw.items():
                            eng.wait_ge(sems[k], v)
                return body
            block.sync(mk("sync"))
            block.scalar(mk("scalar"))
            block.vector(mk("vector"))
            block.gpsimd(mk("gpsimd"))
            block.tensor(mk("tensor"))


class Ctx:
    def __init__(self):
        self.nc = bass.Bass("TRN2", target_bir_lowering=False)
        self.stack = ExitStack()
        self.S = Sched(self.nc, self.stack)
        self.n = 0

    def sb(self, shape, dt, name=None):
        self.n += 1
        return self.stack.enter_context(self.nc.sbuf_tensor(name or f"sb{self.n}", list(shape), dt))

    def ps(self, shape, dt, name=None):
        self.n += 1
        return self.stack.enter_context(self.nc.psum_tensor(name or f"ps{self.n}", list(shape), dt))

    def din(self, name, shape, dt):
        return self.nc.dram_tensor(name, list(shape), dt, kind="ExternalInput").ap()

    def dout(self, name, shape, dt):
        return self.nc.dram_tensor(name, list(shape), dt, kind="ExternalOutput").ap()

    def dscr(self, name, shape, dt):
        return self.nc.dram_tensor(name, list(shape), dt, kind="Internal").ap()

    def finish(self):
        self.S.emit()
        self.stack.close()
        return self.nc


def build_M():
    C = Ctx()
    nc, S = C.nc, C.S
    cT = C.din("cT", [128, 16], F32)
    wm = C.din("wm", [2048, 6144], F32)
    bm = C.din("bm", [1, 6144], F32)
    out = C.dout("mod", [128, 6144], F32)

    c_sb = C.sb([128, 16], F32); b_c = Buf()
    e_sb = C.sb([128, 16], F32); b_e = Buf()
    ca = C.sb([128, 16], F32); b_ca = Buf()
    ones = C.sb([128, 128], F32); b_ones = Buf()
    L = C.sb([128, 16, 128], F32); b_L = Buf()
    bmr = C.sb([1, 6144], F32); b_bmr = Buf()
    W = [C.sb([128, 16, 512], F32) for _ in range(2)]; b_W = [Buf(), Buf()]
    res = C.sb([128, 6144], F32); b_res = Buf()
    P = [C.ps([128, 512], F32) for _ in range(2)]; b_P = [Buf(), Buf()]

    S.dma("sync", lambda e: e.dma_start(out=c_sb[:], in_=cT[:, :]), writes=[b_c])
    S.dma("sync", lambda e: e.dma_start(out=bmr[:], in_=bm[:, :]), writes=[b_bmr])
    S.op("vector", lambda e: e.memset(ones[:], 1.0), writes=[b_ones])
    S.op("scalar", lambda e: e.activation(out=e_sb[:], in_=c_sb[:], func=AF.Exp, scale=-1.0), reads=[b_c], writes=[b_e])
    S.op("vector", lambda e: e.tensor_scalar(out=e_sb[:], in0=e_sb[:], scalar1=1.0, scalar2=None, op0=ALU.add), reads=[b_e], writes=[b_e])
    S.op("vector", lambda e: e.reciprocal(out=e_sb[:], in_=e_sb[:]), reads=[b_e], writes=[b_e])
    S.op("vector", lambda e: e.tensor_tensor(out=ca[:], in0=c_sb[:], in1=e_sb[:], op=ALU.mult), reads=[b_c, b_e], writes=[b_ca])
    for k in range(16):
        S.op("vector", lambda e, k=k: e.tensor_scalar(out=L[:, k, :], in0=ones[:], scalar1=ca[:, k:k + 1], scalar2=None, op0=ALU.mult),
             reads=[b_ones, b_ca], writes=[b_L])
    wv = wm.rearrange("(k p) c -> p k c", p=128)
    for n in range(12):
        w = W[n % 2]; bw = b_W[n % 2]; p = P[n % 2]; bp = b_P[n % 2]
        S.dma("sync", lambda e, w=w, n=n: e.dma_start(out=w[:], in_=wv[:, :, n * 512:(n + 1) * 512]), writes=[bw])
        for k in range(16):
            S.op("tensor", lambda e, w=w, p=p, k=k: e.matmul(p[:, :], lhsT=L[:, k, :], rhs=w[:, k, :], start=(k == 0), stop=False),
                 reads=[b_L, bw], writes=[bp])
        S.op("tensor", lambda e, p=p, n=n: e.matmul(p[:, :], lhsT=ones[0:1, :], rhs=bmr[0:1, n * 512:(n + 1) * 512], start=False, stop=True),
             reads=[b_ones, b_bmr], writes=[bp])
        S.op("scalar", lambda e, p=p, n=n: e.activation(out=res[:, n * 512:(n + 1) * 512], in_=p[:, :], func=AF.Copy), reads=[bp], writes=[b_res])
    S.dma("sync", lambda e: e.dma_start(out=out[:, :], in_=res[:]), reads=[b_res], final=True)
    return C.finish()


def build_L31(do_outproj, do_norm, do_final):
    C = Ctx()
    nc, S = C.nc, C.S
    h_in = C.din("h_in", [1024, 2048], F32)
    if do_outproj:
        mzT = C.din("mzT", [4096, 1024], BF16)
        wout = C.din("wout", [4096, 2048], F32)
        modp = C.din("modp", [128, 6144], F32)
    if do_norm:
        modn = C.din("modn", [128, 6144], F32)
        gn = C.din("gn", [128, 2048], F32)
        uT_out = C.dout("uT", [2048, 1024], BF16)
        h_out = C.dout("h_out", [1024, 2048], F32)
    if do_final:
        gf = C.din("gf", [128, 2048], F32)
        y_out = C.dout("y", [1024, 2048], F32)

    H = C.sb([128, 8, 2048], F32, "H"); b_H = [Buf() for _ in range(8)]
    hv = h_in.rearrange("(t p) f -> p t f", p=128)
    for t in range(8):
        S.dma("sync", lambda e, t=t: e.dma_start(out=H[:, t, :], in_=hv[:, t, :]), writes=[b_H[t]])

    PS = [C.ps([128, 512], F32) for _ in range(4)]; b_PS = [Buf() for _ in range(4)]

    if do_outproj:
        MZ = C.sb([128, 32, 1024], BF16, "MZ"); b_MZ = Buf()
        mzv = mzT.rearrange("(k p) t -> p k t", p=128)
        for q in range(4):
            S.dma("sync", lambda e, q=q: e.dma_start(out=MZ[:, q * 8:(q + 1) * 8, :], in_=mzv[:, q * 8:(q + 1) * 8, :]), writes=[b_MZ])
        gate = C.sb([128, 2048], F32, "gate"); b_gate = Buf()
        S.dma("sync", lambda e: e.dma_start(out=gate[:], in_=modp[:, 4096:6144]), writes=[b_gate])
        W = [C.sb([128, 32, 256], BF16) for _ in range(2)]; b_W = [Buf(), Buf()]
        tmp = [C.sb([128, 256], F32) for _ in range(2)]; b_tmp = [Buf(), Buf()]
        wv = wout.rearrange("(k p) c -> p k c", p=128)
        it = 0
        for f in range(8):
            w = W[f % 2]; bw = b_W[f % 2]
            for q in range(2):
                S.dma("gpsimd", lambda e, w=w, f=f, q=q: e.dma_start(out=w[:, q * 16:(q + 1) * 16, :], in_=wv[:, q * 16:(q + 1) * 16, f * 256:(f + 1) * 256]),
                      writes=[bw])
            for t in range(8):
                p = PS[it % 4]; bp = b_PS[it % 4]; tm = tmp[it % 2]; btm = b_tmp[it % 2]
                it += 1
                for k in range(32):
                    S.op("tensor", lambda e, p=p, w=w, k=k, t=t: e.matmul(p[:, 0:256], lhsT=MZ[:, k, t * 128:(t + 1) * 128], rhs=w[:, k, :],
                                                                            start=(k == 0), stop=(k == 31)),
                         reads=[b_MZ, bw], writes=[bp])
                S.op("vector", lambda e, p=p, tm=tm, f=f: e.tensor_tensor(out=tm[:], in0=p[:, 0:256], in1=gate[:, f * 256:(f + 1) * 256], op=ALU.mult),
                     reads=[bp, b_gate], writes=[btm])
                S.op("gpsimd", lambda e, tm=tm, t=t, f=f: e.tensor_tensor(out=H[:, t, f * 256:(f + 1) * 256], in0=H[:, t, f * 256:(f + 1) * 256], in1=tm[:], op=ALU.add),
                     reads=[btm, b_H[t]], writes=[b_H[t]])

    if do_norm or do_final:
        if do_outproj:
            sq = W[0][:, 16:24, :].rearrange("p a b -> p (a b)"); b_sq = b_W[0]
        else:
            sq = C.sb([128, 2048], BF16, "sq")[:]; b_sq = Buf()
        st = C.sb([128, 8, 4], F32, "st"); b_st = [Buf() for _ in range(8)]
        if do_norm:
            shift = C.sb([128, 2048], F32, "shift"); b_shift = Buf()
            gs = C.sb([128, 2048], F32, "gs"); b_gs = Buf()
            g_sb = C.sb([128, 2048], F32, "g_sb"); b_g = Buf()
            S.dma("sync", lambda e: e.dma_start(out=shift[:], in_=modn[:, 0:2048]), writes=[b_shift])
            S.dma("sync", lambda e: e.dma_start(out=gs[:], in_=modn[:, 2048:4096]), writes=[b_gs])
            S.dma("sync", lambda e: e.dma_start(out=g_sb[:], in_=gn[:, :]), writes=[b_g])
            S.op("vector", lambda e: e.scalar_tensor_tensor(out=gs[:], in0=gs[:], scalar=1.0, in1=g_sb[:], op0=ALU.add, op1=ALU.mult),
                 reads=[b_gs, b_g], writes=[b_gs])
            ident = C.sb([128, 128], BF16, "ident_sb"); b_id = Buf()
            idf = C.sb([128, 128], F32, "idf"); b_idf = Buf()
            idin = C.din("ident", [128, 128], F32)
            S.dma("sync", lambda e: e.dma_start(out=idf[:], in_=idin[:, :]), writes=[b_idf])
            S.op("vector", lambda e: e.tensor_copy(out=ident[:], in_=idf[:]), reads=[b_idf], writes=[b_id])
            if do_outproj:
                ub = [W[1][:, 0:8, :].rearrange("p a b -> p (a b)"), W[1][:, 8:16, :].rearrange("p a b -> p (a b)")]
                b_ub = [b_W[1], b_W[1]]
                uf = gate[:]; b_uf = b_gate
                UT = MZ[:, 0:16, :]; b_UT = [b_MZ]
            else:
                ub = [C.sb([128, 2048], BF16)[:] for _ in range(2)]; b_ub = [Buf(), Buf()]
                uf = C.sb([128, 2048], F32, "uf")[:]; b_uf = Buf()
                UT = C.sb([128, 16, 1024], BF16, "UT")[:]; b_UT = [Buf()]
            PT = [C.ps([128, 512], BF16) for _ in range(2)]; b_PT = [Buf(), Buf()]
            hov = h_out.rearrange("(t p) f -> p t f", p=128)
        else:
            g_sb = C.sb([128, 2048], F32, "g_sb"); b_g = Buf()
            S.dma("sync", lambda e: e.dma_start(out=g_sb[:], in_=gf[:, :]), writes=[b_g])
            yo = [C.sb([128, 2048], F32) for _ in range(2)]; b_yo = [Buf(), Buf()]
            yv = y_out.rearrange("(t p) f -> p t f", p=128)
        nt = 0
        for t in range(8):
            if do_norm:
                S.dma("sync", lambda e, t=t: e.dma_start(out=hov[:, t, :], in_=H[:, t, :]), reads=[b_H[t]], final=True)
            S.op("scalar", lambda e, t=t: e.activation(out=sq, in_=H[:, t, :], func=AF.Square, accum_out=st[:, t, 0:1]),
                 reads=[b_H[t]], writes=[b_sq, b_st[t]])
            S.op("scalar", lambda e, t=t: e.activation(out=st[:, t, 1:2], in_=st[:, t, 0:1], func=AF.Sqrt, scale=1.0 / 2048, bias=EPS),
                 reads=[b_st[t]], writes=[b_st[t]])
            S.op("vector", lambda e, t=t: e.reciprocal(out=st[:, t, 2:3], in_=st[:, t, 1:2]), reads=[b_st[t]], writes=[b_st[t]])
            if do_norm:
                u = ub[t % 2]; bu = b_ub[t % 2]
                S.op("vector", lambda e, t=t: e.scalar_tensor_tensor(out=uf, in0=H[:, t, :], scalar=st[:, t, 2:3], in1=gs[:], op0=ALU.mult, op1=ALU.mult),
                     reads=[b_H[t], b_st[t], b_gs], writes=[b_uf])
                S.op("gpsimd", lambda e, u=u: e.tensor_tensor(out=u, in0=uf, in1=shift[:], op=ALU.add), reads=[b_uf, b_shift], writes=[bu])
                for kq in range(4):
                    pt = PT[nt % 2]; bpt = b_PT[nt % 2]
                    nt += 1
                    for j in range(4):
                        k = kq * 4 + j
                        S.op("tensor", lambda e, pt=pt, u=u, k=k, j=j: e.transpose(out=pt[:, j * 128:(j + 1) * 128], in_=u[:, k * 128:(k + 1) * 128], identity=ident[:]),
                             reads=[bu, b_id], writes=[bpt])
                    S.op("scalar", lambda e, pt=pt, kq=kq, t=t: e.activation(out=UT[:, kq * 4:(kq + 1) * 4, t * 128:(t + 1) * 128],
                                                                              in_=pt[:, :].rearrange("p (a b) -> p a b", a=4), func=AF.Copy),
                         reads=[bpt], writes=[b_UT[0]])
            else:
                y = yo[t % 2]; by = b_yo[t % 2]
                S.op("vector", lambda e, t=t, y=y: e.scalar_tensor_tensor(out=y[:], in0=H[:, t, :], scalar=st[:, t, 2:3], in1=g_sb[:], op0=ALU.mult, op1=ALU.mult),
                     reads=[b_H[t], b_st[t], b_g], writes=[by])
                S.dma("sync", lambda e, t=t, y=y: e.dma_start(out=yv[:, t, :], in_=y[:]), reads=[by], final=True)
        if do_norm:
            uv = uT_out.rearrange("(k p) t -> p k t", p=128)
            for q in range(4):
                S.dma("sync", lambda e, q=q: e.dma_start(out=uv[:, q * 4:(q + 1) * 4, :], in_=UT[:, q * 4:(q + 1) * 4, :]), reads=b_UT, final=True)
    return C.finish()


GL = 3584
GW_A = 3072
GW_B = 2432
NSB = 8
PIPE_ODD = (1, 3)
PIPE_EVEN = (1, 2)


def build_L2(even, lam_init=0.0):
    C = Ctx()
    nc, S = C.nc, C.S
    uT = C.din("uT", [2048, 4096], BF16)
    wc = C.din("wc", [2048, 4096], F32)
    mz_out = C.dout("mz", [1024, 4096], BF16)
    jin = C.din("J", [128, 128], F32)

    PSB = [C.ps([128, 512], F32) for _ in range(8)]
    b_PSB = [Buf() for _ in range(8)]

    cst = C.sb([128, 128], F32, "cst"); b_cst = Buf()
    Jb = C.sb([128, 128], BF16, "Jb"); b_J = Buf()
    S.dma("sync", lambda e: e.dma_start(out=cst[:], in_=jin[:, :]), writes=[b_cst])
    S.op("vector", lambda e: e.tensor_copy(out=Jb[:], in_=cst[:]), reads=[b_cst], writes=[b_J])

    if not even:
        tri_in = C.din("TRI", [128, 128], F32)
        cm_in = C.din("CM", [128, 896], F32)
        TRIb = C.sb([128, 128], BF16, "TRIb"); b_TRI = Buf()
        NEGb = C.sb([128, 128], BF16, "NEGb"); b_NEG = Buf()
        CMb = C.sb([128, 896], BF16, "CMb"); b_CM = Buf()
        cmf = C.sb([128, 896], F32, "cmf"); b_cmf = Buf()
        S.dma("sync", lambda e: e.dma_start(out=cst[:], in_=tri_in[:, :]), writes=[b_cst])
        S.op("vector", lambda e: e.tensor_copy(out=TRIb[:], in_=cst[:]), reads=[b_cst], writes=[b_TRI])
        S.op("vector", lambda e: e.memset(NEGb[:], -1.0), writes=[b_NEG])
        S.dma("sync", lambda e: e.dma_start(out=cmf[:], in_=cm_in[:, :]), writes=[b_cmf])
        S.op("vector", lambda e: e.tensor_copy(out=CMb[:], in_=cmf[:]), reads=[b_cmf], writes=[b_CM])
    else:
        oh_in = C.din("OH", [32, GL], F32)
        ohb_in = C.din("OHB", [32, GL], F32)
        lma_in = C.din("LMA", [1, GL], F32)
        ngb_in = C.din("NGB", [1, GL], F32)
        rb_in = C.din("RB", [32, 8], F32)
        lam_in = C.din("lamv", [1, 256], F32)
        sg_in = C.din("sg", [128, 1], F32)
        frA = C.dscr("frA", [4, GL], BF16); b_frA = Buf()
        frB = C.dscr("frB", [4, GL], BF16); b_frB = Buf()
        ONESb = C.sb([128, 128], BF16, "ONESb"); b_ONES = Buf()
        S.op("vector", lambda e: e.memset(ONESb[:], 1.0), writes=[b_ONES])
        ones4 = C.sb([1, 4], F32, "ones4"); b_ones4 = Buf()
        S.op("vector", lambda e: e.memset(ones4[:], 1.0), writes=[b_ones4])
        RBs = C.sb([32, 8], F32, "RBs"); b_RB = Buf()
        S.dma("sync", lambda e: e.dma_start(out=RBs[:], in_=rb_in[:, :]), writes=[b_RB])
        FRA = C.sb([4, GL], BF16, "FRA"); b_FRA = Buf()
        FRB = C.sb([4, GL], BF16, "FRB"); b_FRB = Buf()
        ohc = [C.sb([32, 512], F32) for _ in range(2)]; b_ohc = [Buf(), Buf()]
        ohbc = [C.sb([32, 512], F32) for _ in range(2)]; b_ohbc = [Buf(), Buf()]
        lmc = [C.sb([1, 512], F32) for _ in range(2)]; b_lmc = [Buf(), Buf()]
        ngc = [C.sb([1, 512], F32) for _ in range(2)]; b_ngc = [Buf(), Buf()]
        for ch in range(GL // 512):
            i = ch % 2
            cs = slice(ch * 512, (ch + 1) * 512)
            S.dma("sync", lambda e, i=i, cs=cs: e.dma_start(out=ohc[i][:], in_=oh_in[:, cs]), writes=[b_ohc[i]])
            S.dma("sync", lambda e, i=i, cs=cs: e.dma_start(out=ohbc[i][:], in_=ohb_in[:, cs]), writes=[b_ohbc[i]])
            S.dma("sync", lambda e, i=i, cs=cs: e.dma_start(out=lmc[i][:], in_=lma_in[:, cs]), writes=[b_lmc[i]])
            S.dma("sync", lambda e, i=i, cs=cs: e.dma_start(out=ngc[i][:], in_=ngb_in[:, cs]), writes=[b_ngc[i]])
            S.op("tensor", lambda e, i=i: e.matmul(PSB[6][0:4, :], lhsT=RBs[:, 0:4], rhs=ohc[i][:], start=True, stop=False),
                 reads=[b_RB, b_ohc[i]], writes=[b_PSB[6]])
            S.op("tensor", lambda e, i=i: e.matmul(PSB[6][0:4, :], lhsT=ones4[0:1, 0:4], rhs=lmc[i][:], start=False, stop=True),
                 reads=[b_ones4, b_lmc[i]], writes=[b_PSB[6]])
            S.op("scalar", lambda e, cs=cs: e.activation(out=FRA[:, cs], in_=PSB[6][0:4, :], func=AF.Copy), reads=[b_PSB[6]], writes=[b_FRA])
            S.op("tensor", lambda e, i=i: e.matmul(PSB[7][0:4, :], lhsT=RBs[:, 4:8], rhs=ohbc[i][:], start=True, stop=False),
                 reads=[b_RB, b_ohbc[i]], writes=[b_PSB[7]])
            S.op("tensor", lambda e, i=i: e.matmul(PSB[7][0:4, :], lhsT=ones4[0:1, 0:4], rhs=ngc[i][:], start=False, stop=True),
                 reads=[b_ones4, b_ngc[i]], writes=[b_PSB[7]])
            S.op("scalar", lambda e, cs=cs: e.activation(out=FRB[:, cs], in_=PSB[7][0:4, :], func=AF.Copy), reads=[b_PSB[7]], writes=[b_FRB])
        S.dma("sync", lambda e: e.dma_start(out=frA[:, :], in_=FRA[:]), reads=[b_FRA], writes=[b_frA])
        S.dma("sync", lambda e: e.dma_start(out=frB[:, :], in_=FRB[:]), reads=[b_FRB], writes=[b_frB])
        GA = [C.sb([128, GW_A], BF16) for _ in range(4)]; b_GA = [Buf() for _ in range(4)]
        GB = [C.sb([128, GW_B], BF16) for _ in range(4)]; b_GB = [Buf() for _ in range(4)]
        for j in range(4):
            srcA = bass.AP(frA.tensor, frA.offset + j * GL, [[1, 128], [1, GW_A]])
            srcB = bass.AP(frB.tensor, frB.offset + j * GL, [[1, 128], [1, GW_B]])
            S.dma("sync", lambda e, j=j, srcA=srcA: e.dma_start(out=GA[j][:], in_=srcA), reads=[b_frA], writes=[b_GA[j]])
            S.dma("sync", lambda e, j=j, srcB=srcB: e.dma_start(out=GB[j][:], in_=srcB), reads=[b_frB], writes=[b_GB[j]])
        lam_sb = C.sb([128, 256], F32, "lam_sb"); b_lam = Buf()
        S.dma("sync", lambda e: e.dma_start(out=lam_sb[:], in_=lam_in[0:1, :].partition_broadcast(128)), writes=[b_lam])
        lt = C.sb([128, 2, 64], F32, "lt"); b_lt = Buf()
        ls = C.sb([128, 8], F32, "ls"); b_ls = Buf()
        S.op("vector", lambda e: e.tensor_tensor(out=lt[:, 0, :], in0=lam_sb[:, 0:64], in1=lam_sb[:, 64:128], op=ALU.mult), reads=[b_lam], writes=[b_lt])
        S.op("vector", lambda e: e.tensor_tensor(out=lt[:, 1, :], in0=lam_sb[:, 128:192], in1=lam_sb[:, 192:256], op=ALU.mult), reads=[b_lam], writes=[b_lt])
        S.op("scalar", lambda e: e.activation(out=lam_sb[:, 0:64], in_=lt[:, 0, :], func=AF.Copy, accum_out=ls[:, 0:1]), reads=[b_lt], writes=[b_lam, b_ls])
        S.op("scalar", lambda e: e.activation(out=lam_sb[:, 64:128], in_=lt[:, 1, :], func=AF.Copy, accum_out=ls[:, 1:2]), reads=[b_lt], writes=[b_lam, b_ls])
        S.op("scalar", lambda e: e.activation(out=ls[:, 2:4], in_=ls[:, 0:2], func=AF.Exp), reads=[b_ls], writes=[b_ls])
        S.op("vector", lambda e: e.tensor_tensor(out=ls[:, 4:5], in0=ls[:, 2:3], in1=ls[:, 3:4], op=ALU.subtract), reads=[b_ls], writes=[b_ls])
        S.op("vector", lambda e: e.tensor_scalar(out=ls[:, 5:6], in0=ls[:, 4:5], scalar1=float(lam_init), scalar2=-1.0, op0=ALU.add, op1=ALU.mult),
             reads=[b_ls], writes=[b_ls])
        neglam = ls[:, 5:6]
        sgs = C.sb([128, 2], F32, "sgs"); b_sg = Buf()
        S.dma("sync", lambda e: e.dma_start(out=sgs[:, 0:1], in_=sg_in[:, :]), writes=[b_sg])
        S.op("vector", lambda e: e.tensor_scalar(out=sgs[:, 1:2], in0=sgs[:, 0:1], scalar1=float(1.0 - lam_init), scalar2=None, op0=ALU.mult),
             reads=[b_sg], writes=[b_sg])
        gsc = sgs[:, 1:2]

    NSET = 2 if not even else 1
    QT = [C.sb([128, 4096], BF16) for _ in range(NSET)]
    KT = [C.sb([128, 4096], BF16) for _ in range(NSET)]
    SZ = [C.sb([128, 4096], BF16) for _ in range(NSET)]
    VV = [C.sb([128, 32, 128], BF16) for _ in range(NSET)]
    b_QT = [[Buf() for _ in range(8)] for _ in range(NSET)]
    b_KT = [[Buf() for _ in range(8)] for _ in range(NSET)]
    b_SZ = [[Buf() for _ in range(8)] for _ in range(NSET)]
    b_VV = [[Buf() for _ in range(8)] for _ in range(NSET)]
    Wh = [C.sb([128, 16, 512], BF16) for _ in range(2)]; b_Wh = [Buf(), Buf()]
    U = [C.sb([128, 16, 512], BF16) for _ in range(2)]; b_U = [Buf(), Buf()]
    wv = wc.rearrange("(k p) c -> p k c", p=128)
    uv = uT.rearrange("(k p) t -> p k t", p=128)
    MZo = [C.sb([128, 512], BF16) for _ in range(2)]; b_MZo = [Buf(), Buf()]
    state = {"u": 0, "pp": 0, "mzo": 0}

    def load_w(j):
        w = Wh[j % 2]
        for q in range(2):
            S.dma("gpsimd", lambda e, w=w, q=q, j=j: e.dma_start(out=w[:, q * 8:(q + 1) * 8, :], in_=wv[:, q * 8:(q + 1) * 8, j * 512:(j + 1) * 512]),
                  writes=[b_Wh[j % 2]])

    def proj(j, qscale):
        s = j % NSET
        w = Wh[j % 2]; bw = b_Wh[j % 2]
        for c in range(8):
            ui = state["u"] % 2
            state["u"] += 1
            u = U[ui]; bu = b_U[ui]
            for q in range(2):
                S.dma("sync", lambda e, u=u, q=q, c=c: e.dma_start(out=u[:, q * 8:(q + 1) * 8, :], in_=uv[:, q * 8:(q + 1) * 8, c * 512:(c + 1) * 512]),
                      writes=[bu])
            cs = slice(c * 512, (c + 1) * 512)
            for which in range(3):
                pi = 6 + state["pp"] % 2
                state["pp"] += 1
                p = PSB[pi]; bp = b_PSB[pi]
                col = (0, 128, 384)[which]
                for kk in range(16):
                    S.op("tensor", lambda e, p=p, w=w, u=u, kk=kk, col=col: e.matmul(p[:, :], lhsT=w[:, kk, col:col + 128], rhs=u[:, kk, :],
                                                                                      start=(kk == 0), stop=(kk == 15)),
                         reads=[bw, bu], writes=[bp])
                if which == 0:
                    S.op("scalar", lambda e, p=p, s=s, cs=cs: e.activation(out=QT[s][:, cs], in_=p[:, :], func=AF.Copy, scale=float(qscale)),
                         reads=[bp], writes=[b_QT[s][c]])
                elif which == 1:
                    S.op("vector", lambda e, p=p, s=s, cs=cs: e.tensor_copy(out=KT[s][:, cs], in_=p[:, :]), reads=[bp], writes=[b_KT[s][c]])
                else:
                    S.op("scalar", lambda e, p=p, s=s, cs=cs: e.activation(out=SZ[s][:, cs], in_=p[:, :], func=AF.Silu), reads=[bp], writes=[b_SZ[s][c]])
            pi = 6 + state["pp"] % 2
            state["pp"] += 1
            p = PSB[pi]; bp = b_PSB[pi]
            for sub in range(4):
                for kk in range(16):
                    S.op("tensor", lambda e, p=p, w=w, u=u, kk=kk, sub=sub: e.matmul(p[:, sub * 128:(sub + 1) * 128], lhsT=u[:, kk, sub * 128:(sub + 1) * 128],
                                                                                      rhs=w[:, kk, 256:384], start=(kk == 0), stop=(kk == 15)),
                         reads=[bw, bu], writes=[bp])
            S.op("vector", lambda e, p=p, s=s, c=c: e.tensor_copy(out=VV[s][:, 4 * c:4 * c + 4, :], in_=p[:, :].rearrange("p (a b) -> p a b", a=4)),
                 reads=[bp], writes=[b_VV[s][c]])

    def store_mz(j, qs, src_fn, reads):
        i = state["mzo"] % 2
        state["mzo"] += 1
        src_fn(MZo[i], b_MZo[i])
        S.dma("sync", lambda e, i=i: e.dma_start(out=mz_out[j * 128:(j + 1) * 128, qs * 512:(qs + 1) * 512], in_=MZo[i][:]),
              reads=[b_MZo[i]], final=True)

    if not even:
        E = [C.sb([128, 512], F32) for _ in range(2)]; b_E = [Buf(), Buf()]
        Lb = [C.sb([128, 512], BF16) for _ in range(3)]; b_Lb = [Buf() for _ in range(3)]
        LA = [C.sb([128, 512], BF16) for _ in range(3)]; b_LA = [Buf() for _ in range(3)]
        Ab = [C.sb([128, 512], BF16) for _ in range(3)]; b_Ab = [Buf() for _ in range(3)]

        def attn_odd(j):
            s = j % NSET
            units = [(qs, kb) for qs in range(NSB) for kb in range(4 * qs + 3, -1, -1)]
            n = len(units)
            lacc = {}

            def S0(u):
                qs, kb = units[u]
                z = PSB[u % 4]; bz = b_PSB[u % 4]
                diag = kb >= 4 * qs
                S.op("tensor", lambda e: e.matmul(z[:, :], lhsT=KT[s][:, kb * 128:(kb + 1) * 128], rhs=QT[s][:, qs * 512:(qs + 1) * 512],
                                                  start=True, stop=False),
                     reads=[b_KT[s][kb // 4], b_QT[s][qs]], writes=[bz])
                if diag:
                    x0 = 512 * qs - 128 * kb + 384
                    S.op("tensor", lambda e: e.matmul(z[:, :], lhsT=Jb[:], rhs=CMb[:, x0:x0 + 512], start=False, stop=False),
                         reads=[b_J, b_CM], writes=[bz])

            def S1(u):
                z = PSB[u % 4]; bz = b_PSB[u % 4]
                ee = E[u % 2]; be = b_E[u % 2]
                S.op("scalar", lambda e: e.activation(out=ee[:], in_=z[:, :], func=AF.Exp), reads=[bz], writes=[be])
                S.op("scalar", lambda e: e.activation(out=Lb[u % 3][:], in_=ee[:], func=AF.Ln, bias=1.0), reads=[be], writes=[b_Lb[u % 3]])

            def S23(u):
                qs, kb = units[u]
                z = PSB[u % 4]; bz = b_PSB[u % 4]
                first = kb == 4 * qs + 3
                last = kb == 0
                S.op("tensor", lambda e: e.matmul(z[:, :], lhsT=TRIb[:], rhs=Lb[u % 3][:], start=False, stop=first),
                     reads=[b_TRI, b_Lb[u % 3]], writes=[bz])
                if not first:
                    pa, pb = lacc[u - 1]
                    S.op("tensor", lambda e: e.matmul(z[:, :], lhsT=NEGb[:], rhs=pa[:], start=False, stop=True),
                         reads=[b_NEG, pb], writes=[bz])
                if not last:
                    if first:
                        lacc[u] = (Lb[u % 3], b_Lb[u % 3])
                    else:
                        pa, pb = lacc[u - 1]
                        S.op("vector", lambda e: e.tensor_tensor(out=LA[u % 3][:], in0=pa[:], in1=Lb[u % 3][:], op=ALU.add),
                             reads=[pb, b_Lb[u % 3]], writes=[b_LA[u % 3]])
                        lacc[u] = (LA[u % 3], b_LA[u % 3])
                lacc.pop(u - 2, None)

            def S45(u):
                qs, kb = units[u]
                z = PSB[u % 4]; bz = b_PSB[u % 4]
                first = kb == 4 * qs + 3
                last = kb == 0
                o = PSB[4 + qs % 2]; bo = b_PSB[4 + qs % 2]
                S.op("scalar", lambda e: e.activation(out=Ab[u % 3][:], in_=z[:, :], func=AF.Exp), reads=[bz], writes=[b_Ab[u % 3]])
                S.op("tensor", lambda e: e.matmul(o[:, :], lhsT=VV[s][:, kb, :], rhs=Ab[u % 3][:], start=first, stop=last),
                     reads=[b_VV[s][kb // 4], b_Ab[u % 3]], writes=[bo])
                if last:
                    def fin(dst, bdst):
                        S.op("vector", lambda e: e.tensor_tensor(out=dst[:], in0=o[:, :], in1=SZ[s][:, qs * 512:(qs + 1) * 512], op=ALU.mult),
                             reads=[bo, b_SZ[s][qs]], writes=[bdst])
                    store_mz(j, qs, fin, None)

            d1, d2 = PIPE_ODD
            for i in range(n + d2):
                if i < n:
                    S0(i)
                if 0 <= i - d1 < n:
                    S1(i - d1)
                    S23(i - d1)
                if 0 <= i - d2 < n:
                    S45(i - d2)

    else:
        Pb = [C.sb([128, 512], BF16) for _ in range(3)]; b_Pb = [Buf() for _ in range(3)]
        Rr = [C.sb([128, 512], F32) for _ in range(2)]; b_Rr = [Buf(), Buf()]
        On = [C.sb([128, 512], F32) for _ in range(3)]; b_On = [Buf() for _ in range(3)]
        Dd = C.sb([128, 512], F32, "Dd"); b_Dd = Buf()
        Dq = C.sb([128, 512], BF16, "Dq"); b_Dq = Buf()
        Sd = C.sb([128, 512], F32, "Sd"); b_Sd = Buf()
        cnt = {"u": 0, "acc": 0, "on": 0, "rr": 0}

        def softmax_sb(s, qs, rows, G, bG, is_A):
            if is_A:
                kbs = [kb for kb in range(4 * qs + 3, -1, -1) if 512 * qs - 128 * kb <= 2176]
            else:
                kbs = list(range(4 * qs + 3, -1, -1))
            ai = cnt["acc"] % 2
            cnt["acc"] += 1
            o = PSB[2 + 2 * ai]; bo = b_PSB[2 + 2 * ai]
            l = PSB[3 + 2 * ai]; bl = b_PSB[3 + 2 * ai]
            n = len(kbs)
            ids = []

            def S0(i):
                kb = kbs[i]
                u = cnt["u"]; cnt["u"] += 1
                ids.append(u)
                sp = PSB[u % 2]; bs = b_PSB[u % 2]
                D = 512 * qs - 128 * kb
                biased = is_A or D <= 1536
                S.op("tensor", lambda e: e.matmul(sp[:, :], lhsT=KT[s][rows, kb * 128:(kb + 1) * 128], rhs=QT[s][rows, qs * 512:(qs + 1) * 512],
                                                  start=True, stop=not biased),
                     reads=[b_KT[s][kb // 4], b_QT[s][qs]], writes=[bs])
                if biased:
                    x0 = D + 384
                    S.op("tensor", lambda e: e.matmul(sp[:, :], lhsT=Jb[:], rhs=G[:, x0:x0 + 512], start=False, stop=True),
                         reads=[b_J, bG], writes=[bs])

            def S1(i):
                u = ids[i]
                sp = PSB[u % 2]; bs = b_PSB[u % 2]
                S.op("scalar", lambda e: e.activation(out=Pb[u % 3][:], in_=sp[:, :], func=AF.Exp), reads=[bs], writes=[b_Pb[u % 3]])

            def S2(i):
                u = ids[i]
                kb = kbs[i]
                S.op("tensor", lambda e: e.matmul(o[:, :], lhsT=VV[s][:, kb, :], rhs=Pb[u % 3][:], start=(i == 0), stop=(i == n - 1)),
                     reads=[b_VV[s][kb // 4], b_Pb[u % 3]], writes=[bo])
                S.op("tensor", lambda e: e.matmul(l[:, :], lhsT=ONESb[:], rhs=Pb[u % 3][:], start=(i == 0), stop=(i == n - 1)),
                     reads=[b_ONES, b_Pb[u % 3]], writes=[bl])

            d1, d2 = PIPE_EVEN
            for i in range(n + d2):
                if i < n:
                    S0(i)
                if 0 <= i - d1 < n:
                    S1(i - d1)
                if 0 <= i - d2 < n:
                    S2(i - d2)
            ri = cnt["rr"] % 2; cnt["rr"] += 1
            oi = cnt["on"] % 3; cnt["on"] += 1
            S.op("vector", lambda e: e.reciprocal(out=Rr[ri][:], in_=l[:, :]), reads=[bl], writes=[b_Rr[ri]])
            S.op("vector", lambda e: e.tensor_tensor(out=On[oi][:], in0=o[:, :], in1=Rr[ri][:], op=ALU.mult), reads=[bo, b_Rr[ri]], writes=[b_On[oi]])
            return On[oi], b_On[oi]

        def attn_A(j):
            s = j % NSET
            for qs in range(NSB):
                on, bon = softmax_sb(s, qs, slice(0, 128), GA[j], b_GA[j], True)

                def fin(dst, bdst, on=on, bon=bon, qs=qs):
                    S.op("vector", lambda e: e.tensor_tensor(out=dst[:], in0=on[:], in1=SZ[s][:, qs * 512:(qs + 1) * 512], op=ALU.mult),
                         reads=[bon, b_SZ[s][qs]], writes=[bdst])
                store_mz(j, qs, fin, None)

        def attn_B(j):
            s = j % NSET
            jb = j - 4
            for qs in range(NSB):
                o1, bo1 = softmax_sb(s, qs, slice(0, 64), GB[jb], b_GB[jb], False)
                o2, bo2 = softmax_sb(s, qs, slice(64, 128), GB[jb], b_GB[jb], False)
                S.op("vector", lambda e, o1=o1, o2=o2: e.scalar_tensor_tensor(out=Dd[:], in0=o2[:], scalar=neglam, in1=o1[:], op0=ALU.mult, op1=ALU.add),
                     reads=[bo1, bo2, b_ls], writes=[b_Dd])
                S.op("vector", lambda e: e.tensor_tensor(out=Dq[:], in0=Dd[:], in1=Dd[:], op=ALU.mult), reads=[b_Dd], writes=[b_Dq])
                S.op("tensor", lambda e: e.matmul(PSB[6][:, :], lhsT=ONESb[:], rhs=Dq[:], start=True, stop=True), reads=[b_ONES, b_Dq], writes=[b_PSB[6]])
                S.op("scalar", lambda e: e.activation(out=Sd[:], in_=PSB[6][:, :], func=AF.Sqrt, scale=1.0 / 128, bias=EPS), reads=[b_PSB[6]], writes=[b_Sd])
                S.op("vector", lambda e: e.reciprocal(out=Sd[:], in_=Sd[:]), reads=[b_Sd], writes=[b_Sd])
                S.op("vector", lambda e: e.tensor_tensor(out=Dd[:], in0=Dd[:], in1=Sd[:], op=ALU.mult), reads=[b_Dd, b_Sd], writes=[b_Dd])

                def fin(dst, bdst, qs=qs):
                    S.op("vector", lambda e: e.scalar_tensor_tensor(out=dst[:], in0=Dd[:], scalar=gsc, in1=SZ[s][:, qs * 512:(qs + 1) * 512],
                                                                     op0=ALU.mult, op1=ALU.mult),
                         reads=[b_Dd, b_sg, b_SZ[s][qs]], writes=[bdst])
                store_mz(j, qs, fin, None)

    load_w(0)
    for j in range(8):
        if j + 1 < 8:
            load_w(j + 1)
        if not even:
            proj(j, 128 ** -0.5)
            attn_odd(j)
        elif j < 4:
            proj(j, 128 ** -0.5)
            attn_A(j)
        else:
            proj(j, 64 ** -0.5)
            attn_B(j)
    return C.finish()


def _t5_bucket_np(dist):
    dist = np.asarray(dist, dtype=np.int64)
    d = np.maximum(dist, 1).astype(np.float32)
    ratio = (np.log(d / np.float32(16.0)) / np.float32(math.log(2048 / 16)) * np.float32(16.0)).astype(np.float32)
    large = 16 + ratio.astype(np.int32)
    large = np.minimum(large, 31)
    return np.where(dist < 16, dist, large).astype(np.int64)


_CONST_CACHE = {}


def host_consts():
    if _CONST_CACHE:
        return _CONST_CACHE
    J = np.zeros((128, 128), np.float32)
    J[np.arange(128), 127 - np.arange(128)] = 1.0
    jj, ss = np.meshgrid(np.arange(128), np.arange(128), indexing="ij")
    TRI = np.where(jj >= ss, -1.0, 0.0).astype(np.float32)
    kk, xx = np.meshgrid(np.arange(128), np.arange(896), indexing="ij")
    CM = np.where(xx + kk - 511 <= 0, NEG, 0.0).astype(np.float32)
    idx = np.arange(GL)
    delta = idx - 511
    valid = delta >= 0
    bk = _t5_bucket_np(np.maximum(delta, 0))
    OH = np.zeros((32, GL), np.float32)
    OH[bk[valid], idx[valid]] = 1.0
    OHB = OH.copy()
    OHB[31, valid] -= 1.0
    mult = ((delta <= 128).astype(np.int64) + ((delta % 4 == 0) & (delta <= 512)).astype(np.int64)
            + ((delta % 16 == 0) & (delta <= 2048)).astype(np.int64))
    mult = np.where(valid, mult, 0)
    LMA = np.where(mult > 0, np.log(np.maximum(mult, 1).astype(np.float64)), NEG).astype(np.float32).reshape(1, GL)
    NGB = np.where(valid, 0.0, NEG).astype(np.float32).reshape(1, GL)
    _CONST_CACHE.update(J=J, TRI=TRI, CM=CM, OH=OH, OHB=OHB, LMA=LMA, NGB=NGB)
    return _CONST_CACHE


def head_cols(g, j, even):
    if not even:
        h = 8 * g + j
        return [(0, h * 128, 128), (128, 4096 + h * 128, 128), (256, 8192 + h * 128, 128), (384, 12288 + h * 128, 128)]
    if j < 4:
        h = 4 * g + j
        return [(0, h * 128, 128), (128, 2048 + h * 128, 128), (256, 4096 + h * 128, 128), (384, 12288 + h * 128, 128)]
    h = 4 * g + j - 4
    return [(0, 6144 + h * 64, 64), (64, 7168 + h * 64, 64), (128, 8192 + h * 64, 64), (192, 9216 + h * 64, 64),
            (256, 10240 + h * 128, 128), (384, 12288 + 2048 + h * 128, 128)]


def make_wc(w_in_l, g, even):
    wc = np.empty((2048, 4096), np.float32)
    for j in range(8):
        for d, s, w in head_cols(g, j, even):
            wc[:, j * 512 + d:j * 512 + d + w] = w_in_l[:, s:s + w]
    return wc


def inner_row(g, j, even):
    if not even:
        return (8 * g + j) * 128
    if j < 4:
        return (4 * g + j) * 128
    return 2048 + (4 * g + j - 4) * 128


U8 = mybir.dt.uint8
ARENA_KB = 200
GROUPS4 = [[0, 1, 2, 3], [4, 5, 6, 7]]


def _dtsize(dt):
    return 4 if dt == F32 else 2


class Arena:
    def __init__(self, C):
        self.t = C.sb([128, ARENA_KB * 1024], U8, "arena")
        self.base = 0
        self.off = 0

    h_free = False
    hoff = 0
    H_BYTES = 8 * 2048 * 4

    def alloc(self, shape, dt):
        n = _dtsize(dt)
        for d in shape[1:]:
            n *= d
        n = (n + 63) // 64 * 64
        if self.h_free and self.hoff + n <= self.H_BYTES:
            v = self.t[:, self.hoff:self.hoff + n].bitcast(dt)
            self.hoff += n
        else:
            assert self.off + n <= ARENA_KB * 1024, ("arena overflow", self.off, n)
            v = self.t[:, self.off:self.off + n].bitcast(dt)
            self.off += n
        nel = 1
        for d in shape[1:]:
            nel *= d
        v = v[:, 0:nel]
        if len(shape) == 3:
            v = v.rearrange("p (a b) -> p a b", a=shape[1])
        if shape[0] < 128:
            v = v[0:shape[0]]
        return v

    def persist(self):
        self.base = self.off

    def reset(self):
        self.off = self.base


def build_fused(nphase=None, dbg=False):
    C = Ctx()
    nc, S = C.nc, C.S
    A = Arena(C)
    hc_names = {}
    x_in = C.din("x", [1024, 2048], F32)
    cT = C.din("cT", [128, 16], F32)
    wm = C.din("wm", [2048, 6144], F32)
    bm = C.din("bm", [1, 6144], F32)
    wc = C.din("wc", [DEPTH, 2048, 4096], F32) if (nphase is None or nphase >= 4) else None
    wout = C.din("wout", [DEPTH, 4096, 2048], F32) if (nphase is None or nphase >= 5) else None
    gn = C.din("gn", [DEPTH, 2048], F32)
    gf = C.din("gf", [1, 2048], F32)
    id_in = C.din("ident", [128, 128], F32)
    j_in = C.din("J", [128, 128], F32)
    tri_in = C.din("TRI", [128, 128], F32)
    cm_in = C.din("CM", [128, 896], F32)
    oh_in = C.din("OH", [32, GL], F32)
    ohb_in = C.din("OHB", [32, GL], F32)
    lma_in = C.din("LMA", [1, GL], F32)
    ngb_in = C.din("NGB", [1, GL], F32)
    rb_in = C.din("RB", [32, 8], F32)
    lam_in = C.din("lamv", [2, 256], F32)
    sg_in = C.din("sg", [2, 128, 1], F32)
    y_out = C.dout("y", [1024, 2048], F32)

    mod_loc = C.dscr("mod_loc", [1, 6144], F32); b_modloc = Buf()
    mod_all = C.dscr("mod_all", [4, 6144], F32); b_modall = Buf()
    uT_loc = C.dscr("uT_loc", [2048, 1024], BF16); b_uTloc = [Buf() for _ in range(4)]
    uT_all = C.dscr("uT_all", [4 * 2048, 1024], BF16); b_uTall = [Buf() for _ in range(4)]
    mz_loc = C.dscr("mz_loc", [1024, 4096], BF16); b_mzloc = [Buf() for _ in range(8)]
    mz_all = C.dscr("mz_all", [4 * 1024, 4096], BF16); b_mzall = Buf()
    frA = C.dscr("frA", [4, GL], BF16); b_frA = Buf()
    frB = C.dscr("frB", [4, GL], BF16); b_frB = Buf()
    h_spill = C.dscr("h_spill", [1024, 2048], F32); b_hsp = Buf()
    spill = {"on": False}

    PSB = [C.ps([128, 512], F32) for _ in range(8)]
    b_PSB = [Buf() for _ in range(8)]

    H = A.alloc([128, 8, 2048], F32); b_H = [Buf() for _ in range(8)]
    ident = A.alloc([128, 128], BF16); b_id = Buf()
    Jb = A.alloc([128, 128], BF16); b_J = Buf()
    TRIb = A.alloc([128, 128], BF16); b_TRI = Buf()
    NEGb = A.alloc([128, 128], BF16); b_NEG = Buf()
    ONESb = A.alloc([128, 128], BF16); b_ONES = Buf()
    CMb = A.alloc([128, 896], BF16); b_CM = Buf()
    A.persist()
    S.dma("gpsimd", lambda e: e.dma_start(out=ident, in_=id_in[:, :]), writes=[b_id])
    S.dma("gpsimd", lambda e: e.dma_start(out=Jb, in_=j_in[:, :]), writes=[b_J])
    S.dma("gpsimd", lambda e: e.dma_start(out=TRIb, in_=tri_in[:, :]), writes=[b_TRI])
    S.dma("gpsimd", lambda e: e.dma_start(out=CMb, in_=cm_in[:, :]), writes=[b_CM])
    S.op("vector", lambda e: e.memset(NEGb, -1.0), writes=[b_NEG])
    S.op("vector", lambda e: e.memset(ONESb, 1.0), writes=[b_ONES])
    hv = x_in.rearrange("(t p) f -> p t f", p=128)
    for t in range(8):
        S.dma("sync", lambda e, t=t: e.dma_start(out=H[:, t, :], in_=hv[:, t, :]), writes=[b_H[t]])

    def phase_M():
        c_sb = A.alloc([128, 16], F32); b_c = Buf()
        e_sb = A.alloc([128, 16], F32); b_e = Buf()
        ca = A.alloc([128, 16], F32); b_ca = Buf()
        ones = A.alloc([128, 128], F32); b_ones = Buf()
        L = A.alloc([128, 16, 128], F32); b_L = Buf()
        bmr = A.alloc([1, 6144], F32); b_bmr = Buf()
        W = [A.alloc([128, 16, 512], F32) for _ in range(2)]; b_W = [Buf(), Buf()]
        res = A.alloc([1, 6144], F32); b_res = Buf()
        S.dma("sync", lambda e: e.dma_start(out=c_sb, in_=cT[:, :]), writes=[b_c])
        S.dma("sync", lambda e: e.dma_start(out=bmr, in_=bm[:, :]), writes=[b_bmr])
        S.op("vector", lambda e: e.memset(ones, 1.0), writes=[b_ones])
        S.op("scalar", lambda e: e.activation(out=e_sb, in_=c_sb, func=AF.Exp, scale=-1.0), reads=[b_c], writes=[b_e])
        S.op("vector", lambda e: e.tensor_scalar(out=e_sb, in0=e_sb, scalar1=1.0, scalar2=None, op0=ALU.add), reads=[b_e], writes=[b_e])
        S.op("vector", lambda e: e.reciprocal(out=e_sb, in_=e_sb), reads=[b_e], writes=[b_e])
        S.op("vector", lambda e: e.tensor_tensor(out=ca, in0=c_sb, in1=e_sb, op=ALU.mult), reads=[b_c, b_e], writes=[b_ca])
        for k in range(16):
            S.op("vector", lambda e, k=k: e.tensor_scalar(out=L[:, k, :], in0=ones, scalar1=ca[:, k:k + 1], scalar2=None, op0=ALU.mult),
                 reads=[b_ones, b_ca], writes=[b_L])
        wv = wm.rearrange("(k p) c -> p k c", p=128)
        for n in range(12):
            w = W[n % 2]; bw = b_W[n % 2]; p = PSB[n % 2]; bp = b_PSB[n % 2]
            S.dma("sync", lambda e, w=w, n=n: e.dma_start(out=w, in_=wv[:, :, n * 512:(n + 1) * 512]), writes=[bw])
            for k in range(16):
                S.op("tensor", lambda e, w=w, p=p, k=k: e.matmul(p[0:1, :], lhsT=L[:, k, 0:1], rhs=w[:, k, :], start=(k == 0), stop=False),
                     reads=[b_L, bw], writes=[bp])
            S.op("tensor", lambda e, p=p, n=n: e.matmul(p[0:1, :], lhsT=ones[0:1, 0:1], rhs=bmr[0:1, n * 512:(n + 1) * 512], start=False, stop=True),
                 reads=[b_ones, b_bmr], writes=[bp])
            S.op("scalar", lambda e, p=p, n=n: e.activation(out=res[0:1, n * 512:(n + 1) * 512], in_=p[0:1, :], func=AF.Copy), reads=[bp], writes=[b_res])
        S.dma("sync", lambda e: e.dma_start(out=mod_loc[:, :], in_=res), reads=[b_res], writes=[b_modloc])
        S.cc(lambda e: e.collective_compute("AllGather", ALU.bypass, replica_groups=GROUPS4, ins=[mod_loc[:, :]], outs=[mod_all[:, :]]),
             reads=[b_modloc], writes=[b_modall])

    def phase_G():
        ones4 = A.alloc([1, 4], F32); b_ones4 = Buf()
        S.op("vector", lambda e: e.memset(ones4, 1.0), writes=[b_ones4])
        RBs = A.alloc([32, 8], F32); b_RB = Buf()
        S.dma("sync", lambda e: e.dma_start(out=RBs, in_=rb_in[:, :]), writes=[b_RB])
        FRA = A.alloc([4, GL], BF16); b_FRA = Buf()
        FRB = A.alloc([4, GL], BF16); b_FRB = Buf()
        ohc = [A.alloc([32, 512], F32) for _ in range(2)]; b_ohc = [Buf(), Buf()]
        ohbc = [A.alloc([32, 512], F32) for _ in range(2)]; b_ohbc = [Buf(), Buf()]
        lmc = [A.alloc([1, 512], F32) for _ in range(2)]; b_lmc = [Buf(), Buf()]
        ngc = [A.alloc([1, 512], F32) for _ in range(2)]; b_ngc = [Buf(), Buf()]
        for ch in range(GL // 512):
            i = ch % 2
            cs = slice(ch * 512, (ch + 1) * 512)
            S.dma("sync", lambda e, i=i, cs=cs: e.dma_start(out=ohc[i], in_=oh_in[:, cs]), writes=[b_ohc[i]])
            S.dma("sync", lambda e, i=i, cs=cs: e.dma_start(out=ohbc[i], in_=ohb_in[:, cs]), writes=[b_ohbc[i]])
            S.dma("sync", lambda e, i=i, cs=cs: e.dma_start(out=lmc[i], in_=lma_in[:, cs]), writes=[b_lmc[i]])
            S.dma("sync", lambda e, i=i, cs=cs: e.dma_start(out=ngc[i], in_=ngb_in[:, cs]), writes=[b_ngc[i]])
            S.op("tensor", lambda e, i=i: e.matmul(PSB[6][0:4, :], lhsT=RBs[:, 0:4], rhs=ohc[i], start=True, stop=False),
                 reads=[b_RB, b_ohc[i]], writes=[b_PSB[6]])
            S.op("tensor", lambda e, i=i: e.matmul(PSB[6][0:4, :], lhsT=ones4[0:1, 0:4], rhs=lmc[i], start=False, stop=True),
                 reads=[b_ones4, b_lmc[i]], writes=[b_PSB[6]])
            S.op("scalar", lambda e, cs=cs: e.activation(out=FRA[:, cs], in_=PSB[6][0:4, :], func=AF.Copy), reads=[b_PSB[6]], writes=[b_FRA])
            S.op("tensor", lambda e, i=i: e.matmul(PSB[7][0:4, :], lhsT=RBs[:, 4:8], rhs=ohbc[i], start=True, stop=False),
                 reads=[b_RB, b_ohbc[i]], writes=[b_PSB[7]])
            S.op("tensor", lambda e, i=i: e.matmul(PSB[7][0:4, :], lhsT=ones4[0:1, 0:4], rhs=ngc[i], start=False, stop=True),
                 reads=[b_ones4, b_ngc[i]], writes=[b_PSB[7]])
            S.op("scalar", lambda e, cs=cs: e.activation(out=FRB[:, cs], in_=PSB[7][0:4, :], func=AF.Copy), reads=[b_PSB[7]], writes=[b_FRB])
        S.dma("sync", lambda e: e.dma_start(out=frA[:, :], in_=FRA), reads=[b_FRA], writes=[b_frA])
        S.dma("sync", lambda e: e.dma_start(out=frB[:, :], in_=FRB), reads=[b_FRB], writes=[b_frB])

    def phase_L31(l_prev, l_next, final):
        do_outproj = l_prev is not None
        do_norm = l_next is not None
        if spill["on"]:
            spill["on"] = False
            hsv = h_spill.rearrange("(t p) f -> p t f", p=128)
            for t in range(8):
                S.dma("sync", lambda e, t=t: e.dma_start(out=H[:, t, :], in_=hsv[:, t, :]), reads=[b_hsp], writes=[b_H[t]])
        if do_outproj:
            even = l_prev % 2 == 0
            MZ = A.alloc([128, 32, 1024], BF16); b_MZ = Buf()
            W = [A.alloc([128, 16, 512], BF16) for _ in range(2)]; b_W = [Buf(), Buf()]
            gate = A.alloc([128, 2048], F32); b_gate = Buf()
            tmp = [A.alloc([128, 512], F32) for _ in range(2)]; b_tmp = [Buf(), Buf()]
            mzv = mz_all.rearrange("(k p) t -> p k t", p=128)
            for q in range(4):
                S.dma("sync", lambda e, q=q: e.dma_start(out=MZ[:, q * 8:(q + 1) * 8, :],
                                                          in_=mzv[:, q * 8:(q + 1) * 8, bass.ds((S.rt["pid"] % 4) * 1024, 1024)]),
                      reads=[b_mzall], writes=[b_MZ])
            S.dma("sync", lambda e: e.dma_start(out=gate, in_=mod_all[l_prev:l_prev + 1, 4096:6144].partition_broadcast(128)),
                  reads=[b_modall], writes=[b_gate])
            if not even:
                wsrc = [wout[l_prev].rearrange("(g j p) c -> p j g c", g=4, j=8)[:, j] for j in range(8)]
            else:
                wa = wout[l_prev][0:2048, :].rearrange("(g j p) c -> p j g c", g=4, j=4)
                wb = wout[l_prev][2048:4096, :].rearrange("(g j p) c -> p j g c", g=4, j=4)
                wsrc = [wa[:, j] for j in range(4)] + [wb[:, j] for j in range(4)]

            def load_wout(idx):
                f, hf = idx // 2, idx % 2
                w = W[idx % 2]; bw = b_W[idx % 2]
                fs = slice(f * 512, (f + 1) * 512)
                for j in range(4):
                    S.dma("gpsimd", lambda e, w=w, fs=fs, j=j, hf=hf: e.dma_start(out=w[:, j * 4:(j + 1) * 4, :], in_=wsrc[hf * 4 + j][:, :, fs]), writes=[bw])

            it = 0
            load_wout(0)
            for idx in range(8):
                f, hf = idx // 2, idx % 2
                w = W[idx % 2]; bw = b_W[idx % 2]
                fs = slice(f * 512, (f + 1) * 512)
                if idx + 1 < 8:
                    load_wout(idx + 1)
                for t in range(8):
                    p = PSB[t]; bp = b_PSB[t]
                    for kl in range(16):
                        kk = hf * 16 + kl
                        S.op("tensor", lambda e, p=p, w=w, kl=kl, kk=kk, t=t, hf=hf: e.matmul(p[:, :], lhsT=MZ[:, kk, t * 128:(t + 1) * 128], rhs=w[:, kl, :],
                                                                                                start=(hf == 0 and kl == 0), stop=(hf == 1 and kl == 15)),
                             reads=[b_MZ, bw], writes=[bp])
                if hf == 1:
                    for t in range(8):
                        p = PSB[t]; bp = b_PSB[t]; tm = tmp[it % 2]; btm = b_tmp[it % 2]
                        it += 1
                        S.op("vector", lambda e, p=p, tm=tm, fs=fs: e.tensor_tensor(out=tm, in0=p[:, :], in1=gate[:, fs], op=ALU.mult),
                             reads=[bp, b_gate], writes=[btm])
                        S.op("vector", lambda e, tm=tm, t=t, fs=fs: e.tensor_tensor(out=H[:, t, fs], in0=H[:, t, fs], in1=tm, op=ALU.add),
                             reads=[btm, b_H[t]], writes=[b_H[t]])
            sq = W[0][:, 0:4, :].rearrange("p a b -> p (a b)"); b_sq = b_W[0]
            ub = [W[1][:, 0:4, :].rearrange("p a b -> p (a b)"), W[1][:, 4:8, :].rearrange("p a b -> p (a b)")]
            b_ub = [b_W[1], b_W[1]]
            uf = gate; b_uf = b_gate
            UT = MZ[:, 0:16, :]; b_UT = b_MZ
        else:
            sq = A.alloc([128, 2048], BF16); b_sq = Buf()
            ub = [A.alloc([128, 2048], BF16) for _ in range(2)]; b_ub = [Buf(), Buf()]
            uf = A.alloc([128, 2048], F32); b_uf = Buf()
            UT = A.alloc([128, 16, 1024], BF16); b_UT = Buf()
        st = A.alloc([128, 8, 4], F32); b_st = [Buf() for _ in range(8)]
        gs = A.alloc([128, 2048], F32); b_gs = Buf()
        if do_norm:
            shift = A.alloc([128, 2048], F32); b_shift = Buf()
            S.dma("sync", lambda e: e.dma_start(out=shift, in_=mod_all[l_next:l_next + 1, 0:2048].partition_broadcast(128)), reads=[b_modall], writes=[b_shift])
            S.dma("sync", lambda e: e.dma_start(out=gs, in_=mod_all[l_next:l_next + 1, 2048:4096].partition_broadcast(128)), reads=[b_modall], writes=[b_gs])
            S.dma("sync", lambda e: e.dma_start(out=uf, in_=gn[l_next:l_next + 1, :].partition_broadcast(128)), writes=[b_uf])
            S.op("vector", lambda e: e.scalar_tensor_tensor(out=gs, in0=gs, scalar=1.0, in1=uf, op0=ALU.add, op1=ALU.mult),
                 reads=[b_gs, b_uf], writes=[b_gs])
        else:
            S.dma("sync", lambda e: e.dma_start(out=gs, in_=gf[0:1, :].partition_broadcast(128)), writes=[b_gs])
            yo = [uf, A.alloc([128, 2048], F32)]; b_yo = [b_uf, Buf()]
            yv = y_out.rearrange("(t p) f -> p t f", p=128)
        nt = 0
        for t in range(8):
            S.op("scalar", lambda e, t=t: e.activation(out=sq, in_=H[:, t, :], func=AF.Square, accum_out=st[:, t, 0:1]),
                 reads=[b_H[t]], writes=[b_sq, b_st[t]])
            S.op("scalar", lambda e, t=t: e.activation(out=st[:, t, 1:2], in_=st[:, t, 0:1], func=AF.Sqrt, scale=1.0 / 2048, bias=EPS),
                 reads=[b_st[t]], writes=[b_st[t]])
            S.op("vector", lambda e, t=t: e.reciprocal(out=st[:, t, 2:3], in_=st[:, t, 1:2]), reads=[b_st[t]], writes=[b_st[t]])
            if do_norm:
                u = ub[t % 2]; bu = b_ub[t % 2]
                S.op("vector", lambda e, t=t: e.scalar_tensor_tensor(out=uf, in0=H[:, t, :], scalar=st[:, t, 2:3], in1=gs, op0=ALU.mult, op1=ALU.mult),
                     reads=[b_H[t], b_st[t], b_gs], writes=[b_uf])
                S.op("gpsimd", lambda e, u=u: e.tensor_tensor(out=u, in0=uf, in1=shift, op=ALU.add), reads=[b_uf, b_shift], writes=[bu])
                for kq in range(4):
                    pi = 4 + nt % 2
                    nt += 1
                    pt = PSB[pi][:, 0:256].bitcast(BF16); bpt = b_PSB[pi]
                    for j in range(4):
                        k = kq * 4 + j
                        S.op("tensor", lambda e, pt=pt, u=u, k=k, j=j: e.transpose(out=pt[:, j * 128:(j + 1) * 128], in_=u[:, k * 128:(k + 1) * 128], identity=ident),
                             reads=[bu, b_id], writes=[bpt])
                    S.op("scalar", lambda e, pt=pt, kq=kq, t=t: e.activation(out=UT[:, kq * 4:(kq + 1) * 4, t * 128:(t + 1) * 128],
                                                                              in_=pt.rearrange("p (a b) -> p a b", a=4), func=AF.Copy),
                         reads=[bpt], writes=[b_UT])
            else:
                y = yo[t % 2]; by = b_yo[t % 2]
                S.op("vector", lambda e, t=t, y=y: e.scalar_tensor_tensor(out=y, in0=H[:, t, :], scalar=st[:, t, 2:3], in1=gs, op0=ALU.mult, op1=ALU.mult),
                     reads=[b_H[t], b_st[t], b_gs], writes=[by])
                S.dma("sync", lambda e, t=t, y=y: e.dma_start(out=yv[:, t, :], in_=y), reads=[by], final=True)
        if do_norm:
            uv = uT_loc.rearrange("(k p) t -> p k t", p=128)
            for q in range(4):
                S.dma("sync", lambda e, q=q: e.dma_start(out=uv[:, q * 4:(q + 1) * 4, :], in_=UT[:, q * 4:(q + 1) * 4, :]), reads=[b_UT], writes=[b_uTloc[q]])
                S.cc(lambda e, q=q: e.collective_compute("AllGather", ALU.bypass, replica_groups=GROUPS4, ins=[uT_loc[q * 512:(q + 1) * 512, :]],
                                                         outs=[uT_all[q * 2048:(q + 1) * 2048, :]]),
                     reads=[b_uTloc[q]], writes=[b_uTall[q]], serialize=(q == 0))
            for q in range(4):
                b_uTall[q].last_w = ("cc", S.count["cc"])

    def phase_L2(l):
        even = l % 2 == 0
        lam_init = 0.8 - 0.6 * math.exp(-0.3 * l)
        NSET = 2
        CW = 512 if even else 256
        if even:
            hsv = h_spill.rearrange("(t p) f -> p t f", p=128)
            for t in range(8):
                S.dma("sync", lambda e, t=t: e.dma_start(out=hsv[:, t, :], in_=H[:, t, :]), reads=[b_H[t]], writes=[b_hsp])
            S.barrier()
            spill["on"] = True
            A.h_free = True
            A.hoff = 0
        QTs = [A.alloc([128, 4096], BF16) for _ in range(NSET)]; KTs = [A.alloc([128, 4096], BF16) for _ in range(NSET)]
        SZs = [A.alloc([128, 4096], BF16) for _ in range(NSET)]; VVs = [A.alloc([128, 32, 128], BF16) for _ in range(NSET)]
        b_QTs = [[Buf() for _ in range(8)] for _ in range(NSET)]; b_KTs = [[Buf() for _ in range(8)] for _ in range(NSET)]
        b_SZs = [[Buf() for _ in range(8)] for _ in range(NSET)]; b_VVs = [[Buf() for _ in range(8)] for _ in range(NSET)]
        QT, KT, SZ, VV = QTs[0], KTs[0], SZs[0], VVs[0]
        b_QT, b_KT, b_SZ, b_VV = b_QTs[0], b_KTs[0], b_SZs[0], b_VVs[0]
        Wh = [A.alloc([128, 16, 512], BF16) for _ in range(2)]; b_Wh = [Buf(), Buf()]
        U = [A.alloc([128, 16, CW], BF16) for _ in range(2)]; b_U = [Buf(), Buf()]
        MZo = [A.alloc([128, 512], BF16) for _ in range(2)]; b_MZo = [Buf(), Buf()]
        wv = wc[l].rearrange("(k p) c -> p k c", p=128)
        state = {"u": 0, "pp": 0, "mzo": 0}

        def load_w(j):
            w = Wh[j % 2]
            for q in range(2):
                S.dma("gpsimd", lambda e, w=w, q=q, j=j: e.dma_start(out=w[:, q * 8:(q + 1) * 8, :], in_=wv[:, q * 8:(q + 1) * 8, j * 512:(j + 1) * 512]),
                      writes=[b_Wh[j % 2]])

        def proj_items(j, qscale):
            s = j % NSET
            w = Wh[j % 2]; bw = b_Wh[j % 2]
            items = []
            NCH = 4096 // CW
            ubuf = {}

            def load_u(c):
                ui = state["u"] % 2
                state["u"] += 1
                u = U[ui]; bu = b_U[ui]
                ubuf[c] = (u, bu)
                r = (c * CW) // 1024
                co = (c * CW) % 1024
                for q in range(4):
                    src = uT_all[q * 2048 + r * 512:q * 2048 + (r + 1) * 512, :].rearrange("(k p) t -> p k t", p=128)
                    S.dma("sync", lambda e, u=u, q=q, src=src, co=co: e.dma_start(out=u[:, q * 4:(q + 1) * 4, :], in_=src[:, :, co:co + CW]),
                          reads=[b_uTall[q]], writes=[bu])

            items.append(lambda: load_u(0))
            for c in range(NCH):
                if c + 1 < NCH:
                    items.append(lambda c=c: load_u(c + 1))
                cs = slice(c * CW, (c + 1) * CW)
                c8 = (c * CW) // 512
                for which in range(3):
                    pi = 6 + state["pp"] % 2
                    state["pp"] += 1
                    col = (0, 128, 384)[which]
                    for k4 in range(4):
                        def it(c=c, pi=pi, col=col, k4=k4, which=which, cs=cs, c8=c8):
                            u, bu = ubuf[c]
                            p = PSB[pi]; bp = b_PSB[pi]
                            for kk in range(4 * k4, 4 * k4 + 4):
                                S.op("tensor", lambda e, kk=kk: e.matmul(p[:, 0:CW], lhsT=w[:, kk, col:col + 128], rhs=u[:, kk, :],
                                                                          start=(kk == 0), stop=(kk == 15)),
                                     reads=[bw, bu], writes=[bp])
                            if k4 == 3:
                                if which == 0:
                                    S.op("scalar", lambda e: e.activation(out=QTs[s][:, cs], in_=p[:, 0:CW], func=AF.Copy, scale=float(qscale)),
                                         reads=[bp], writes=[b_QTs[s][c8]])
                                elif which == 1:
                                    S.op("vector", lambda e: e.tensor_copy(out=KTs[s][:, cs], in_=p[:, 0:CW]), reads=[bp], writes=[b_KTs[s][c8]])
                                else:
                                    S.op("vector", lambda e: e.tensor_copy(out=SZs[s][:, cs], in_=p[:, 0:CW]), reads=[bp], writes=[b_SZs[s][c8]])
                        items.append(it)
                pi = 6 + state["pp"] % 2
                state["pp"] += 1
                nsub = CW // 128
                for sub in range(nsub):
                    for k4 in range(4):
                        def it(c=c, pi=pi, sub=sub, k4=k4, c8=c8, nsub=nsub):
                            u, bu = ubuf[c]
                            p = PSB[pi]; bp = b_PSB[pi]
                            for kk in range(4 * k4, 4 * k4 + 4):
                                S.op("tensor", lambda e, kk=kk: e.matmul(p[:, sub * 128:(sub + 1) * 128], lhsT=u[:, kk, sub * 128:(sub + 1) * 128],
                                                                          rhs=w[:, kk, 256:384], start=(kk == 0), stop=(kk == 15)),
                                     reads=[bw, bu], writes=[bp])
                            if sub == nsub - 1 and k4 == 3:
                                b0 = (c * CW) // 128
                                S.op("vector", lambda e: e.tensor_copy(out=VVs[s][:, b0:b0 + nsub, :],
                                                                        in_=p[:, 0:CW].rearrange("p (a b) -> p a b", a=nsub)),
                                     reads=[bp], writes=[b_VVs[s][c8]])
                        items.append(it)
            return items

        def silu_sz(j):
            s = j % NSET
            for c in range(8):
                cs = slice(c * 512, (c + 1) * 512)
                S.op("scalar", lambda e, cs=cs: e.activation(out=SZs[s][:, cs], in_=SZs[s][:, cs], func=AF.Silu), reads=[b_SZs[s][c]], writes=[b_SZs[s][c]])

        def proj(j, qscale):
            w = Wh[j % 2]; bw = b_Wh[j % 2]
            for c in range(8):
                ui = state["u"] % 2
                state["u"] += 1
                u = U[ui]; bu = b_U[ui]
                r = c // 2
                co = (c % 2) * 512
                for q in range(4):
                    src = uT_all[q * 2048 + r * 512:q * 2048 + (r + 1) * 512, :].rearrange("(k p) t -> p k t", p=128)
                    S.dma("sync", lambda e, u=u, q=q, src=src, co=co: e.dma_start(out=u[:, q * 4:(q + 1) * 4, :], in_=src[:, :, co:co + 512]),
                          reads=[b_uTall[q]], writes=[bu])
                cs = slice(c * 512, (c + 1) * 512)
                for which in range(3):
                    pi = 6 + state["pp"] % 2
                    state["pp"] += 1
                    p = PSB[pi]; bp = b_PSB[pi]
                    col = (0, 128, 384)[which]
                    for kk in range(16):
                        S.op("tensor", lambda e, p=p, w=w, u=u, kk=kk, col=col: e.matmul(p[:, :], lhsT=w[:, kk, col:col + 128], rhs=u[:, kk, :],
                                                                                          start=(kk == 0), stop=(kk == 15)),
                             reads=[bw, bu], writes=[bp])
                    if which == 0:
                        S.op("scalar", lambda e, p=p, cs=cs: e.activation(out=QT[:, cs], in_=p[:, :], func=AF.Copy, scale=float(qscale)),
                             reads=[bp], writes=[b_QT[c]])
                    elif which == 1:
                        S.op("vector", lambda e, p=p, cs=cs: e.tensor_copy(out=KT[:, cs], in_=p[:, :]), reads=[bp], writes=[b_KT[c]])
                    else:
                        S.op("scalar", lambda e, p=p, cs=cs: e.activation(out=SZ[:, cs], in_=p[:, :], func=AF.Silu), reads=[bp], writes=[b_SZ[c]])
                pi = 6 + state["pp"] % 2
                state["pp"] += 1
                p = PSB[pi]; bp = b_PSB[pi]
                for sub in range(4):
                    for kk in range(16):
                        S.op("tensor", lambda e, p=p, w=w, u=u, kk=kk, sub=sub: e.matmul(p[:, sub * 128:(sub + 1) * 128], lhsT=u[:, kk, sub * 128:(sub + 1) * 128],
                                                                                          rhs=w[:, kk, 256:384], start=(kk == 0), stop=(kk == 15)),
                             reads=[bw, bu], writes=[bp])
                S.op("vector", lambda e, p=p, c=c: e.tensor_copy(out=VV[:, 4 * c:4 * c + 4, :], in_=p[:, :].rearrange("p (a b) -> p a b", a=4)),
                     reads=[bp], writes=[b_VV[c]])

        def store_mz(j, qs, src_fn):
            i = state["mzo"] % 2
            state["mzo"] += 1
            src_fn(MZo[i], b_MZo[i])
            S.dma("sync", lambda e, i=i: e.dma_start(out=mz_loc[j * 128:(j + 1) * 128, qs * 512:(qs + 1) * 512], in_=MZo[i]),
                  reads=[b_MZo[i]], writes=[b_mzloc[j]])

        if not even:
            E = [A.alloc([128, 512], F32) for _ in range(2)]; b_E = [Buf(), Buf()]
            Lb = [A.alloc([128, 512], BF16) for _ in range(3)]; b_Lb = [Buf() for _ in range(3)]
            LA = [A.alloc([128, 512], BF16) for _ in range(3)]; b_LA = [Buf() for _ in range(3)]
            Ab = [A.alloc([128, 512], BF16) for _ in range(3)]; b_Ab = [Buf() for _ in range(3)]

            def attn_odd(j, bg):
                s_ = j % NSET
                QT, KT, SZ, VV = QTs[s_], KTs[s_], SZs[s_], VVs[s_]
                b_QT, b_KT, b_SZ, b_VV = b_QTs[s_], b_KTs[s_], b_SZs[s_], b_VVs[s_]
                units = [(qs, kb) for qs in range(NSB) for kb in range(4 * qs + 3, -1, -1)]
                n = len(units)
                lacc = {}
                nbg = len(bg)
                done = [0]

                def S0(u):
                    qs, kb = units[u]
                    z = PSB[u % 4]; bz = b_PSB[u % 4]
                    diag = kb >= 4 * qs
                    S.op("tensor", lambda e: e.matmul(z[:, :], lhsT=KT[:, kb * 128:(kb + 1) * 128], rhs=QT[:, qs * 512:(qs + 1) * 512], start=True, stop=False),
                         reads=[b_KT[kb // 4], b_QT[qs]], writes=[bz])
                    if diag:
                        x0 = 512 * qs - 128 * kb + 384
                        S.op("tensor", lambda e: e.matmul(z[:, :], lhsT=Jb, rhs=CMb[:, x0:x0 + 512], start=False, stop=False),
                             reads=[b_J, b_CM], writes=[bz])

                def S1a(u):
                    z = PSB[u % 4]; bz = b_PSB[u % 4]
                    ee = E[u % 2]; be = b_E[u % 2]
                    S.op("scalar", lambda e: e.activation(out=ee, in_=z[:, :], func=AF.Exp), reads=[bz], writes=[be])

                def S1b(u):
                    ee = E[u % 2]; be = b_E[u % 2]
                    S.op("scalar", lambda e: e.activation(out=Lb[u % 3], in_=ee, func=AF.Ln, bias=1.0), reads=[be], writes=[b_Lb[u % 3]])

                def S23(u):
                    qs, kb = units[u]
                    z = PSB[u % 4]; bz = b_PSB[u % 4]
                    first = kb == 4 * qs + 3
                    last = kb == 0
                    S.op("tensor", lambda e: e.matmul(z[:, :], lhsT=TRIb, rhs=Lb[u % 3], start=False, stop=first),
                         reads=[b_TRI, b_Lb[u % 3]], writes=[bz])
                    if not first:
                        pa, pb = lacc[u - 1]
                        S.op("tensor", lambda e: e.matmul(z[:, :], lhsT=NEGb, rhs=pa, start=False, stop=True), reads=[b_NEG, pb], writes=[bz])
                    if not last:
                        if first:
                            lacc[u] = (Lb[u % 3], b_Lb[u % 3])
                        else:
                            pa, pb = lacc[u - 1]
                            S.op("vector", lambda e: e.tensor_tensor(out=LA[u % 3], in0=pa, in1=Lb[u % 3], op=ALU.add),
                                 reads=[pb, b_Lb[u % 3]], writes=[b_LA[u % 3]])
                            lacc[u] = (LA[u % 3], b_LA[u % 3])
                    lacc.pop(u - 2, None)

                def S4(u):
                    z = PSB[u % 4]; bz = b_PSB[u % 4]
                    S.op("scalar", lambda e: e.activation(out=Ab[u % 3], in_=z[:, :], func=AF.Exp), reads=[bz], writes=[b_Ab[u % 3]])

                def S5(u):
                    qs, kb = units[u]
                    first = kb == 4 * qs + 3
                    last = kb == 0
                    o = PSB[4 + qs % 2]; bo = b_PSB[4 + qs % 2]
                    S.op("tensor", lambda e: e.matmul(o[:, :], lhsT=VV[:, kb, :], rhs=Ab[u % 3], start=first, stop=last),
                         reads=[b_VV[kb // 4], b_Ab[u % 3]], writes=[bo])
                    if last:
                        def fin(dst, bdst):
                            S.op("vector", lambda e: e.tensor_tensor(out=dst, in0=o[:, :], in1=SZ[:, qs * 512:(qs + 1) * 512], op=ALU.mult),
                                 reads=[bo, b_SZ[qs]], writes=[bdst])
                        store_mz(j, qs, fin)

                d1, d2 = PIPE_ODD
                for i in range(n + d2):
                    tgt = min(nbg, ((i + 1) * nbg + n - 1) // n) if i < n else nbg
                    while done[0] < tgt:
                        bg[done[0]]()
                        done[0] += 1
                    if i < n:
                        S0(i)
                    if 0 <= i - d1 < n:
                        S1a(i - d1)
                    if 0 <= i - d2 < n:
                        S4(i - d2)
                    if 0 <= i - d1 < n:
                        S1b(i - d1)
                        S23(i - d1)
                    if 0 <= i - d2 < n:
                        S5(i - d2)
        else:
            e_idx = l // 2
            G = [A.alloc([128, GW_A], BF16) for _ in range(2)]; b_G = [Buf(), Buf()]
            Pb = [A.alloc([128, 512], BF16) for _ in range(3)]; b_Pb = [Buf() for _ in range(3)]
            Rr = [A.alloc([128, 512], F32) for _ in range(1)]; b_Rr = [Buf()]
            On = [A.alloc([128, 512], F32) for _ in range(2)]; b_On = [Buf() for _ in range(2)]
            Pacc = [[A.alloc([128, 512], BF16) for _ in range(2)] for _ in range(2)]; b_Pacc = [[Buf(), Buf()], [Buf(), Buf()]]
            Qp = [[A.alloc([128, 512], BF16) for _ in range(2)] for _ in range(2)]; b_Qp = [[Buf(), Buf()], [Buf(), Buf()]]
            for m_ in range(2):
                for r_ in range(2):
                    S.op("vector", lambda e, m_=m_, r_=r_: e.memset(Qp[m_][r_], 0.0), writes=[b_Qp[m_][r_]])
            Dd = A.alloc([128, 512], F32); b_Dd = Buf()
            Dq = A.alloc([128, 512], BF16); b_Dq = Buf()
            lam_sb = A.alloc([128, 256], F32); b_lam = Buf()
            lt = A.alloc([128, 2, 64], F32); b_lt = Buf()
            ls = A.alloc([128, 8], F32); b_ls = Buf()
            sgs = A.alloc([128, 2], F32); b_sg = Buf()
            S.dma("sync", lambda e: e.dma_start(out=lam_sb, in_=lam_in[e_idx:e_idx + 1, :].partition_broadcast(128)), writes=[b_lam])
            S.op("vector", lambda e: e.tensor_tensor(out=lt[:, 0, :], in0=lam_sb[:, 0:64], in1=lam_sb[:, 64:128], op=ALU.mult), reads=[b_lam], writes=[b_lt])
            S.op("vector", lambda e: e.tensor_tensor(out=lt[:, 1, :], in0=lam_sb[:, 128:192], in1=lam_sb[:, 192:256], op=ALU.mult), reads=[b_lam], writes=[b_lt])
            S.op("scalar", lambda e: e.activation(out=lam_sb[:, 0:64], in_=lt[:, 0, :], func=AF.Copy, accum_out=ls[:, 0:1]), reads=[b_lt], writes=[b_lam, b_ls])
            S.op("scalar", lambda e: e.activation(out=lam_sb[:, 64:128], in_=lt[:, 1, :], func=AF.Copy, accum_out=ls[:, 1:2]), reads=[b_lt], writes=[b_lam, b_ls])
            S.op("scalar", lambda e: e.activation(out=ls[:, 2:4], in_=ls[:, 0:2], func=AF.Exp), reads=[b_ls], writes=[b_ls])
            S.op("vector", lambda e: e.tensor_tensor(out=ls[:, 4:5], in0=ls[:, 2:3], in1=ls[:, 3:4], op=ALU.subtract), reads=[b_ls], writes=[b_ls])
            S.op("vector", lambda e: e.tensor_scalar(out=ls[:, 5:6], in0=ls[:, 4:5], scalar1=float(lam_init), scalar2=-1.0, op0=ALU.add, op1=ALU.mult),
                 reads=[b_ls], writes=[b_ls])
            neglam = ls[:, 5:6]
            S.dma("sync", lambda e: e.dma_start(out=sgs[:, 0:1], in_=sg_in[e_idx]), writes=[b_sg])
            S.op("vector", lambda e: e.tensor_scalar(out=sgs[:, 1:2], in0=sgs[:, 0:1], scalar1=float(1.0 - lam_init), scalar2=None, op0=ALU.mult),
                 reads=[b_sg], writes=[b_sg])
            gsc = sgs[:, 1:2]
            cnt = {"u": 0, "acc": 0, "on": 0, "rr": 0, "qp": 0}

            def load_G(j):
                gi = j % 2
                if j < 4:
                    src = bass.AP(frA.tensor, frA.offset + j * GL, [[1, 128], [1, GW_A]])
                    S.dma("sync", lambda e: e.dma_start(out=G[gi], in_=src), reads=[b_frA], writes=[b_G[gi]])
                else:
                    src = bass.AP(frB.tensor, frB.offset + (j - 4) * GL, [[1, 128], [1, GW_B]])
                    S.dma("sync", lambda e: e.dma_start(out=G[gi][:, 0:GW_B], in_=src), reads=[b_frB], writes=[b_G[gi]])

            def softmax_sb(qs, qap, bq, Gt, bG, is_A, deferred=None):
                if is_A:
                    kbs = [kb for kb in range(4 * qs + 3, -1, -1) if 512 * qs - 128 * kb <= 2176]
                else:
                    kbs = list(range(4 * qs + 3, -1, -1))
                ai = cnt["acc"] % 2
                cnt["acc"] += 1
                o = PSB[3 + ai]; bo = b_PSB[3 + ai]
                lq = PSB[5]; bl = b_PSB[5]
                pac = Pacc[ai]; bpac = b_Pacc[ai]
                KT, VV = cur["KT"], cur["VV"]
                b_KT, b_VV = cur["b_KT"], cur["b_VV"]
                n = len(kbs)
                ids = []
                used = [False, False]

                def S0(i):
                    kb = kbs[i]
                    u = cnt["u"]; cnt["u"] += 1
                    ids.append(u)
                    sp = PSB[u % 3]; bs = b_PSB[u % 3]
                    D = 512 * qs - 128 * kb
                    biased = is_A or D <= 1536
                    S.op("tensor", lambda e: e.matmul(sp[:, :], lhsT=KT[:, kb * 128:(kb + 1) * 128], rhs=qap, start=True, stop=not biased),
                         reads=[b_KT[kb // 4], bq], writes=[bs])
                    if biased:
                        x0 = D + 384
                        S.op("tensor", lambda e: e.matmul(sp[:, :], lhsT=Jb, rhs=Gt[:, x0:x0 + 512], start=False, stop=True),
                             reads=[b_J, bG], writes=[bs])

                def S1(i):
                    u = ids[i]
                    sp = PSB[u % 3]; bs = b_PSB[u % 3]
                    S.op("scalar", lambda e: e.activation(out=Pb[u % 3], in_=sp[:, :], func=AF.Exp), reads=[bs], writes=[b_Pb[u % 3]])

                def S2(i):
                    u = ids[i]
                    kb = kbs[i]
                    S.op("tensor", lambda e: e.matmul(o[:, :], lhsT=VV[:, kb, :], rhs=Pb[u % 3], start=(i == 0), stop=(i == n - 1)),
                         reads=[b_VV[kb // 4], b_Pb[u % 3]], writes=[bo])
                    w_ = i % 2
                    eng = ("vector", "gpsimd")[w_]
                    if not used[w_]:
                        used[w_] = True
                        S.op(eng, lambda e: e.tensor_copy(out=pac[w_], in_=Pb[u % 3]), reads=[b_Pb[u % 3]], writes=[bpac[w_]])
                    else:
                        S.op(eng, lambda e: e.tensor_tensor(out=pac[w_], in0=pac[w_], in1=Pb[u % 3], op=ALU.add),
                             reads=[b_Pb[u % 3], bpac[w_]], writes=[bpac[w_]])

                d1, d2 = PIPE_EVEN
                tot = n + d2
                dpos = min(3, tot - 1)
                for i in range(tot):
                    if i < n:
                        bgs = cur["bg"]
                        left = len(bgs["items"]) - bgs["done"]
                        take = -(-left // max(1, bgs["units"]))
                        bgs["units"] -= 1
                        for _ in range(take):
                            bgs["items"][bgs["done"]]()
                            bgs["done"] += 1
                        S0(i)
                    if 0 <= i - d1 < n:
                        S1(i - d1)
                    if 0 <= i - d2 < n:
                        S2(i - d2)
                    if i == dpos and deferred is not None:
                        deferred()

                def norm():
                    S.op("tensor", lambda e: e.matmul(lq[:, :], lhsT=ONESb, rhs=pac[0], start=True, stop=not used[1]),
                         reads=[b_ONES, bpac[0]], writes=[bl])
                    if used[1]:
                        S.op("tensor", lambda e: e.matmul(lq[:, :], lhsT=ONESb, rhs=pac[1], start=False, stop=True),
                             reads=[b_ONES, bpac[1]], writes=[bl])
                    oi = cnt["on"] % 2; cnt["on"] += 1
                    S.op("vector", lambda e: e.reciprocal(out=Rr[0], in_=lq[:, :]), reads=[bl], writes=[b_Rr[0]])
                    S.op("vector", lambda e: e.tensor_tensor(out=On[oi], in0=o[:, :], in1=Rr[0], op=ALU.mult), reads=[bo, b_Rr[0]], writes=[b_On[oi]])
                    return On[oi], b_On[oi]
                return norm

            def attn_A(j):
                QT, SZ, b_QT, b_SZ = cur["QT"], cur["SZ"], cur["b_QT"], cur["b_SZ"]
                pending = [None]
                for qs in range(NSB):
                    nrm = softmax_sb(qs, QT[:, qs * 512:(qs + 1) * 512], b_QT[qs], G[j % 2], b_G[j % 2], True, deferred=pending[0])

                    def fin_all(nrm=nrm, qs=qs):
                        on, bon = nrm()

                        def fin(dst, bdst):
                            S.op("vector", lambda e: e.tensor_tensor(out=dst, in0=on, in1=SZ[:, qs * 512:(qs + 1) * 512], op=ALU.mult),
                                 reads=[bon, b_SZ[qs]], writes=[bdst])
                        store_mz(j, qs, fin)
                    pending[0] = fin_all
                pending[0]()

            def attn_B(j):
                QT, SZ, b_QT, b_SZ = cur["QT"], cur["SZ"], cur["b_QT"], cur["b_SZ"]
                pending = [None]
                for qs in range(NSB):
                    r_ = cnt["qp"] % 2; cnt["qp"] += 1
                    S.op("gpsimd", lambda e, r_=r_, qs=qs: e.tensor_copy(out=Qp[0][r_][0:64, :], in_=QT[0:64, qs * 512:(qs + 1) * 512]),
                         reads=[b_QT[qs]], writes=[b_Qp[0][r_]])
                    S.op("gpsimd", lambda e, r_=r_, qs=qs: e.tensor_copy(out=Qp[1][r_][64:128, :], in_=QT[64:128, qs * 512:(qs + 1) * 512]),
                         reads=[b_QT[qs]], writes=[b_Qp[1][r_]])
                    n1 = softmax_sb(qs, Qp[0][r_], b_Qp[0][r_], G[j % 2], b_G[j % 2], False, deferred=pending[0])
                    res1 = {}

                    def d1(n1=n1, res1=res1):
                        res1["o"] = n1()
                    n2 = softmax_sb(qs, Qp[1][r_], b_Qp[1][r_], G[j % 2], b_G[j % 2], False, deferred=d1)

                    def fin_all(n2=n2, res1=res1, qs=qs):
                        o1, bo1 = res1["o"]
                        o2, bo2 = n2()
                        S.op("vector", lambda e: e.scalar_tensor_tensor(out=Dd, in0=o2, scalar=neglam, in1=o1, op0=ALU.mult, op1=ALU.add),
                             reads=[bo1, bo2, b_ls], writes=[b_Dd])
                        S.op("vector", lambda e: e.tensor_tensor(out=Dq, in0=Dd, in1=Dd, op=ALU.mult), reads=[b_Dd], writes=[b_Dq])
                        S.op("tensor", lambda e: e.matmul(PSB[5][:, :], lhsT=ONESb, rhs=Dq, start=True, stop=True), reads=[b_ONES, b_Dq], writes=[b_PSB[5]])
                        S.op("scalar", lambda e: e.activation(out=Rr[0], in_=PSB[5][:, :], func=AF.Ln, scale=1.0 / 128, bias=EPS), reads=[b_PSB[5]], writes=[b_Rr[0]])
                        S.op("scalar", lambda e: e.activation(out=Rr[0], in_=Rr[0], func=AF.Exp, scale=-0.5), reads=[b_Rr[0]], writes=[b_Rr[0]])
                        S.op("vector", lambda e: e.tensor_tensor(out=Dd, in0=Dd, in1=Rr[0], op=ALU.mult), reads=[b_Dd, b_Rr[0]], writes=[b_Dd])

                        def fin(dst, bdst):
                            S.op("vector", lambda e: e.scalar_tensor_tensor(out=dst, in0=Dd, scalar=gsc, in1=SZ[:, qs * 512:(qs + 1) * 512], op0=ALU.mult, op1=ALU.mult),
                                 reads=[b_Dd, b_sg, b_SZ[qs]], writes=[bdst])
                        store_mz(j, qs, fin)
                    pending[0] = fin_all
                pending[0]()

        load_w(0)
        if not even:
            load_w(1)
            for itf in proj_items(0, 128 ** -0.5):
                itf()
        cur = {}
        if even:
            load_w(1)
            for itf in proj_items(0, 128 ** -0.5):
                itf()
        for j in range(8):
            if even:
                if j + 2 < 8:
                    load_w(j + 2)
                load_G(j)
                silu_sz(j)
                s_ = j % NSET
                nxt_scale = (128 ** -0.5) if (j + 1) < 4 else (64 ** -0.5)
                cur.update(QT=QTs[s_], KT=KTs[s_], SZ=SZs[s_], VV=VVs[s_], b_QT=b_QTs[s_], b_KT=b_KTs[s_], b_SZ=b_SZs[s_], b_VV=b_VVs[s_])
                units = sum(len([kb for kb in range(4 * qs + 3, -1, -1) if 512 * qs - 128 * kb <= 2176]) for qs in range(NSB)) if j < 4 \
                    else 2 * sum(4 * qs + 4 for qs in range(NSB))
                cur["bg"] = {"items": proj_items(j + 1, nxt_scale) if j + 1 < 8 else [], "done": 0, "units": units}
                if j < 4:
                    attn_A(j)
                else:
                    attn_B(j)
                bgs = cur["bg"]
                while bgs["done"] < len(bgs["items"]):
                    bgs["items"][bgs["done"]]()
                    bgs["done"] += 1
                S.cc(lambda e, j=j: e.collective_compute("AllGather", ALU.bypass, replica_groups=GROUPS4, ins=[mz_loc[j * 128:(j + 1) * 128, :]],
                                                         outs=[mz_all[j * 512:(j + 1) * 512, :]]),
                     reads=[b_mzloc[j]], writes=[b_mzall])
                continue
            if not even:
                if j + 2 < 8:
                    load_w(j + 2)
                silu_sz(j)
                bg = proj_items(j + 1, 128 ** -0.5) if j + 1 < 8 else []
                attn_odd(j, bg)
            elif j < 4:
                load_G(j)
                proj(j, 128 ** -0.5)
                attn_A(j)
            else:
                load_G(j)
                proj(j, 64 ** -0.5)
                attn_B(j)
            S.cc(lambda e, j=j: e.collective_compute("AllGather", ALU.bypass, replica_groups=GROUPS4, ins=[mz_loc[j * 128:(j + 1) * 128, :]],
                                                     outs=[mz_all[j * 512:(j + 1) * 512, :]]),
                 reads=[b_mzloc[j]], writes=[b_mzall])

    _pl2 = phase_L2

    def phase_L2(l):
        _pl2(l)
        A.h_free = False

    phases = [phase_M, phase_G, lambda: phase_L31(None, 0, False)]
    for l in range(DEPTH):
        last = l == DEPTH - 1
        phases.append(lambda l=l: phase_L2(l))
        phases.append(lambda l=l, last=last: phase_L31(l, None if last else l + 1, last))
    for i, ph in enumerate(phases):
        if nphase is not None and i >= nphase:
            break
        ph()
        S.barrier(); A.reset()
    if nphase is not None and nphase < len(phases):
        yv = y_out.rearrange("(t p) f -> p t f", p=128)
        for t in range(8):
            S.dma("sync", lambda e, t=t: e.dma_start(out=yv[:, t, :], in_=H[:, t, :]), reads=[b_H[t]], final=True)
    if dbg:
        d1 = C.dout("dbg_uT", [4 * 2048, 1024], BF16)
        d2 = C.dout("dbg_mz", [4 * 1024, 4096], BF16)
        d3 = C.dout("dbg_mod", [4, 6144], F32)
        for q in range(16 if (nphase is None or nphase >= 3) else 0):
            S.dma("sync", lambda e, q=q: e.dma_start(out=d1[q * 512:(q + 1) * 512, :], in_=uT_all[q * 512:(q + 1) * 512, :]), reads=b_uTall, final=True)
        if nphase is None or nphase >= 4:
            for q in range(32):
                S.dma("sync", lambda e, q=q: e.dma_start(out=d2[q * 128:(q + 1) * 128, :], in_=mz_all[q * 128:(q + 1) * 128, :]), reads=[b_mzall], final=True)
        S.dma("sync", lambda e: e.dma_start(out=d3[:, :], in_=mod_all[:, :]), reads=[b_modall], final=True)
    return C.finish()


_PROGS = {}


def kernel(x, c, norm_g, w_mod, b_mod, w_in, w_out, rel_bias, diff_lambda, diff_subln_g, final_norm_g):
    f32 = lambda a: np.ascontiguousarray(np.asarray(a, np.float32))
    x = f32(x); c = f32(c); norm_g = f32(norm_g); w_mod = f32(w_mod); b_mod = f32(b_mod); w_in = f32(w_in); w_out = f32(w_out)
    rel_bias = f32(rel_bias); diff_lambda = f32(diff_lambda); diff_subln_g = f32(diff_subln_g); final_norm_g = f32(final_norm_g)
    if "fused" not in _PROGS:
        _PROGS["fused"] = build_fused()
    nc = _PROGS["fused"]
    ims = make_inputs(x, c, norm_g, w_mod, b_mod, w_in, w_out, rel_bias, diff_lambda, diff_subln_g, final_norm_g)
    res = run_bass_kernel_spmd(nc, ims, core_ids=list(range(8))).results
    out = np.concatenate([np.asarray(res[i]["y"]) for i in range(8)], axis=0)
    return out.reshape(BATCH, SEQ, D_MODEL).astype(np.float32)


def make_inputs(x, c, norm_g, w_mod, b_mod, w_in, w_out, rel_bias, diff_lambda, diff_subln_g, final_norm_g):
    hc = host_consts()
    xf = x.reshape(8192, 2048)
    ident = np.eye(128, dtype=np.float32)
    ims = []
    for i in range(8):
        b, g = i // 4, i % 4
        lm = i % 4
        wcs = np.stack([make_wc(w_in[l], g, l % 2 == 0) for l in range(DEPTH)])
        cols = [4 * g + j for j in range(4)] + [16 + 4 * g + j for j in range(4)]
        ims.append({
            "x": np.ascontiguousarray(xf[i * 1024:(i + 1) * 1024]),
            "cT": np.ascontiguousarray(c[b].reshape(16, 128).T),
            "wm": w_mod[lm], "bm": np.ascontiguousarray(b_mod[lm].reshape(1, 6144)),
            "wc": wcs, "wout": w_out, "gn": norm_g, "gf": final_norm_g.reshape(1, 2048),
            "ident": ident, "J": hc["J"], "TRI": hc["TRI"], "CM": hc["CM"],
            "OH": hc["OH"], "OHB": hc["OHB"], "LMA": hc["LMA"], "NGB": hc["NGB"],
            "RB": np.ascontiguousarray(rel_bias[:, cols]),
            "lamv": np.ascontiguousarray(diff_lambda.reshape(2, 256)),
            "sg": np.ascontiguousarray(diff_subln_g.reshape(2, 128, 1)),
        })
    return ims
```
